# Optimizing a Trainium2 kernel written in Bass

```python
import jax, jax.numpy as jnp
from jax import lax
import numpy as np

D_MODEL = 1024
BATCH = 8
SEQ = 8192
DEPTH = 1

CHUNK = 64
GMLP_BLOCK = 128
GMLP_WIDTH = D_MODEL
GMLP_GROUPS = 8
GMLP_GROUP_DIM = GMLP_WIDTH // GMLP_GROUPS
LRU_WIDTH = D_MODEL
LRU_HEADS = 16
LRU_HEAD_DIM = LRU_WIDTH // LRU_HEADS
CONV_WIDTH = 4
LRU_C = 8.0
D_FF = ((-(-8 * D_MODEL // 3) + 255) // 256) * 256
PLE_DIM = 256
EPS = 1e-6
SPLITS = (GMLP_WIDTH, 2 * GMLP_WIDTH, 2 * GMLP_WIDTH + LRU_WIDTH,
          2 * GMLP_WIDTH + 2 * LRU_WIDTH, 2 * GMLP_WIDTH + 2 * LRU_WIDTH + D_MODEL)
IN_COLS = 2 * GMLP_WIDTH + 2 * LRU_WIDTH + 2 * D_MODEL

kernel_name = "hybrid_gmlp_rglru_sandwich_block"


def rms_norm(x, g):
    xf = x.astype(jnp.float32)
    y = xf * lax.rsqrt(jnp.mean(xf * xf, axis=-1, keepdims=True) + EPS)
    return (y * g.astype(jnp.float32)).astype(x.dtype)


def layer_norm(x, g, b):
    xf = x.astype(jnp.float32)
    mu = jnp.mean(xf, axis=-1, keepdims=True)
    var = jnp.mean(jnp.square(xf - mu), axis=-1, keepdims=True)
    y = (xf - mu) * lax.rsqrt(var + EPS)
    return (y * g.astype(jnp.float32) + b.astype(jnp.float32)).astype(x.dtype)


def gmlp_spatial_gate(u, v, ln_g, ln_b, w_s, b_s):
    bsz, seq, _ = v.shape
    n_blk = seq // GMLP_BLOCK
    v = layer_norm(v, ln_g, ln_b)
    vb = v.reshape(bsz, n_blk, GMLP_BLOCK, GMLP_GROUPS, GMLP_GROUP_DIM)
    chunk_id = jnp.arange(GMLP_BLOCK) // CHUNK
    mask = chunk_id[None, :] <= chunk_id[:, None]
    w = jnp.where(mask[None], w_s, jnp.zeros_like(w_s))
    mixed = jnp.einsum('gij,bnjgc->bnigc', w, vb) + b_s.T[None, None, :, :, None]
    return u * mixed.reshape(bsz, seq, GMLP_WIDTH)


def causal_depthwise_conv(x, w, b):
    c = x.shape[-1]
    y = lax.conv_general_dilated(
        x, w[:, None, :].astype(x.dtype), window_strides=(1,),
        padding=[(CONV_WIDTH - 1, 0)],
        dimension_numbers=('NWC', 'WIO', 'NWC'), feature_group_count=c)
    return y + b


def block_diag_linear(x, w, b):
    bsz, seq, _ = x.shape
    xh = x.reshape(bsz, seq, LRU_HEADS, LRU_HEAD_DIM)
    y = jnp.einsum('bshi,hij->bshj', xh, w).reshape(bsz, seq, LRU_WIDTH)
    return y + b


def rg_lru(x, w_r, b_r, w_i, b_i, lam):
    r = jax.nn.sigmoid(block_diag_linear(x, w_r, b_r).astype(jnp.float32))
    i = jax.nn.sigmoid(block_diag_linear(x, w_i, b_i).astype(jnp.float32))
    log_a = -LRU_C * r * jax.nn.softplus(-lam.astype(jnp.float32))
    a = jnp.exp(log_a)
    gated_x = jnp.sqrt(-jnp.expm1(2.0 * log_a)) * (i * x.astype(jnp.float32))

    def combine(left, right):
        a1, h1 = left
        a2, h2 = right
        return a1 * a2, a2 * h1 + h2

    _, h = lax.associative_scan(combine, (a, gated_x), axis=1)
    return h.astype(x.dtype)


def setup_inputs(seed: int = 0) -> dict:
    key = jax.random.key(seed)
    ks = iter(jax.random.split(key, 40))

    def nrm(shape, fan_in):
        return jax.random.normal(next(ks), shape, jnp.float32) * (fan_in ** -0.5)

    def gain(shape):
        return 1.0 + 0.05 * jax.random.normal(next(ks), shape, jnp.float32)

    def bias(shape, scale=0.02):
        return scale * jax.random.normal(next(ks), shape, jnp.float32)

    L = DEPTH
    x = jax.random.normal(next(ks), (BATCH, SEQ, D_MODEL), jnp.float32)
    p = jax.random.normal(next(ks), (L, BATCH, SEQ, PLE_DIM), jnp.float32)
    a0 = jax.random.uniform(next(ks), (L, LRU_WIDTH), jnp.float32, minval=0.9, maxval=0.999)
    s = a0 ** (1.0 / LRU_C)
    lru_lambda = jnp.log(s) - jnp.log1p(-s)
    return {
        "x": x,
        "p": p,
        "norm_mix_pre": gain((L, D_MODEL)),
        "norm_mix_post": gain((L, D_MODEL)),
        "w_in": nrm((L, D_MODEL, IN_COLS), D_MODEL),
        "gmlp_ln_g": gain((L, GMLP_WIDTH)),
        "gmlp_ln_b": bias((L, GMLP_WIDTH)),
        "gmlp_w_s": nrm((L, GMLP_GROUPS, GMLP_BLOCK, GMLP_BLOCK), GMLP_BLOCK),
        "gmlp_b_s": 1.0 + 0.1 * jax.random.normal(next(ks), (L, GMLP_GROUPS, GMLP_BLOCK), jnp.float32),
        "conv_w": nrm((L, CONV_WIDTH, LRU_WIDTH), CONV_WIDTH),
        "conv_b": bias((L, LRU_WIDTH)),
        "lru_w_r": nrm((L, LRU_HEADS, LRU_HEAD_DIM, LRU_HEAD_DIM), LRU_HEAD_DIM),
        "lru_b_r": bias((L, LRU_WIDTH)),
        "lru_w_i": nrm((L, LRU_HEADS, LRU_HEAD_DIM, LRU_HEAD_DIM), LRU_HEAD_DIM),
        "lru_b_i": bias((L, LRU_WIDTH)),
        "lru_lambda": lru_lambda,
        "w_branch_a": nrm((L, GMLP_WIDTH, D_MODEL), GMLP_WIDTH),
        "w_branch_b": nrm((L, LRU_WIDTH, D_MODEL), LRU_WIDTH),
        "w_out": nrm((L, D_MODEL, D_MODEL), D_MODEL),
        "norm_ffn_pre": gain((L, D_MODEL)),
        "norm_ffn_post": gain((L, D_MODEL)),
        "ffn_w_gate": nrm((L, D_MODEL, D_FF), D_MODEL),
        "ffn_w_up": nrm((L, D_MODEL, D_FF), D_MODEL),
        "ffn_w_down": nrm((L, D_FF, D_MODEL), D_FF),
        "norm_ple_pre": gain((L, D_MODEL)),
        "norm_ple_post": gain((L, D_MODEL)),
        "ple_w_in": nrm((L, PLE_DIM, D_MODEL), PLE_DIM),
        "ple_w_gate": nrm((L, D_MODEL, D_MODEL), D_MODEL),
    }


def reference(x, p, norm_mix_pre, norm_mix_post, w_in, gmlp_ln_g, gmlp_ln_b, gmlp_w_s,
              gmlp_b_s, conv_w, conv_b, lru_w_r, lru_b_r, lru_w_i, lru_b_i, lru_lambda,
              w_branch_a, w_branch_b, w_out, norm_ffn_pre, norm_ffn_post, ffn_w_gate,
              ffn_w_up, ffn_w_down, norm_ple_pre, norm_ple_post, ple_w_in, ple_w_gate):
    for l in range(DEPTH):
        h = rms_norm(x, norm_mix_pre[l])
        z = h @ w_in[l]
        u, v, xr, yr, ga, gb = jnp.split(z, SPLITS, axis=-1)
        a_out = gmlp_spatial_gate(jax.nn.gelu(u), jax.nn.gelu(v), gmlp_ln_g[l], gmlp_ln_b[l],
                                  gmlp_w_s[l], gmlp_b_s[l])
        xr = causal_depthwise_conv(xr, conv_w[l], conv_b[l])
        b_out = rg_lru(xr, lru_w_r[l], lru_b_r[l], lru_w_i[l], lru_b_i[l], lru_lambda[l]) * jax.nn.gelu(yr)
        merged = jax.nn.sigmoid(ga) * (a_out @ w_branch_a[l]) + jax.nn.sigmoid(gb) * (b_out @ w_branch_b[l])
        x = x + rms_norm(merged @ w_out[l], norm_mix_post[l])
        h = rms_norm(x, norm_ffn_pre[l])
        f = (jax.nn.silu(h @ ffn_w_gate[l]) * (h @ ffn_w_up[l])) @ ffn_w_down[l]
        x = x + rms_norm(f, norm_ffn_post[l])
        gate = jax.nn.sigmoid(rms_norm(x, norm_ple_pre[l]) @ ple_w_gate[l])
        e = p[l] @ ple_w_in[l]
        x = x + rms_norm(gate * e, norm_ple_post[l])
    return x
```

```python
import numpy as np
from contextlib import ExitStack
import concourse.bass as bass
import concourse.mybir as mybir
from concourse.bass_utils import run_bass_kernel_spmd

F32 = mybir.dt.float32
BF16 = mybir.dt.bfloat16
I32 = mybir.dt.int32
AF = mybir.ActivationFunctionType
ALU = mybir.AluOpType

_ISZ = {F32: 4, BF16: 2, I32: 4}

S_TOK = 8192
D = 1024
NT_DEFAULT = 16
T = 512
DFF = 2816
NF = 22
MAGIC = 1597463007.0
EPS = 1e-6
RING = 5
SLOT = 4096


PE_LABELS = []
CUR_LABEL = ["setup"]


class Sem:
    def __init__(self, h):
        self.h = h
        self.n = 0


class Eng:
    def __init__(self, name, sem):
        self.name = name
        self.sem = sem
        self.ops = []
        self.seen = {}

    def wait(self, sem, val):
        if sem is self.sem and self.name == "pe":
            return
        if self.seen.get(sem, 0) >= val:
            return
        self.seen[sem] = val
        h = sem.h
        self.ops.append(lambda e, h=h, val=val: e.wait_ge(h, val))


def _rng(ap):
    a = ap.ap
    isz = _ISZ[ap.dtype]
    row = a[0][0]
    off = ap.offset
    lo = off % row if row > 0 else off
    span = 1
    for (st, cnt) in a[1:]:
        span += abs(st) * (cnt - 1)
    return ap.tensor.name, lo * isz, (lo + span) * isz


class Prog:
    def __init__(self):
        self.psum = set()
        self.recs = {}

    def deps_and_record(self, E, reads, writes, ticket_fn):
        acc = []
        for lst_, w in ((reads, False), (writes, True)):
            for ap in lst_:
                k, lo, hi = _rng(ap)
                if k in self.psum:
                    for bnk in range(lo // 2048, (hi - 1) // 2048 + 1):
                        acc.append(((k, bnk), 0, 1 << 30, w, True))
                else:
                    acc.append((k, lo, hi, w, False))
        waits = {}
        for (k, lo, hi, w, ps) in acc:
            for r in self.recs.get(k, ()):
                if r[1] <= lo or r[0] >= hi:
                    continue
                same = r[3] is E
                if same and E.name == "pe":
                    continue
                if (not w) and (not r[2]):
                    if not ps or same:
                        continue
                s_, v = r[4], r[5]
                if waits.get(s_, 0) < v:
                    waits[s_] = v
        for s_, v in waits.items():
            E.wait(s_, v)
        sem, val = ticket_fn()
        for (k, lo, hi, w, ps) in acc:
            lst = self.recs.setdefault(k, [])
            if w:
                lst[:] = [r for r in lst if not (r[0] >= lo and r[1] <= hi)]
            else:
                lst[:] = [r for r in lst if not ((not r[2]) and r[4] is sem and r[0] >= lo and r[1] <= hi)]
            lst.append([lo, hi, w, E, sem, val])

    def op(self, E, fn, reads=(), writes=()):
        def ticket():
            E.sem.n += 1
            return E.sem, E.sem.n
        self.deps_and_record(E, reads, writes, ticket)
        h = E.sem.h
        E.ops.append(lambda e, fn=fn, h=h: fn(e).then_inc(h, 1))

    def group(self, E, fns, reads=(), writes=(), accum=False):
        PE_LABELS.append((CUR_LABEL[0], len(fns)))
        if not accum:
            for ap in writes:
                k, lo, hi = _rng(ap)
                if k not in self.psum:
                    continue
                for bnk in range(lo // 2048, (hi - 1) // 2048 + 1):
                    lst = self.recs.get((k, bnk), ())
                    if lst and lst[-1][2] and lst[-1][3] is E:
                        raise AssertionError("PSUM bank %s overwritten before being read (%s)" % (bnk, CUR_LABEL[0]))

        def ticket():
            E.sem.n += 1
            return E.sem, E.sem.n
        self.deps_and_record(E, reads, writes, ticket)
        h = E.sem.h
        for fn in fns[:-1]:
            E.ops.append(fn)
        last = fns[-1]
        E.ops.append(lambda e, fn=last, h=h: fn(e).then_inc(h, 1))

    def dma(self, Q, dsem, out, in_, reads=(), writes=()):
        def ticket():
            dsem.n += 16
            return dsem, dsem.n
        self.deps_and_record(Q, reads, writes, ticket)
        h = dsem.h
        Q.ops.append(lambda e, out=out, in_=in_, h=h: e.dma_start(out=out, in_=in_).then_inc(h, 16))
        return dsem, dsem.n


C_GMIX, C_GFFN, C_GPLE, C_HALF, C_ONE = 0, 8, 16, 24, 25
C_LNG, C_LNB, C_CVW, C_CVB, C_BR, C_BI, C_LAM = 26, 34, 42, 74, 82, 90, 98
NCST = 106


def _slab_plan():
    sl = []

    def add(name, kc0, nkc, c0, nc_, sc, per):
        sl.append(dict(name=name, kc0=kc0, nkc=nkc, c0=c0, ncols=nc_, sc=sc, per=per))

    for c0 in (3072, 3584):
        add("w_in", 0, 8, c0, 512, C_GMIX, True)
    for c0 in (1024, 1536):
        add("w_in", 0, 8, c0, 512, C_GMIX, True)
    for c0 in (2048, 2560):
        add("w_in", 0, 8, c0, 512, C_GMIX, True)
    for c0 in (0, 512):
        add("w_in", 0, 8, c0, 512, C_GMIX, True)
    for h in range(2):
        add("w_in", 0, 8, 4096 + h * 512, 512, C_GMIX, True)
        add("w_in", 0, 8, 5120 + h * 512, 512, C_GMIX, True)
        add("w_a", 0, 8, h * 512, 512, C_ONE, False)
        add("w_b", 0, 8, h * 512, 512, C_ONE, False)
    for h in range(2):
        add("w_out", 0, 8, h * 512, 512, C_HALF, False)
    for j in range(6):
        ncols = 512 if j < 5 else 256
        add("f_gate", 0, 8, j * 512, ncols, C_GFFN, True)
        add("f_up", 0, 8, j * 512, ncols, C_GFFN, True)
    for h in range(2):
        for (k0, nk) in ((0, 8), (8, 8), (16, 6)):
            add("f_down", k0, nk, h * 512, 512, C_ONE, False)
    for h in range(2):
        add("p_gate", 0, 8, h * 512, 512, C_GPLE, True)
    for h in range(2):
        add("p_in", 0, 2, h * 512, 512, C_HALF, False)
    off = 0
    for s in sl:
        s["off"] = off
        s["len"] = s["nkc"] * s["ncols"]
        off += s["len"]
    return sl, off


SLABS, WTOT = _slab_plan()
NSL = len(SLABS)
(SL_YR, SL_V, SL_XR, SL_U, SL_MERGE, SL_WOUT, SL_FFN, SL_DOWN, SL_PGATE, SL_PIN) = (0, 2, 4, 6, 8, 16, 18, 30, 36, 38)


def _pack_weights(mats):
    wf = np.empty((128, WTOT), np.float32)
    for s in SLABS:
        W = mats[s["name"]]
        blk = W[s["kc0"] * 128:(s["kc0"] + s["nkc"]) * 128, s["c0"]:s["c0"] + s["ncols"]]
        blk = blk.reshape(s["nkc"], 128, s["ncols"]).transpose(1, 0, 2).reshape(128, -1)
        wf[:, s["off"]:s["off"] + s["len"]] = blk
    return wf


def build_nc(NT):
    nc = bass.Bass("TRN2", target_bir_lowering=False)
    ntok = NT * T
    x_d = nc.dram_tensor("x", [ntok, D], F32, kind="ExternalInput").ap()
    p_d = nc.dram_tensor("p", [ntok, 256], F32, kind="ExternalInput").ap()
    wf_d = nc.dram_tensor("wf", [128, WTOT], F32, kind="ExternalInput").ap()
    cst_d = nc.dram_tensor("cst", [128, NCST], F32, kind="ExternalInput").ap()
    idn_d = nc.dram_tensor("idn", [128, 128], F32, kind="ExternalInput").ap()
    wst_d = nc.dram_tensor("wst", [128, 1024], F32, kind="ExternalInput").ap()
    bdr_d = nc.dram_tensor("bdr", [128, 1024], F32, kind="ExternalInput").ap()
    bdi_d = nc.dram_tensor("bdi", [128, 1024], F32, kind="ExternalInput").ap()
    bcv_d = nc.dram_tensor("bcv", [1, 4096], F32, kind="ExternalInput").ap()
    y_d = nc.dram_tensor("y", [ntok, D], F32, kind="ExternalOutput").ap()
    wbf_d = nc.dram_tensor("wbf", [128, WTOT], BF16, kind="Internal").ap()

    with ExitStack() as es:
        def sb(name, shape, dt):
            return es.enter_context(nc.sbuf_tensor(name, shape, dt))

        def sem(name):
            return Sem(es.enter_context(nc.semaphore(name)))

        P = Prog()
        pe = Eng("pe", sem("s_pe"))
        act = Eng("act", sem("s_act"))
        dve = Eng("dve", sem("s_dve"))
        pool = Eng("pool", sem("s_pool"))
        sp = Eng("sp", None)

        psum_all = es.enter_context(nc.psum_tensor("psum", [128, 8, 512], F32))
        P.psum.add("psum")
        banks = [psum_all[:, i, :] for i in range(8)]
        bank_i = [0]

        class _Bk:
            def __init__(self, ap):
                self.ap = ap

            def __getitem__(self, key):
                return self.ap[key]

        def nbank():
            b = _Bk(banks[bank_i[0] % 8])
            bank_i[0] += 1
            return b

        def nbank_pair():
            if bank_i[0] % 2:
                bank_i[0] += 1
            i = bank_i[0] % 8
            bank_i[0] += 2
            return psum_all[:, i:i + 2, :]

        xs = sb("xs", [128, 4, 1024], F32)
        xin = sb("xin", [128, 4, 1024], F32)
        pin = sb("pin", [128, 4, 256], F32)
        pTs = [sb("pT%d" % i, [128, 2, 512], BF16) for i in range(2)]
        hns = [sb("hn%d" % i, [128, 1024], BF16) for i in range(2)]
        hn_i = [0]

        def nhn():
            h_ = hns[hn_i[0] % 2]
            hn_i[0] += 1
            return h_
        hTA = sb("hTA", [128, 8, 512], BF16)
        hTB = sb("hTB", [128, 4, 8, 128], BF16)
        gu = sb("gu", [128, 8, 512], BF16)
        gy = sb("gy", [128, 8, 512], BF16)
        xrb = sb("xrb", [128, 8, 516], BF16)
        xcm = sb("xcm", [128, 8, 512], BF16)
        arena = sb("arena", [128, 12288], BF16)
        TP = [sb("tp%d" % i, [128, 512], F32) for i in range(4)]
        get = sb("get", [128, 1024], F32)
        get2 = sb("get2", [128, 1024], F32)
        junks = [sb("junk%d" % i, [128, 1024], BF16) for i in range(2)]
        junk_i = [0]

        def njunk():
            j_ = junks[junk_i[0] % 2]
            junk_i[0] += 1
            return j_
        csb = sb("csb", [128, NCST], F32)
        drv = sb("drv", [128, 40], F32)
        identb = sb("identb", [128, 128], BF16)
        WsT = sb("WsT", [128, 8, 128], BF16)
        BDr = sb("BDr", [128, 8, 128], BF16)
        BDi = sb("BDi", [128, 8, 128], BF16)
        Dcv = sb("Dcv", [128, 4, 8, 128], BF16)
        Bg = sb("Bg", [128, 8, 128], F32)
        gpost = sb("gpost", [128, 3, 1024], F32)
        stt = sb("stt", [128, 128], F32)
        mvs = sb("mvs", [128, 64], F32)
        hstate = sb("hstate", [128, 8], F32)
        wring = [sb("wr%d" % i, [128, SLOT], BF16) for i in range(RING)]

        d_ring = [sem("d_wr%d" % i) for i in range(RING)]
        d_xs = sem("d_xs")
        d_pin = sem("d_pin")
        d_out = [sem("d_out%d" % i) for i in range(4)]
        d_c = [sem("d_c%d" % i) for i in range(7)]
        d_st = [sem("d_st%d" % i) for i in range(2)]

        tp_i = [0]

        def ntp():
            t_ = TP[tp_i[0] % len(TP)]
            tp_i[0] += 1
            return t_

        act_v = arena[:, 0:NF * 512].rearrange("p (a b) -> p a b", a=NF)
        vln = arena[:, 0:4096].rearrange("p (a b) -> p a b", a=4)
        gv = arena[:, 4096:12288].bitcast(F32).rearrange("p (a b) -> p a b", a=4)
        a_t = arena[:, 0:4096].bitcast(F32).rearrange("p (a b) -> p a b", a=4)
        a2_t = arena[:, 4096:8192].bitcast(F32).rearrange("p (a b) -> p a b", a=4)
        q_t = arena[:, 8192:12288].bitcast(F32).rearrange("p (a b) -> p a b", a=4)

        def rstd_newton(v, y, tmp, scalar_v=False):
            vi = v.bitcast(I32)
            yi = y.bitcast(I32)
            P.op(dve, lambda e: e.tensor_scalar(out=yi, in0=vi, scalar1=-0.5, scalar2=MAGIC, op0=ALU.mult, op1=ALU.add),
                 reads=[v], writes=[y])
            if scalar_v:
                hv = tmp[:, 0:1]
                t_ = tmp[:, 1:2]
                P.op(dve, lambda e: e.tensor_scalar(out=hv, in0=v, scalar1=-0.5, scalar2=None, op0=ALU.mult), reads=[v], writes=[hv])
                for _ in range(2):
                    P.op(dve, lambda e: e.scalar_tensor_tensor(out=t_, in0=y, scalar=hv, in1=y, op0=ALU.mult, op1=ALU.mult),
                         reads=[y, hv], writes=[t_])
                    P.op(dve, lambda e: e.scalar_tensor_tensor(out=y, in0=t_, scalar=1.5, in1=y, op0=ALU.add, op1=ALU.mult),
                         reads=[t_, y], writes=[y])
                return
            for _ in range(2):
                P.op(dve, lambda e: e.tensor_tensor(out=tmp, in0=y, in1=y, op=ALU.mult), reads=[y], writes=[tmp])
                P.op(dve, lambda e: e.tensor_tensor(out=tmp, in0=tmp, in1=v, op=ALU.mult), reads=[tmp, v], writes=[tmp])
                P.op(dve, lambda e: e.tensor_scalar(out=tmp, in0=tmp, scalar1=-0.5, scalar2=1.5, op0=ALU.mult, op1=ALU.add),
                     reads=[tmp], writes=[tmp])
                P.op(dve, lambda e: e.tensor_tensor(out=y, in0=y, in1=tmp, op=ALU.mult), reads=[y, tmp], writes=[y])

        P.dma(sp, d_c[0], csb[:], cst_d[:, :], writes=[csb[:]])
        identf = TP[0][:, 0:128]
        P.dma(sp, d_c[1], identf, idn_d[:, :], writes=[identf])
        wsf = arena[:, 0:2048].bitcast(F32)
        bdrf = arena[:, 2048:4096].bitcast(F32)
        bdif = arena[:, 4096:6144].bitcast(F32)
        bsbc = arena[:, 6144:8192].bitcast(F32)
        P.dma(sp, d_c[2], wsf, wst_d[:, :], writes=[wsf])
        P.dma(sp, d_c[3], bdrf, bdr_d[:, :], writes=[bdrf])
        P.dma(sp, d_c[4], bdif, bdi_d[:, :], writes=[bdif])
        P.dma(sp, d_c[5], bsbc, bcv_d[0:1, 0:1024].partition_broadcast(128), writes=[bsbc])
        P.dma(sp, d_c[6], gpost[:].rearrange("p a b -> p (a b)"), bcv_d[0:1, 1024:4096].partition_broadcast(128),
              writes=[gpost[:]])

        P.op(dve, lambda e: e.tensor_copy(out=identb[:], in_=identf), reads=[identf], writes=[identb[:]])
        wsf3 = wsf.rearrange("p (g i) -> p g i", g=8)
        P.op(dve, lambda e: e.memset(wsf3[64:128, :, 0:64], 0.0), reads=[], writes=[wsf])
        P.op(dve, lambda e: e.tensor_copy(out=WsT[:].rearrange("p g i -> p (g i)"), in_=wsf), reads=[wsf], writes=[WsT[:]])
        P.op(dve, lambda e: e.tensor_copy(out=BDr[:].rearrange("p g i -> p (g i)"), in_=bdrf), reads=[bdrf], writes=[BDr[:]])
        P.op(dve, lambda e: e.tensor_copy(out=BDi[:].rearrange("p g i -> p (g i)"), in_=bdif), reads=[bdif], writes=[BDi[:]])
        ones = TP[1][:, 0:128]
        P.op(dve, lambda e: e.memset(ones, 1.0), writes=[ones])
        for h in range(2):
            bk = nbank()
            P.group(pe, [lambda e, bk=bk, h=h: e.matmul(bk[:], lhsT=ones, rhs=wsf[:, h * 512:(h + 1) * 512], start=True, stop=True)],
                    reads=[ones, wsf], writes=[bk[:]])
            for gg in range(4):
                g = h * 4 + gg
                P.op(dve, lambda e, bk=bk, g=g, gg=gg: e.scalar_tensor_tensor(
                    out=Bg[:, g, :], in0=bk[:, gg * 128:(gg + 1) * 128], scalar=csb[:, C_LNB + g:C_LNB + g + 1],
                    in1=bsbc[:, g * 128:(g + 1) * 128], op0=ALU.mult, op1=ALU.add),
                    reads=[bk[:], csb[:], bsbc], writes=[Bg[:, g, :]])
        for k in range(4):
            for ch in range(8):
                col = C_CVW + k * 8 + ch
                P.op(dve, lambda e, k=k, ch=ch, col=col: e.tensor_scalar(
                    out=Dcv[:, k, ch, :], in0=identf, scalar1=csb[:, col:col + 1], scalar2=None, op0=ALU.mult),
                    reads=[identf, csb[:]], writes=[Dcv[:, k, ch, :]])
        P.op(dve, lambda e: e.tensor_scalar(out=drv[:, 0:16], in0=csb[:, C_BR:C_BR + 16], scalar1=0.5, scalar2=None, op0=ALU.mult),
             reads=[csb[:]], writes=[drv[:, 0:16]])
        P.op(act, lambda e: e.activation(out=drv[:, 32:40], in_=csb[:, C_LAM:C_LAM + 8], func=AF.Exp, scale=-1.0),
             reads=[csb[:]], writes=[drv[:, 32:40]])
        P.op(act, lambda e: e.activation(out=drv[:, 32:40], in_=drv[:, 32:40], func=AF.Ln, bias=1.0),
             reads=[drv[:, 32:40]], writes=[drv[:, 32:40]])
        P.op(dve, lambda e: e.tensor_scalar(out=drv[:, 16:24], in0=drv[:, 32:40], scalar1=-8.0, scalar2=None, op0=ALU.mult),
             reads=[drv[:, 32:40]], writes=[drv[:, 16:24]])
        P.op(dve, lambda e: e.tensor_scalar(out=drv[:, 24:32], in0=drv[:, 32:40], scalar1=-4.0, scalar2=None, op0=ALU.mult),
             reads=[drv[:, 32:40]], writes=[drv[:, 24:32]])
        P.op(dve, lambda e: e.memset(hstate[:], 0.0), writes=[hstate[:]])
        P.op(dve, lambda e: e.memset(xrb[:], 0.0), writes=[xrb[:]])

        stage = [xs[:].rearrange("p a b -> p (a b)"), xin[:].rearrange("p a b -> p (a b)"), arena[:, 0:8192].bitcast(F32)]
        ld_sems = [d_xs, d_pin, d_c[0]]
        st_sems = [d_st[0], d_st[1], d_c[1]]
        cvt_engs = [act, dve]
        ci = 0
        last_store = {}
        def prep_load(si):
            s = SLABS[si]
            L = s["len"]
            sti = stage[si % 3]
            P.dma(sp, ld_sems[si % 3], sti[:, 0:L], wf_d[:, s["off"]:s["off"] + L], writes=[sti[:, 0:L]])

        prep_load(0)
        prep_load(1)
        for si, s in enumerate(SLABS):
            L = s["len"]
            sti = stage[si % 3]
            sto = wring[si % 3]
            if si + 2 < len(SLABS):
                prep_load(si + 2)
            for kc in range(s["nkc"]):
                col = (s["sc"] + kc) if s["per"] else s["sc"]
                lo, hi = kc * s["ncols"], (kc + 1) * s["ncols"]
                E = cvt_engs[ci % 2]
                ci += 1
                if E is act:
                    P.op(E, lambda e, sti=sti, sto=sto, lo=lo, hi=hi, col=col: e.activation(
                        out=sto[:, lo:hi], in_=sti[:, lo:hi], func=AF.Copy, scale=csb[:, col:col + 1]),
                        reads=[sti[:, lo:hi], csb[:]], writes=[sto[:, lo:hi]])
                else:
                    P.op(E, lambda e, sti=sti, sto=sto, lo=lo, hi=hi, col=col: e.tensor_scalar(
                        out=sto[:, lo:hi], in0=sti[:, lo:hi], scalar1=csb[:, col:col + 1], scalar2=None, op0=ALU.mult),
                        reads=[sti[:, lo:hi], csb[:]], writes=[sto[:, lo:hi]])
            last_store[si % 3] = P.dma(sp, st_sems[si % 3], wbf_d[:, s["off"]:s["off"] + L], sto[:, 0:L], reads=[sto[:, 0:L]])
        for k_, tk in last_store.items():
            sp.wait(tk[0], tk[1])

        SEQ = [0, 1, 4, 5]
        POS = {(0, 0): 0, (0, 1): 1, (0, 4): 2, (0, 5): 3}
        for t_ in range(NT):
            order = [2, 3, 6, 7] + list(range(8, 18))
            for sid in order:
                POS[(t_, sid)] = len(SEQ)
                SEQ.append(sid)
            if t_ + 1 < NT:
                for sid in (0, 1):
                    POS[(t_ + 1, sid)] = len(SEQ)
                    SEQ.append(sid)
            for sid in range(18, 36):
                POS[(t_, sid)] = len(SEQ)
                SEQ.append(sid)
            if t_ + 1 < NT:
                for sid in (4, 5):
                    POS[(t_ + 1, sid)] = len(SEQ)
                    SEQ.append(sid)
            for sid in range(36, 40):
                POS[(t_, sid)] = len(SEQ)
                SEQ.append(sid)
        NSEQ = len(SEQ)

        def gp(t_, sid):
            return POS[(t_, sid)]

        next_load = [0]

        def emit_load(gi):
            s = SLABS[SEQ[gi]]
            slot = gi % RING
            L = s["len"]
            P.dma(sp, d_ring[slot], wring[slot][:, 0:L], wbf_d[:, s["off"]:s["off"] + L], writes=[wring[slot][:, 0:L]])

        def loads_upto(gi):
            gi = min(gi, NSEQ - 1)
            while next_load[0] <= gi:
                emit_load(next_load[0])
                next_load[0] += 1

        def slab_view(gi):
            s = SLABS[SEQ[gi]]
            assert next_load[0] > gi
            return wring[gi % RING][:, 0:s["len"]].rearrange("p (k c) -> p k c", k=s["nkc"])

        def fm_group(bk, wv, j, hT, submajor=False):
            fns = []
            for kc in range(8):
                rhs = hT[:, :, kc, :] if submajor else hT[:, kc, :]
                fns.append(lambda e, kc=kc, rhs=rhs: e.matmul(bk[:], lhsT=wv[:, kc, j * 128:(j + 1) * 128], rhs=rhs,
                                                              start=(kc == 0), stop=(kc == 7)))
            P.group(pe, fns, reads=[wv, hT[:]], writes=[bk[:]])

        def load_x(t):
            src = x_d[t * T:(t + 1) * T, :].rearrange("(s p) f -> p s f", p=128)
            P.dma(sp, d_xs, xs[:], src, writes=[xs[:]])

        def load_p(t):
            src = p_d[t * T:(t + 1) * T, :].rearrange("(s p) f -> p s f", p=128)
            P.dma(sp, d_pin, pin[:], src, writes=[pin[:]])

        def mk_prenorm(src, hT, stc):
            ss = stt[:, stc:stc + 4]
            vv = stt[:, stc + 4:stc + 8]
            rs = stt[:, stc + 8:stc + 12]
            tm = stt[:, stc + 12:stc + 16]

            def stage1():
                for s in range(4):
                    jk = njunk()
                    P.op(act, lambda e, s=s, jk=jk: e.activation(out=jk[:], in_=src[:, s, :], func=AF.Square, scale=1.0 / 32.0,
                                                                 accum_out=ss[:, s:s + 1]),
                         reads=[src[:, s, :]], writes=[ss[:, s:s + 1], jk[:]])
                P.op(dve, lambda e: e.tensor_scalar(out=vv, in0=ss, scalar1=EPS, scalar2=None, op0=ALU.add),
                     reads=[ss], writes=[vv])
                rstd_newton(vv, rs, tm)

            def stage2():
                for s in range(4):
                    hn = nhn()
                    P.op(act, lambda e, s=s, hn=hn: e.activation(out=hn[:], in_=src[:, s, :], func=AF.Copy, scale=rs[:, s:s + 1]),
                         reads=[src[:, s, :], rs], writes=[hn[:]])
                    bk = nbank()
                    bkb = bk[:].bitcast(BF16)
                    fns = [lambda e, kc=kc, bkb=bkb, hn=hn: e.transpose(out=bkb[:, kc * 128:(kc + 1) * 128], in_=hn[:, kc * 128:(kc + 1) * 128], identity=identb[:])
                           for kc in range(8)]
                    P.group(pe, fns, reads=[hn[:], identb[:]], writes=[bk[:]])
                    P.op(dve, lambda e, s=s, bkb=bkb: e.tensor_copy(out=hT[:, :, s * 128:(s + 1) * 128],
                                                                     in_=bkb.rearrange("p (k t) -> p k t", k=8)),
                         reads=[bk[:]], writes=[hT[:, :, s * 128:(s + 1) * 128]])

            return stage1, stage2

        def mk_pre(s, stc):
            ss = stt[:, stc:stc + 1]
            vv = stt[:, stc + 1:stc + 2]
            rs = stt[:, stc + 2:stc + 3]
            tm = stt[:, stc + 3:stc + 5]
            st = {}

            def sq():
                jk = njunk()
                P.op(act, lambda e, jk=jk: e.activation(out=jk[:], in_=xin[:, s, :], func=AF.Square, scale=1.0 / 32.0, accum_out=ss),
                     reads=[xin[:, s, :]], writes=[ss, jk[:]])

            def newton():
                P.op(dve, lambda e: e.tensor_scalar(out=vv, in0=ss, scalar1=EPS, scalar2=None, op0=ALU.add), reads=[ss], writes=[vv])
                rstd_newton(vv, rs, tm, scalar_v=True)

            def hnf():
                hn = nhn()
                st["hn"] = hn
                P.op(act, lambda e, hn=hn: e.activation(out=hn[:], in_=xin[:, s, :], func=AF.Copy, scale=rs),
                     reads=[xin[:, s, :], rs], writes=[hn[:]])

            def tr():
                hn = st["hn"]
                bk = nbank()
                st["bk"] = bk
                bkb = bk[:].bitcast(BF16)
                fns = [lambda e, kc=kc, bkb=bkb, hn=hn: e.transpose(out=bkb[:, kc * 128:(kc + 1) * 128], in_=hn[:, kc * 128:(kc + 1) * 128], identity=identb[:])
                       for kc in range(8)]
                P.group(pe, fns, reads=[hn[:], identb[:]], writes=[bk[:]])

            def evac():
                bk = st["bk"]
                bkb = bk[:].bitcast(BF16)
                P.op(act, lambda e, bkb=bkb: e.activation(out=hTB[:, s, :, :], in_=bkb.rearrange("p (k t) -> p k t", k=8), func=AF.Copy),
                     reads=[bk[:]], writes=[hTB[:, s, :, :]])

            return sq, newton, hnf, tr, evac

        def p_transpose(pT):
            for s in range(4):
                hn = nhn()
                P.op(pool, lambda e, s=s, hn=hn: e.tensor_copy(out=hn[:, 0:256], in_=pin[:, s, :]), reads=[pin[:, s, :]], writes=[hn[:, 0:256]])
                bk = nbank()
                bkb = bk[:].bitcast(BF16)
                fns = [lambda e, kc=kc, bkb=bkb, hn=hn: e.transpose(out=bkb[:, kc * 128:(kc + 1) * 128], in_=hn[:, kc * 128:(kc + 1) * 128], identity=identb[:])
                       for kc in range(2)]
                P.group(pe, fns, reads=[hn[:, 0:256], identb[:]], writes=[bk[:]])
                P.op(dve, lambda e, s=s, bkb=bkb: e.tensor_copy(out=pT[:, :, s * 128:(s + 1) * 128],
                                                                 in_=bkb[:, 0:256].rearrange("p (k t) -> p k t", k=2)),
                     reads=[bk[:]], writes=[pT[:, :, s * 128:(s + 1) * 128]])

        def post_norm_res(src, gidx, base_sub, out_sub, stc):
            ss = stt[:, stc:stc + 1]
            vv = stt[:, stc + 1:stc + 2]
            rs = stt[:, stc + 2:stc + 3]
            tm = stt[:, stc + 3:stc + 5]
            jk = njunk()
            shp = list(src.shape)
            jkv = jk[:] if len(shp) == 2 else jk[:].rearrange("p (a b) -> p a b", a=2)
            P.op(act, lambda e: e.activation(out=jkv, in_=src, func=AF.Square, scale=1.0 / 32.0, accum_out=ss),
                 reads=[src], writes=[ss, jk[:]])
            P.op(dve, lambda e: e.tensor_scalar(out=vv, in0=ss, scalar1=EPS, scalar2=None, op0=ALU.add), reads=[ss], writes=[vv])
            rstd_newton(vv, rs, tm, scalar_v=True)
            gp = gpost[:, gidx, :] if len(shp) == 2 else gpost[:, gidx, :].rearrange("p (a b) -> p a b", a=2)
            tv = get2[:] if len(shp) == 2 else get2[:].rearrange("p (a b) -> p a b", a=2)
            P.op(dve, lambda e: e.scalar_tensor_tensor(out=tv, in0=src, scalar=rs, in1=gp, op0=ALU.mult, op1=ALU.mult),
                 reads=[src, rs, gpost[:, gidx, :]], writes=[get2[:]])
            P.op(pool, lambda e: e.tensor_tensor(out=out_sub, in0=base_sub, in1=get2[:], op=ALU.add),
                 reads=[base_sub, get2[:]], writes=[out_sub])

        hTBf = hTB[:].rearrange("p s k t -> p (s k t)")
        lru_a = hTBf[:, 0:2048].bitcast(F32).rearrange("p (a b) -> p a b", a=2)
        lru_a2 = hTBf[:, 2048:4096].bitcast(F32).rearrange("p (a b) -> p a b", a=2)
        lru_q = get[:].rearrange("p (a b) -> p a b", a=2)

        def fm_gelu(gi, dst, chunks=(0, 1, 2, 3), h=0):
            wv = slab_view(gi)
            for j in chunks:
                m = h * 4 + j
                bk = nbank()
                fm_group(bk, wv, j, hTA)
                P.op(act, lambda e, bk=bk, m=m, dst=dst: e.activation(out=dst[:, m, :], in_=bk[:], func=AF.Gelu_apprx_tanh),
                     reads=[bk[:]], writes=[dst[:, m, :]])

        def lru_batch(b0):
            for c in range(2):
                ch = b0 + c
                bkr, bki = nbank(), nbank()
                P.group(pe, [lambda e, bkr=bkr, ch=ch: e.matmul(bkr[:], lhsT=BDr[:, ch, :], rhs=xcm[:, ch, :], start=True, stop=True)],
                        reads=[BDr[:], xcm[:, ch, :]], writes=[bkr[:]])
                P.group(pe, [lambda e, bki=bki, ch=ch: e.matmul(bki[:], lhsT=BDi[:, ch, :], rhs=xcm[:, ch, :], start=True, stop=True)],
                        reads=[BDi[:], xcm[:, ch, :]], writes=[bki[:]])
                tr = ntp()
                P.op(act, lambda e, bkr=bkr, ch=ch, tr=tr: e.activation(out=tr[:], in_=bkr[:], func=AF.Tanh, scale=0.5,
                                                                        bias=drv[:, ch:ch + 1]),
                     reads=[bkr[:], drv[:]], writes=[tr[:]])
                P.op(act, lambda e, c=c, ch=ch, tr=tr: e.activation(out=lru_a[:, c, :], in_=tr[:], func=AF.Exp,
                                                                    scale=drv[:, 24 + ch:25 + ch], bias=drv[:, 24 + ch:25 + ch]),
                     reads=[tr[:], drv[:]], writes=[lru_a[:, c, :]])
                P.op(pool, lambda e, c=c: e.tensor_tensor(out=lru_a2[:, c, :], in0=lru_a[:, c, :], in1=lru_a[:, c, :], op=ALU.mult),
                     reads=[lru_a[:, c, :]], writes=[lru_a2[:, c, :]])
                ti = ntp()
                P.op(act, lambda e, bki=bki, ch=ch, ti=ti: e.activation(out=ti[:], in_=bki[:], func=AF.Tanh, scale=0.5,
                                                                        bias=drv[:, 8 + ch:9 + ch]),
                     reads=[bki[:], drv[:]], writes=[ti[:]])
                P.op(dve, lambda e, c=c, ch=ch, ti=ti: e.scalar_tensor_tensor(out=lru_q[:, c, :], in0=ti[:], scalar=1.0, in1=xcm[:, ch, :],
                                                                             op0=ALU.add, op1=ALU.mult),
                     reads=[ti[:], xcm[:, ch, :]], writes=[lru_q[:, c, :]])
            for c in range(2):
                P.op(act, lambda e, c=c: e.activation(out=lru_a2[:, c, :], in_=lru_a2[:, c, :], func=AF.Sqrt, scale=-0.25, bias=0.25),
                     reads=[lru_a2[:, c, :]], writes=[lru_a2[:, c, :]])
            for c in range(2):
                ch = b0 + c
                P.op(dve, lambda e, c=c: e.tensor_tensor(out=lru_q[:, c, :], in0=lru_q[:, c, :], in1=lru_a2[:, c, :], op=ALU.mult),
                     reads=[lru_q[:, c, :], lru_a2[:, c, :]], writes=[lru_q[:, c, :]])
                hb = ntp()
                P.op(dve, lambda e, c=c, ch=ch, hb=hb: e.tensor_tensor_scan(out=hb[:], data0=lru_a[:, c, :], data1=lru_q[:, c, :],
                                                                          initial=hstate[:, ch:ch + 1], op0=ALU.mult, op1=ALU.add),
                     reads=[lru_a[:, c, :], lru_q[:, c, :], hstate[:, ch:ch + 1]], writes=[hb[:]])
                P.op(dve, lambda e, ch=ch, hb=hb: e.tensor_copy(out=hstate[:, ch:ch + 1], in_=hb[:, 511:512]),
                     reads=[hb[:, 511:512]], writes=[hstate[:, ch:ch + 1]])
                P.op(pool, lambda e, ch=ch, hb=hb: e.tensor_tensor(out=gy[:, ch, :], in0=hb[:], in1=gy[:, ch, :], op=ALU.mult),
                     reads=[hb[:], gy[:, ch, :]], writes=[gy[:, ch, :]])

        def yr_half(t, h):
            CUR_LABEL[0] = "yr"
            gi = gp(t, 0 + h)
            loads_upto(gi)
            fm_gelu(gi, gy, h=h)
            loads_upto(gi + RING)

        def xr_half(t, h):
            CUR_LABEL[0] = "xr_conv"
            gi = gp(t, 4 + h)
            loads_upto(gi)
            wv = slab_view(gi)
            for j in range(4):
                ch = h * 4 + j
                bk = nbank()
                fm_group(bk, wv, j, hTA)
                P.op(act, lambda e, bk=bk, ch=ch: e.activation(out=xrb[:, ch, 3:515], in_=bk[:], func=AF.Copy),
                     reads=[bk[:]], writes=[xrb[:, ch, 3:515]])
            loads_upto(gi + RING)

        def conv_half(h):
            CUR_LABEL[0] = "xr_conv"
            for ch in range(4 * h, 4 * h + 4):
                bk = nbank()
                fns = [lambda e, k=k, bk=bk, ch=ch: e.matmul(bk[:], lhsT=Dcv[:, k, ch, :], rhs=xrb[:, ch, k:k + 512],
                                                             start=(k == 0), stop=(k == 3)) for k in range(4)]
                P.group(pe, fns, reads=[Dcv[:, :, ch, :], xrb[:, ch, 0:515]], writes=[bk[:]])
                P.op(act, lambda e, bk=bk, ch=ch: e.activation(out=xcm[:, ch, :], in_=bk[:], func=AF.Identity,
                                                                bias=csb[:, C_CVB + ch:C_CVB + ch + 1]),
                     reads=[bk[:], csb[:]], writes=[xcm[:, ch, :]])
                P.op(pool, lambda e, ch=ch: e.tensor_copy(out=xrb[:, ch, 0:3], in_=xrb[:, ch, 512:515]),
                     reads=[xrb[:, ch, 512:515]], writes=[xrb[:, ch, 0:3]])

        def phase_mix(t):
            hT = hTA
            nxt = t + 1 < NT
            for s in range(4):
                P.op(pool, lambda e, s=s: e.tensor_copy(out=xin[:, s, :], in_=xs[:, s, :]), reads=[xs[:, s, :]], writes=[xin[:, s, :]])
            if nxt:
                load_x(t + 1)
                pn1, pn2 = mk_prenorm(xs, hTA, 64)
            CUR_LABEL[0] = "v"
            giv = gp(t, 2)
            loads_upto(giv + 1)
            wv0, wv1 = slab_view(giv), slab_view(giv + 1)
            for s in range(4):
                for h, wv in ((0, wv0), (1, wv1)):
                    bk = nbank()
                    fns = [lambda e, kc=kc, bk=bk, wv=wv, s=s: e.matmul(bk[:], lhsT=hT[:, kc, s * 128:(s + 1) * 128], rhs=wv[:, kc, :],
                                                                       start=(kc == 0), stop=(kc == 7)) for kc in range(8)]
                    P.group(pe, fns, reads=[wv, hT[:]], writes=[bk[:]])
                    P.op(act, lambda e, bk=bk, s=s, h=h: e.activation(out=gv[:, s, h * 512:(h + 1) * 512], in_=bk[:], func=AF.Gelu_apprx_tanh),
                         reads=[bk[:]], writes=[gv[:, s, h * 512:(h + 1) * 512]])
                    P.op(dve, lambda e, s=s, h=h: e.bn_stats(out=mvs[:, s * 12 + h * 6:s * 12 + h * 6 + 6], in_=gv[:, s, h * 512:(h + 1) * 512]),
                         reads=[gv[:, s, h * 512:(h + 1) * 512]], writes=[mvs[:, s * 12 + h * 6:s * 12 + h * 6 + 6]])
                P.op(dve, lambda e, s=s: e.bn_aggr(out=mvs[:, 48 + 2 * s:50 + 2 * s], in_=mvs[:, s * 12:s * 12 + 12]),
                     reads=[mvs[:, s * 12:s * 12 + 12]], writes=[mvs[:, 48 + 2 * s:50 + 2 * s]])
            loads_upto(giv + 1 + RING)
            mv2 = mvs[:, 48:56].rearrange("p (s two) -> p s two", two=2)
            vv = stt[:, 48:52]
            rs = stt[:, 52:56]
            tm = stt[:, 56:60]
            P.op(dve, lambda e: e.tensor_scalar(out=vv, in0=mv2[:, :, 1], scalar1=EPS, scalar2=None, op0=ALU.add),
                 reads=[mvs[:, 48:56]], writes=[vv])
            rstd_newton(vv, rs, tm)
            for s in range(4):
                P.op(dve, lambda e, s=s: e.tensor_scalar(out=vln[:, s, :], in0=gv[:, s, :], scalar1=mvs[:, 48 + 2 * s:49 + 2 * s],
                                                          scalar2=rs[:, s:s + 1], op0=ALU.subtract, op1=ALU.mult),
                     reads=[gv[:, s, :], mvs[:, 48:56], rs], writes=[vln[:, s, :]])
            giu = gp(t, 6)
            loads_upto(giu + 1)
            for k in range(4):
                CUR_LABEL[0] = "lru"
                lru_batch(2 * k)
                CUR_LABEL[0] = "u"
                fm_gelu(giu + k // 2, gu, chunks=(2 * (k % 2), 2 * (k % 2) + 1), h=k // 2)
            loads_upto(giu + 1 + RING)
            if nxt:
                pn1()
            CUR_LABEL[0] = "spatial"
            for g in range(8):
                bk = nbank()
                fns = [lambda e, n=n, bk=bk, g=g: e.matmul(bk[:, n * 128:(n + 1) * 128], lhsT=vln[:, n, g * 128:(g + 1) * 128], rhs=WsT[:, g, :],
                                                           start=True, stop=True) for n in range(4)]
                P.group(pe, fns, reads=[vln[:], WsT[:]], writes=[bk[:]])
                tmp = ntp()
                bgb = Bg[:, g:g + 1, :].to_broadcast([128, 4, 128])
                P.op(dve, lambda e, bk=bk, g=g, tmp=tmp, bgb=bgb: e.scalar_tensor_tensor(
                    out=tmp[:].rearrange("p (n i) -> p n i", n=4), in0=bk[:].rearrange("p (n i) -> p n i", n=4),
                    scalar=csb[:, C_LNG + g:C_LNG + g + 1], in1=bgb, op0=ALU.mult, op1=ALU.add),
                    reads=[bk[:], csb[:], Bg[:, g, :]], writes=[tmp[:]])
                P.op(pool, lambda e, g=g, tmp=tmp: e.tensor_tensor(out=gu[:, g, :], in0=tmp[:], in1=gu[:, g, :], op=ALU.mult),
                     reads=[tmp[:], gu[:, g, :]], writes=[gu[:, g, :]])
            CUR_LABEL[0] = "merge"
            for h in range(2):
                gi = gp(t, 8 + 4 * h)
                loads_upto(gi + 3)
                wga, wgb, wa, wb = (slab_view(gi + i) for i in range(4))
                for j in range(4):
                    m = h * 4 + j
                    bga, bgb_, bpa, bpb = nbank(), nbank(), nbank(), nbank()
                    fm_group(bga, wga, j, hT)
                    fm_group(bgb_, wgb, j, hT)
                    fm_group(bpa, wa, j, gu)
                    fm_group(bpb, wb, j, gy)
                    ta, tb = ntp(), ntp()
                    P.op(act, lambda e, bga=bga, ta=ta: e.activation(out=ta[:], in_=bga[:], func=AF.Tanh, scale=0.5), reads=[bga[:]], writes=[ta[:]])
                    P.op(act, lambda e, bgb_=bgb_, tb=tb: e.activation(out=tb[:], in_=bgb_[:], func=AF.Tanh, scale=0.5), reads=[bgb_[:]], writes=[tb[:]])
                    P.op(dve, lambda e, ta=ta, bpa=bpa: e.scalar_tensor_tensor(out=ta[:], in0=ta[:], scalar=1.0, in1=bpa[:], op0=ALU.add, op1=ALU.mult),
                         reads=[ta[:], bpa[:]], writes=[ta[:]])
                    P.op(dve, lambda e, tb=tb, bpb=bpb: e.scalar_tensor_tensor(out=tb[:], in0=tb[:], scalar=1.0, in1=bpb[:], op0=ALU.add, op1=ALU.mult),
                         reads=[tb[:], bpb[:]], writes=[tb[:]])
                    P.op(pool, lambda e, ta=ta, tb=tb, m=m: e.tensor_tensor(out=xcm[:, m, :], in0=ta[:], in1=tb[:], op=ALU.add),
                         reads=[ta[:], tb[:]], writes=[xcm[:, m, :]])
                loads_upto(gi + 3 + RING)
            if nxt:
                CUR_LABEL[0] = "next_prenorm"
                pn2()
            CUR_LABEL[0] = "wout"
            gi = gp(t, 16)
            loads_upto(gi + 1)
            w0, w1 = slab_view(gi), slab_view(gi + 1)
            pres = [mk_pre(s_, 80 + 5 * s_) for s_ in range(4)]

            def wout_sub(s):
                CUR_LABEL[0] = "wout"
                pr = nbank_pair()
                for hh, wv in enumerate((w0, w1)):
                    fns = [lambda e, kc=kc, pr=pr, hh=hh, wv=wv, s=s: e.matmul(pr[:, hh, :], lhsT=xcm[:, kc, s * 128:(s + 1) * 128], rhs=wv[:, kc, :],
                                                                              start=(kc == 0), stop=(kc == 7)) for kc in range(8)]
                    P.group(pe, fns, reads=[wv, xcm[:]], writes=[pr[:, hh, :]])
                post_norm_res(pr, 0, xin[:, s, :], xin[:, s, :], 16 + 5 * s)

            def pst(s, k):
                CUR_LABEL[0] = "ffn_prenorm"
                pres[s][k]()

            wout_sub(0)
            wout_sub(1); pst(0, 0)
            wout_sub(2); pst(0, 1); pst(0, 2); pst(1, 0)
            wout_sub(3)
            loads_upto(gi + 1 + RING)
            if nxt:
                yr_half(t + 1, 0)
            pst(0, 3); pst(0, 4); pst(1, 1); pst(1, 2); pst(2, 0)
            if nxt:
                yr_half(t + 1, 1)
            pst(1, 3); pst(1, 4); pst(2, 1); pst(2, 2); pst(3, 0)
            pst(2, 3); pst(2, 4); pst(3, 1); pst(3, 2); pst(3, 3); pst(3, 4)
            if nxt:
                load_p(t + 1)

        def phase_ffn(t):
            nxt = t + 1 < NT
            CUR_LABEL[0] = "ffn_gateup"
            for jb in range(6):
                gi = gp(t, 18 + 2 * jb)
                loads_upto(gi + 1)
                wg, wu = slab_view(gi), slab_view(gi + 1)
                for jj in range(4 if jb < 5 else 2):
                    f = jb * 4 + jj
                    bg, bu = nbank(), nbank()
                    fm_group(bg, wg, jj, hTB, submajor=True)
                    fm_group(bu, wu, jj, hTB, submajor=True)
                    sg = ntp()
                    P.op(act, lambda e, bg=bg, sg=sg: e.activation(out=sg[:], in_=bg[:], func=AF.Silu), reads=[bg[:]], writes=[sg[:]])
                    P.op(dve, lambda e, bu=bu, sg=sg, f=f: e.tensor_tensor(out=act_v[:, f, :], in0=sg[:], in1=bu[:], op=ALU.mult),
                         reads=[sg[:], bu[:]], writes=[act_v[:, f, :]])
                loads_upto(gi + 1 + RING)
                if jb == 3 and nxt:
                    CUR_LABEL[0] = "p_transpose"
                    p_transpose(pTs[(t + 1) % 2])
                    CUR_LABEL[0] = "ffn_gateup"
            CUR_LABEL[0] = "down"
            prd = [nbank_pair() for _ in range(4)]
            for h in range(2):
                for pi_, (k0, nk) in enumerate(((0, 8), (8, 8), (16, 6))):
                    gi = gp(t, 30 + 3 * h + pi_)
                    loads_upto(gi)
                    wv = slab_view(gi)
                    for s in range(4):
                        bk = _Bk(prd[s][:, h, :])
                        fns = [lambda e, kc=kc, bk=bk, wv=wv, s=s, k0=k0: e.matmul(
                            bk[:], lhsT=act_v[:, k0 + kc, s * 128:(s + 1) * 128], rhs=wv[:, kc, :],
                            start=(k0 + kc == 0), stop=(k0 + kc == NF - 1)) for kc in range(nk)]
                        P.group(pe, fns, reads=[wv, act_v[:, k0:k0 + nk, :]], writes=[bk[:]], accum=(k0 > 0))
                    loads_upto(gi + RING)
            pT = pTs[t % 2]
            ple_w = {}

            def ple_sub(s):
                CUR_LABEL[0] = "ple"
                if not ple_w:
                    gi = gp(t, 36)
                    loads_upto(gi + 3)
                    ple_w["g"] = (slab_view(gi), slab_view(gi + 1))
                    ple_w["i"] = (slab_view(gi + 2), slab_view(gi + 3))
                for h in range(2):
                    wg, wi = ple_w["g"][h], ple_w["i"][h]
                    bg, be = nbank(), nbank()
                    fns = [lambda e, kc=kc, bg=bg, wg=wg: e.matmul(bg[:], lhsT=hTB[:, s, kc, :], rhs=wg[:, kc, :],
                                                                  start=(kc == 0), stop=(kc == 7)) for kc in range(8)]
                    P.group(pe, fns, reads=[wg, hTB[:, s, :, :]], writes=[bg[:]])
                    fns = [lambda e, kc=kc, be=be, wi=wi: e.matmul(be[:], lhsT=pT[:, kc, s * 128:(s + 1) * 128], rhs=wi[:, kc, :],
                                                                  start=(kc == 0), stop=(kc == 1)) for kc in range(2)]
                    P.group(pe, fns, reads=[wi, pT[:]], writes=[be[:]])
                    tg = ntp()
                    P.op(act, lambda e, bg=bg, tg=tg: e.activation(out=tg[:], in_=bg[:], func=AF.Tanh, scale=0.5), reads=[bg[:]], writes=[tg[:]])
                    P.op(dve, lambda e, tg=tg, be=be, h=h: e.scalar_tensor_tensor(out=get[:, h * 512:(h + 1) * 512], in0=tg[:], scalar=1.0, in1=be[:],
                                                                                 op0=ALU.add, op1=ALU.mult),
                         reads=[tg[:], be[:]], writes=[get[:, h * 512:(h + 1) * 512]])
                post_norm_res(get[:], 2, xin[:, s, :], xin[:, s, :], 36 + 5 * s)
                dst = y_d[t * T + s * 128:t * T + (s + 1) * 128, :]
                P.dma(pool, d_out[s], dst, xin[:, s, :], reads=[xin[:, s, :]])

            def pn(s):
                CUR_LABEL[0] = "down"
                post_norm_res(prd[s], 1, xin[:, s, :], xin[:, s, :], 16 + 5 * s)

            pres = [mk_pre(s_, 80 + 5 * s_) for s_ in range(4)]

            def pst(s, k):
                CUR_LABEL[0] = "ple_prenorm"
                pres[s][k]()

            pn(0); pn(1); pst(0, 0); pn(2); pst(0, 1); pst(0, 2); pst(1, 0); pn(3)
            if nxt:
                xr_half(t + 1, 0)
            pst(0, 3); pst(0, 4); pst(1, 1); pst(1, 2); pst(2, 0)
            if nxt:
                xr_half(t + 1, 1)
            pst(1, 3); pst(1, 4); ple_sub(0); pst(2, 1); pst(2, 2); pst(3, 0)
            if nxt:
                conv_half(0)
            pst(2, 3); pst(2, 4); ple_sub(1); pst(3, 1); pst(3, 2)
            if nxt:
                conv_half(1)
            pst(3, 3); pst(3, 4); ple_sub(2); ple_sub(3)
            loads_upto(gp(t, 39) + RING)

        loads_upto(RING - 1)
        load_x(0)
        load_p(0)
        p0a, p0b = mk_prenorm(xs, hTA, 0)
        p0a()
        p0b()
        p_transpose(pTs[0])
        yr_half(0, 0)
        yr_half(0, 1)
        xr_half(0, 0)
        xr_half(0, 1)
        conv_half(0)
        conv_half(1)
        for t in range(NT):
            phase_mix(t)
            phase_ffn(t)
        for dd in d_out:
            sp.wait(dd, dd.n)
            pool.wait(dd, dd.n)

        with nc.Block() as block:
            @block.sync
            def _(e):
                for f in sp.ops:
                    f(e)

            @block.tensor
            def _(e):
                for f in pe.ops:
                    f(e)

            @block.scalar
            def _(e):
                for f in act.ops:
                    f(e)

            @block.vector
            def _(e):
                for f in dve.ops:
                    f(e)

            @block.gpsimd
            def _(e):
                for f in pool.ops:
                    f(e)
    return nc


def _host_consts(inp):
    def colT(v):
        return np.ascontiguousarray(np.asarray(v, np.float32).reshape(8, 128).T)

    cst = np.zeros((128, NCST), np.float32)
    cst[:, C_GMIX:C_GMIX + 8] = colT(inp["norm_mix_pre"][0])
    cst[:, C_GFFN:C_GFFN + 8] = colT(inp["norm_ffn_pre"][0])
    cst[:, C_GPLE:C_GPLE + 8] = colT(inp["norm_ple_pre"][0])
    cst[:, C_HALF] = 0.5
    cst[:, C_ONE] = 1.0
    cst[:, C_LNG:C_LNG + 8] = colT(inp["gmlp_ln_g"][0])
    cst[:, C_LNB:C_LNB + 8] = colT(inp["gmlp_ln_b"][0])
    for k in range(4):
        cst[:, C_CVW + k * 8:C_CVW + k * 8 + 8] = colT(inp["conv_w"][0, k])
    cst[:, C_CVB:C_CVB + 8] = colT(inp["conv_b"][0])
    cst[:, C_BR:C_BR + 8] = colT(inp["lru_b_r"][0])
    cst[:, C_BI:C_BI + 8] = colT(inp["lru_b_i"][0])
    cst[:, C_LAM:C_LAM + 8] = colT(inp["lru_lambda"][0])
    wst = np.ascontiguousarray(np.asarray(inp["gmlp_w_s"][0], np.float32).transpose(2, 0, 1)).reshape(128, 1024)

    def bd(w):
        w = np.asarray(w, np.float32)
        out = np.zeros((128, 8, 128), np.float32)
        for hd in range(16):
            ch, hl = hd // 2, hd % 2
            out[hl * 64:(hl + 1) * 64, ch, hl * 64:(hl + 1) * 64] = w[hd]
        return out.reshape(128, 1024)

    bcv = np.concatenate([np.asarray(inp["gmlp_b_s"][0], np.float32).reshape(-1),
                          np.asarray(inp["norm_mix_post"][0], np.float32),
                          np.asarray(inp["norm_ffn_post"][0], np.float32),
                          np.asarray(inp["norm_ple_post"][0], np.float32)]).reshape(1, 4096)
    mats = {"w_in": np.asarray(inp["w_in"][0], np.float32), "w_a": np.asarray(inp["w_branch_a"][0], np.float32),
            "w_b": np.asarray(inp["w_branch_b"][0], np.float32), "w_out": np.asarray(inp["w_out"][0], np.float32),
            "f_gate": np.asarray(inp["ffn_w_gate"][0], np.float32), "f_up": np.asarray(inp["ffn_w_up"][0], np.float32),
            "f_down": np.asarray(inp["ffn_w_down"][0], np.float32), "p_gate": np.asarray(inp["ple_w_gate"][0], np.float32),
            "p_in": np.asarray(inp["ple_w_in"][0], np.float32)}
    return dict(wf=_pack_weights(mats), cst=cst, idn=np.eye(128, dtype=np.float32), wst=wst,
                bdr=bd(inp["lru_w_r"][0]), bdi=bd(inp["lru_w_i"][0]), bcv=bcv)


def kernel(**inputs):
    x = np.asarray(inputs["x"], np.float32)
    p = np.asarray(inputs["p"], np.float32)
    B = x.shape[0]
    NT = x.shape[1] // T
    shared = _host_consts(inputs)
    nc = build_nc(NT)
    in_maps = []
    for b in range(B):
        m = dict(shared)
        m["x"] = np.ascontiguousarray(x[b])
        m["p"] = np.ascontiguousarray(p[0, b])
        in_maps.append(m)
    res = run_bass_kernel_spmd(nc, in_maps, core_ids=list(range(B)))
    return np.stack([np.asarray(r["y"], np.float32) for r in res.results], axis=0)
```

```python
import numpy as np
from contextlib import ExitStack
import concourse.bass as bass
import concourse.mybir as mybir
from concourse.bass_utils import run_bass_kernel_spmd

F32 = mybir.dt.float32
BF16 = mybir.dt.bfloat16
I32 = mybir.dt.int32
AF = mybir.ActivationFunctionType
ALU = mybir.AluOpType

_ISZ = {F32: 4, BF16: 2, I32: 4}

S_TOK = 8192
D = 1024
NT_DEFAULT = 16
T = 512
DFF = 2816
NF = 22
MAGIC = 1597463007.0
EPS = 1e-6
RING = 5
SLOT = 4096


PE_LABELS = []
CUR_LABEL = ["setup"]


class Sem:
    def __init__(self, h):
        self.h = h
        self.n = 0


class Eng:
    def __init__(self, name, sem):
        self.name = name
        self.sem = sem
        self.ops = []
        self.seen = {}

    def wait(self, sem, val):
        if sem is self.sem and self.name == "pe":
            return
        if self.seen.get(sem, 0) >= val:
            return
        self.seen[sem] = val
        h = sem.h
        self.ops.append(lambda e, h=h, val=val: e.wait_ge(h, val))


def _rng(ap):
    a = ap.ap
    isz = _ISZ[ap.dtype]
    row = a[0][0]
    off = ap.offset
    lo = off % row if row > 0 else off
    span = 1
    for (st, cnt) in a[1:]:
        span += abs(st) * (cnt - 1)
    return ap.tensor.name, lo * isz, (lo + span) * isz


class Prog:
    def __init__(self):
        self.psum = set()
        self.recs = {}

    def deps_and_record(self, E, reads, writes, ticket_fn):
        acc = []
        for lst_, w in ((reads, False), (writes, True)):
            for ap in lst_:
                k, lo, hi = _rng(ap)
                if k in self.psum:
                    for bnk in range(lo // 2048, (hi - 1) // 2048 + 1):
                        acc.append(((k, bnk), 0, 1 << 30, w, True))
                else:
                    acc.append((k, lo, hi, w, False))
        waits = {}
        for (k, lo, hi, w, ps) in acc:
            for r in self.recs.get(k, ()):
                if r[1] <= lo or r[0] >= hi:
                    continue
                same = r[3] is E
                if same and E.name == "pe":
                    continue
                if (not w) and (not r[2]):
                    if not ps or same:
                        continue
                s_, v = r[4], r[5]
                if waits.get(s_, 0) < v:
                    waits[s_] = v
        for s_, v in waits.items():
            E.wait(s_, v)
        sem, val = ticket_fn()
        for (k, lo, hi, w, ps) in acc:
            lst = self.recs.setdefault(k, [])
            if w:
                lst[:] = [r for r in lst if not (r[0] >= lo and r[1] <= hi)]
            else:
                lst[:] = [r for r in lst if not ((not r[2]) and r[4] is sem and r[0] >= lo and r[1] <= hi)]
            lst.append([lo, hi, w, E, sem, val])

    def op(self, E, fn, reads=(), writes=()):
        def ticket():
            E.sem.n += 1
            return E.sem, E.sem.n
        self.deps_and_record(E, reads, writes, ticket)
        h = E.sem.h
        E.ops.append(lambda e, fn=fn, h=h: fn(e).then_inc(h, 1))

    def group(self, E, fns, reads=(), writes=(), accum=False):
        PE_LABELS.append((CUR_LABEL[0], len(fns)))
        if not accum:
            for ap in writes:
                k, lo, hi = _rng(ap)
                if k not in self.psum:
                    continue
                for bnk in range(lo // 2048, (hi - 1) // 2048 + 1):
                    lst = self.recs.get((k, bnk), ())
                    if lst and lst[-1][2] and lst[-1][3] is E:
                        raise AssertionError("PSUM bank %s overwritten before being read (%s)" % (bnk, CUR_LABEL[0]))

        def ticket():
            E.sem.n += 1
            return E.sem, E.sem.n
        self.deps_and_record(E, reads, writes, ticket)
        h = E.sem.h
        for fn in fns[:-1]:
            E.ops.append(fn)
        last = fns[-1]
        E.ops.append(lambda e, fn=last, h=h: fn(e).then_inc(h, 1))

    def dma(self, Q, dsem, out, in_, reads=(), writes=()):
        def ticket():
            dsem.n += 16
            return dsem, dsem.n
        self.deps_and_record(Q, reads, writes, ticket)
        h = dsem.h
        Q.ops.append(lambda e, out=out, in_=in_, h=h: e.dma_start(out=out, in_=in_).then_inc(h, 16))
        return dsem, dsem.n


C_GMIX, C_GFFN, C_GPLE, C_HALF, C_ONE = 0, 8, 16, 24, 25
C_LNG, C_LNB, C_CVW, C_CVB, C_BR, C_BI, C_LAM = 26, 34, 42, 74, 82, 90, 98
NCST = 106


def _slab_plan():
    sl = []

    def add(name, kc0, nkc, c0, nc_, sc, per):
        sl.append(dict(name=name, kc0=kc0, nkc=nkc, c0=c0, ncols=nc_, sc=sc, per=per))

    for c0 in (3072, 3584):
        add("w_in", 0, 8, c0, 512, C_GMIX, True)
    for c0 in (1024, 1536):
        add("w_in", 0, 8, c0, 512, C_GMIX, True)
    for c0 in (2048, 2560):
        add("w_in", 0, 8, c0, 512, C_GMIX, True)
    for c0 in (0, 512):
        add("w_in", 0, 8, c0, 512, C_GMIX, True)
    for h in range(2):
        add("w_in", 0, 8, 4096 + h * 512, 512, C_GMIX, True)
        add("w_in", 0, 8, 5120 + h * 512, 512, C_GMIX, True)
        add("w_a", 0, 8, h * 512, 512, C_ONE, False)
        add("w_b", 0, 8, h * 512, 512, C_ONE, False)
    for h in range(2):
        add("w_out", 0, 8, h * 512, 512, C_HALF, False)
    for j in range(6):
        ncols = 512 if j < 5 else 256
        add("f_gate", 0, 8, j * 512, ncols, C_GFFN, True)
        add("f_up", 0, 8, j * 512, ncols, C_GFFN, True)
    for h in range(2):
        for (k0, nk) in ((0, 8), (8, 8), (16, 6)):
            add("f_down", k0, nk, h * 512, 512, C_ONE, False)
    for h in range(2):
        add("p_gate", 0, 8, h * 512, 512, C_GPLE, True)
    for h in range(2):
        add("p_in", 0, 2, h * 512, 512, C_HALF, False)
    off = 0
    for s in sl:
        s["off"] = off
        s["len"] = s["nkc"] * s["ncols"]
        off += s["len"]
    return sl, off


SLABS, WTOT = _slab_plan()
NSL = len(SLABS)
(SL_YR, SL_V, SL_XR, SL_U, SL_MERGE, SL_WOUT, SL_FFN, SL_DOWN, SL_PGATE, SL_PIN) = (0, 2, 4, 6, 8, 16, 18, 30, 36, 38)


def _pack_weights(mats):
    wf = np.empty((128, WTOT), np.float32)
    for s in SLABS:
        W = mats[s["name"]]
        blk = W[s["kc0"] * 128:(s["kc0"] + s["nkc"]) * 128, s["c0"]:s["c0"] + s["ncols"]]
        blk = blk.reshape(s["nkc"], 128, s["ncols"]).transpose(1, 0, 2).reshape(128, -1)
        wf[:, s["off"]:s["off"] + s["len"]] = blk
    return wf


def build_nc(NT):
    nc = bass.Bass("TRN2", target_bir_lowering=False)
    ntok = NT * T
    x_d = nc.dram_tensor("x", [ntok, D], F32, kind="ExternalInput").ap()
    p_d = nc.dram_tensor("p", [ntok, 256], F32, kind="ExternalInput").ap()
    wf_d = nc.dram_tensor("wf", [128, WTOT], F32, kind="ExternalInput").ap()
    cst_d = nc.dram_tensor("cst", [128, NCST], F32, kind="ExternalInput").ap()
    idn_d = nc.dram_tensor("idn", [128, 128], F32, kind="ExternalInput").ap()
    wst_d = nc.dram_tensor("wst", [128, 1024], F32, kind="ExternalInput").ap()
    bdr_d = nc.dram_tensor("bdr", [128, 1024], F32, kind="ExternalInput").ap()
    bdi_d = nc.dram_tensor("bdi", [128, 1024], F32, kind="ExternalInput").ap()
    bcv_d = nc.dram_tensor("bcv", [1, 4096], F32, kind="ExternalInput").ap()
    y_d = nc.dram_tensor("y", [ntok, D], F32, kind="ExternalOutput").ap()
    wbf_d = nc.dram_tensor("wbf", [128, WTOT], BF16, kind="Internal").ap()

    with ExitStack() as es:
        def sb(name, shape, dt):
            return es.enter_context(nc.sbuf_tensor(name, shape, dt))

        def sem(name):
            return Sem(es.enter_context(nc.semaphore(name)))

        P = Prog()
        pe = Eng("pe", sem("s_pe"))
        act = Eng("act", sem("s_act"))
        dve = Eng("dve", sem("s_dve"))
        pool = Eng("pool", sem("s_pool"))
        sp = Eng("sp", None)

        psum_all = es.enter_context(nc.psum_tensor("psum", [128, 8, 512], F32))
        P.psum.add("psum")
        banks = [psum_all[:, i, :] for i in range(8)]
        bank_i = [0]

        class _Bk:
            def __init__(self, ap):
                self.ap = ap

            def __getitem__(self, key):
                return self.ap[key]

        def nbank():
            b = _Bk(banks[bank_i[0] % 8])
            bank_i[0] += 1
            return b

        def nbank_pair():
            if bank_i[0] % 2:
                bank_i[0] += 1
            i = bank_i[0] % 8
            bank_i[0] += 2
            return psum_all[:, i:i + 2, :]

        xs = sb("xs", [128, 4, 1024], F32)
        xin = sb("xin", [128, 4, 1024], F32)
        pin = sb("pin", [128, 4, 256], F32)
        pTs = [sb("pT%d" % i, [128, 2, 512], BF16) for i in range(2)]
        hns = [sb("hn%d" % i, [128, 1024], BF16) for i in range(2)]
        hn_i = [0]

        def nhn():
            h_ = hns[hn_i[0] % 2]
            hn_i[0] += 1
            return h_
        hTA = sb("hTA", [128, 8, 512], BF16)
        hTB = sb("hTB", [128, 4, 8, 128], BF16)
        gu = sb("gu", [128, 8, 512], BF16)
        gy = sb("gy", [128, 8, 512], BF16)
        xrb = sb("xrb", [128, 8, 516], BF16)
        xcm = sb("xcm", [128, 8, 512], BF16)
        arena = sb("arena", [128, 12288], BF16)
        TP = [sb("tp%d" % i, [128, 512], F32) for i in range(4)]
        get = sb("get", [128, 1024], F32)
        get2 = sb("get2", [128, 1024], F32)
        junks = [sb("junk%d" % i, [128, 1024], BF16) for i in range(2)]
        junk_i = [0]

        def njunk():
            j_ = junks[junk_i[0] % 2]
            junk_i[0] += 1
            return j_
        csb = sb("csb", [128, NCST], F32)
        drv = sb("drv", [128, 40], F32)
        identb = sb("identb", [128, 128], BF16)
        WsT = sb("WsT", [128, 8, 128], BF16)
        BDr = sb("BDr", [128, 8, 128], BF16)
        BDi = sb("BDi", [128, 8, 128], BF16)
        Dcv = sb("Dcv", [128, 4, 8, 128], BF16)
        Bg = sb("Bg", [128, 8, 128], F32)
        gpost = sb("gpost", [128, 3, 1024], F32)
        stt = sb("stt", [128, 128], F32)
        mvs = sb("mvs", [128, 64], F32)
        hstate = sb("hstate", [128, 8], F32)
        wring = [sb("wr%d" % i, [128, SLOT], BF16) for i in range(RING)]

        d_ring = [sem("d_wr%d" % i) for i in range(RING)]
        d_xs = sem("d_xs")
        d_pin = sem("d_pin")
        d_out = [sem("d_out%d" % i) for i in range(4)]
        d_c = [sem("d_c%d" % i) for i in range(7)]
        d_st = [sem("d_st%d" % i) for i in range(2)]

        tp_i = [0]

        def ntp():
            t_ = TP[tp_i[0] % len(TP)]
            tp_i[0] += 1
            return t_

        act_v = arena[:, 0:NF * 512].rearrange("p (a b) -> p a b", a=NF)
        vln = arena[:, 0:4096].rearrange("p (a b) -> p a b", a=4)
        gv = arena[:, 4096:12288].bitcast(F32).rearrange("p (a b) -> p a b", a=4)
        a_t = arena[:, 0:4096].bitcast(F32).rearrange("p (a b) -> p a b", a=4)
        a2_t = arena[:, 4096:8192].bitcast(F32).rearrange("p (a b) -> p a b", a=4)
        q_t = arena[:, 8192:12288].bitcast(F32).rearrange("p (a b) -> p a b", a=4)

        def rstd_newton(v, y, tmp, scalar_v=False):
            vi = v.bitcast(I32)
            yi = y.bitcast(I32)
            P.op(dve, lambda e: e.tensor_scalar(out=yi, in0=vi, scalar1=-0.5, scalar2=MAGIC, op0=ALU.mult, op1=ALU.add),
                 reads=[v], writes=[y])
            if scalar_v:
                hv = tmp[:, 0:1]
                t_ = tmp[:, 1:2]
                P.op(dve, lambda e: e.tensor_scalar(out=hv, in0=v, scalar1=-0.5, scalar2=None, op0=ALU.mult), reads=[v], writes=[hv])
                for _ in range(2):
                    P.op(dve, lambda e: e.scalar_tensor_tensor(out=t_, in0=y, scalar=hv, in1=y, op0=ALU.mult, op1=ALU.mult),
                         reads=[y, hv], writes=[t_])
                    P.op(dve, lambda e: e.scalar_tensor_tensor(out=y, in0=t_, scalar=1.5, in1=y, op0=ALU.add, op1=ALU.mult),
                         reads=[t_, y], writes=[y])
                return
            for _ in range(2):
                P.op(dve, lambda e: e.tensor_tensor(out=tmp, in0=y, in1=y, op=ALU.mult), reads=[y], writes=[tmp])
                P.op(dve, lambda e: e.tensor_tensor(out=tmp, in0=tmp, in1=v, op=ALU.mult), reads=[tmp, v], writes=[tmp])
                P.op(dve, lambda e: e.tensor_scalar(out=tmp, in0=tmp, scalar1=-0.5, scalar2=1.5, op0=ALU.mult, op1=ALU.add),
                     reads=[tmp], writes=[tmp])
                P.op(dve, lambda e: e.tensor_tensor(out=y, in0=y, in1=tmp, op=ALU.mult), reads=[y, tmp], writes=[y])

        P.dma(sp, d_c[0], csb[:], cst_d[:, :], writes=[csb[:]])
        identf = TP[0][:, 0:128]
        P.dma(sp, d_c[1], identf, idn_d[:, :], writes=[identf])
        wsf = arena[:, 0:2048].bitcast(F32)
        bdrf = arena[:, 2048:4096].bitcast(F32)
        bdif = arena[:, 4096:6144].bitcast(F32)
        bsbc = arena[:, 6144:8192].bitcast(F32)
        P.dma(sp, d_c[2], wsf, wst_d[:, :], writes=[wsf])
        P.dma(sp, d_c[3], bdrf, bdr_d[:, :], writes=[bdrf])
        P.dma(sp, d_c[4], bdif, bdi_d[:, :], writes=[bdif])
        P.dma(sp, d_c[5], bsbc, bcv_d[0:1, 0:1024].partition_broadcast(128), writes=[bsbc])
        P.dma(sp, d_c[6], gpost[:].rearrange("p a b -> p (a b)"), bcv_d[0:1, 1024:4096].partition_broadcast(128),
              writes=[gpost[:]])

        P.op(dve, lambda e: e.tensor_copy(out=identb[:], in_=identf), reads=[identf], writes=[identb[:]])
        wsf3 = wsf.rearrange("p (g i) -> p g i", g=8)
        P.op(dve, lambda e: e.memset(wsf3[64:128, :, 0:64], 0.0), reads=[], writes=[wsf])
        P.op(dve, lambda e: e.tensor_copy(out=WsT[:].rearrange("p g i -> p (g i)"), in_=wsf), reads=[wsf], writes=[WsT[:]])
        P.op(dve, lambda e: e.tensor_copy(out=BDr[:].rearrange("p g i -> p (g i)"), in_=bdrf), reads=[bdrf], writes=[BDr[:]])
        P.op(dve, lambda e: e.tensor_copy(out=BDi[:].rearrange("p g i -> p (g i)"), in_=bdif), reads=[bdif], writes=[BDi[:]])
        ones = TP[1][:, 0:128]
        P.op(dve, lambda e: e.memset(ones, 1.0), writes=[ones])
        for h in range(2):
            bk = nbank()
            P.group(pe, [lambda e, bk=bk, h=h: e.matmul(bk[:], lhsT=ones, rhs=wsf[:, h * 512:(h + 1) * 512], start=True, stop=True)],
                    reads=[ones, wsf], writes=[bk[:]])
            for gg in range(4):
                g = h * 4 + gg
                P.op(dve, lambda e, bk=bk, g=g, gg=gg: e.scalar_tensor_tensor(
                    out=Bg[:, g, :], in0=bk[:, gg * 128:(gg + 1) * 128], scalar=csb[:, C_LNB + g:C_LNB + g + 1],
                    in1=bsbc[:, g * 128:(g + 1) * 128], op0=ALU.mult, op1=ALU.add),
                    reads=[bk[:], csb[:], bsbc], writes=[Bg[:, g, :]])
        for k in range(4):
            for ch in range(8):
                col = C_CVW + k * 8 + ch
                P.op(dve, lambda e, k=k, ch=ch, col=col: e.tensor_scalar(
                    out=Dcv[:, k, ch, :], in0=identf, scalar1=csb[:, col:col + 1], scalar2=None, op0=ALU.mult),
                    reads=[identf, csb[:]], writes=[Dcv[:, k, ch, :]])
        P.op(dve, lambda e: e.tensor_scalar(out=drv[:, 0:16], in0=csb[:, C_BR:C_BR + 16], scalar1=0.5, scalar2=None, op0=ALU.mult),
             reads=[csb[:]], writes=[drv[:, 0:16]])
        P.op(act, lambda e: e.activation(out=drv[:, 32:40], in_=csb[:, C_LAM:C_LAM + 8], func=AF.Exp, scale=-1.0),
             reads=[csb[:]], writes=[drv[:, 32:40]])
        P.op(act, lambda e: e.activation(out=drv[:, 32:40], in_=drv[:, 32:40], func=AF.Ln, bias=1.0),
             reads=[drv[:, 32:40]], writes=[drv[:, 32:40]])
        P.op(dve, lambda e: e.tensor_scalar(out=drv[:, 16:24], in0=drv[:, 32:40], scalar1=-8.0, scalar2=None, op0=ALU.mult),
             reads=[drv[:, 32:40]], writes=[drv[:, 16:24]])
        P.op(dve, lambda e: e.tensor_scalar(out=drv[:, 24:32], in0=drv[:, 32:40], scalar1=-4.0, scalar2=None, op0=ALU.mult),
             reads=[drv[:, 32:40]], writes=[drv[:, 24:32]])
        P.op(dve, lambda e: e.memset(hstate[:], 0.0), writes=[hstate[:]])
        P.op(dve, lambda e: e.memset(xrb[:], 0.0), writes=[xrb[:]])

        stage = [xs[:].rearrange("p a b -> p (a b)"), xin[:].rearrange("p a b -> p (a b)")]
        cvt_engs = [act, dve]
        ci = 0
        last_store = {}
        def prep_load(si):
            s = SLABS[si]
            L = s["len"]
            sti = stage[si % 2]
            P.dma(sp, d_xs if si % 2 == 0 else d_pin, sti[:, 0:L], wf_d[:, s["off"]:s["off"] + L], writes=[sti[:, 0:L]])

        prep_load(0)
        for si, s in enumerate(SLABS):
            L = s["len"]
            sti = stage[si % 2]
            sto = wring[si % 2]
            if si + 1 < len(SLABS):
                prep_load(si + 1)
            for kc in range(s["nkc"]):
                col = (s["sc"] + kc) if s["per"] else s["sc"]
                lo, hi = kc * s["ncols"], (kc + 1) * s["ncols"]
                E = cvt_engs[ci % 2]
                ci += 1
                if E is act:
                    P.op(E, lambda e, sti=sti, sto=sto, lo=lo, hi=hi, col=col: e.activation(
                        out=sto[:, lo:hi], in_=sti[:, lo:hi], func=AF.Copy, scale=csb[:, col:col + 1]),
                        reads=[sti[:, lo:hi], csb[:]], writes=[sto[:, lo:hi]])
                else:
                    P.op(E, lambda e, sti=sti, sto=sto, lo=lo, hi=hi, col=col: e.tensor_scalar(
                        out=sto[:, lo:hi], in0=sti[:, lo:hi], scalar1=csb[:, col:col + 1], scalar2=None, op0=ALU.mult),
                        reads=[sti[:, lo:hi], csb[:]], writes=[sto[:, lo:hi]])
            last_store[si % 2] = P.dma(sp, d_st[si % 2], wbf_d[:, s["off"]:s["off"] + L], sto[:, 0:L], reads=[sto[:, 0:L]])
        for k_, tk in last_store.items():
            sp.wait(tk[0], tk[1])

        SEQ = [0, 1, 4, 5]
        POS = {(0, 0): 0, (0, 1): 1, (0, 4): 2, (0, 5): 3}
        for t_ in range(NT):
            order = [2, 3, 6, 7] + list(range(8, 18))
            for sid in order:
                POS[(t_, sid)] = len(SEQ)
                SEQ.append(sid)
            if t_ + 1 < NT:
                for sid in (0, 1):
                    POS[(t_ + 1, sid)] = len(SEQ)
                    SEQ.append(sid)
            for sid in range(18, 36):
                POS[(t_, sid)] = len(SEQ)
                SEQ.append(sid)
            if t_ + 1 < NT:
                for sid in (4, 5):
                    POS[(t_ + 1, sid)] = len(SEQ)
                    SEQ.append(sid)
            for sid in range(36, 40):
                POS[(t_, sid)] = len(SEQ)
                SEQ.append(sid)
        NSEQ = len(SEQ)

        def gp(t_, sid):
            return POS[(t_, sid)]

        next_load = [0]

        def emit_load(gi):
            s = SLABS[SEQ[gi]]
            slot = gi % RING
            L = s["len"]
            P.dma(sp, d_ring[slot], wring[slot][:, 0:L], wbf_d[:, s["off"]:s["off"] + L], writes=[wring[slot][:, 0:L]])

        def loads_upto(gi):
            gi = min(gi, NSEQ - 1)
            while next_load[0] <= gi:
                emit_load(next_load[0])
                next_load[0] += 1

        def slab_view(gi):
            s = SLABS[SEQ[gi]]
            assert next_load[0] > gi
            return wring[gi % RING][:, 0:s["len"]].rearrange("p (k c) -> p k c", k=s["nkc"])

        def fm_group(bk, wv, j, hT, submajor=False):
            fns = []
            for kc in range(8):
                rhs = hT[:, :, kc, :] if submajor else hT[:, kc, :]
                fns.append(lambda e, kc=kc, rhs=rhs: e.matmul(bk[:], lhsT=wv[:, kc, j * 128:(j + 1) * 128], rhs=rhs,
                                                              start=(kc == 0), stop=(kc == 7)))
            P.group(pe, fns, reads=[wv, hT[:]], writes=[bk[:]])

        def load_x(t):
            src = x_d[t * T:(t + 1) * T, :].rearrange("(s p) f -> p s f", p=128)
            P.dma(sp, d_xs, xs[:], src, writes=[xs[:]])

        def load_p(t):
            src = p_d[t * T:(t + 1) * T, :].rearrange("(s p) f -> p s f", p=128)
            P.dma(sp, d_pin, pin[:], src, writes=[pin[:]])

        def mk_prenorm(src, hT, stc):
            ss = stt[:, stc:stc + 4]
            vv = stt[:, stc + 4:stc + 8]
            rs = stt[:, stc + 8:stc + 12]
            tm = stt[:, stc + 12:stc + 16]

            def stage1():
                for s in range(4):
                    jk = njunk()
                    P.op(act, lambda e, s=s, jk=jk: e.activation(out=jk[:], in_=src[:, s, :], func=AF.Square, scale=1.0 / 32.0,
                                                                 accum_out=ss[:, s:s + 1]),
                         reads=[src[:, s, :]], writes=[ss[:, s:s + 1], jk[:]])
                P.op(dve, lambda e: e.tensor_scalar(out=vv, in0=ss, scalar1=EPS, scalar2=None, op0=ALU.add),
                     reads=[ss], writes=[vv])
                rstd_newton(vv, rs, tm)

            def stage2():
                for s in range(4):
                    hn = nhn()
                    P.op(act, lambda e, s=s, hn=hn: e.activation(out=hn[:], in_=src[:, s, :], func=AF.Copy, scale=rs[:, s:s + 1]),
                         reads=[src[:, s, :], rs], writes=[hn[:]])
                    bk = nbank()
                    bkb = bk[:].bitcast(BF16)
                    fns = [lambda e, kc=kc, bkb=bkb, hn=hn: e.transpose(out=bkb[:, kc * 128:(kc + 1) * 128], in_=hn[:, kc * 128:(kc + 1) * 128], identity=identb[:])
                           for kc in range(8)]
                    P.group(pe, fns, reads=[hn[:], identb[:]], writes=[bk[:]])
                    P.op(dve, lambda e, s=s, bkb=bkb: e.tensor_copy(out=hT[:, :, s * 128:(s + 1) * 128],
                                                                     in_=bkb.rearrange("p (k t) -> p k t", k=8)),
                         reads=[bk[:]], writes=[hT[:, :, s * 128:(s + 1) * 128]])

            return stage1, stage2

        def mk_pre(s, stc):
            ss = stt[:, stc:stc + 1]
            vv = stt[:, stc + 1:stc + 2]
            rs = stt[:, stc + 2:stc + 3]
            tm = stt[:, stc + 3:stc + 5]
            st = {}

            def sq():
                jk = njunk()
                P.op(act, lambda e, jk=jk: e.activation(out=jk[:], in_=xin[:, s, :], func=AF.Square, scale=1.0 / 32.0, accum_out=ss),
                     reads=[xin[:, s, :]], writes=[ss, jk[:]])

            def newton():
                P.op(dve, lambda e: e.tensor_scalar(out=vv, in0=ss, scalar1=EPS, scalar2=None, op0=ALU.add), reads=[ss], writes=[vv])
                rstd_newton(vv, rs, tm, scalar_v=True)

            def hnf():
                hn = nhn()
                st["hn"] = hn
                P.op(act, lambda e, hn=hn: e.activation(out=hn[:], in_=xin[:, s, :], func=AF.Copy, scale=rs),
                     reads=[xin[:, s, :], rs], writes=[hn[:]])

            def tr():
                hn = st["hn"]
                bk = nbank()
                st["bk"] = bk
                bkb = bk[:].bitcast(BF16)
                fns = [lambda e, kc=kc, bkb=bkb, hn=hn: e.transpose(out=bkb[:, kc * 128:(kc + 1) * 128], in_=hn[:, kc * 128:(kc + 1) * 128], identity=identb[:])
                       for kc in range(8)]
                P.group(pe, fns, reads=[hn[:], identb[:]], writes=[bk[:]])

            def evac():
                bk = st["bk"]
                bkb = bk[:].bitcast(BF16)
                P.op(act, lambda e, bkb=bkb: e.activation(out=hTB[:, s, :, :], in_=bkb.rearrange("p (k t) -> p k t", k=8), func=AF.Copy),
                     reads=[bk[:]], writes=[hTB[:, s, :, :]])

            return sq, newton, hnf, tr, evac

        def p_transpose(pT):
            for s in range(4):
                hn = nhn()
                P.op(pool, lambda e, s=s, hn=hn: e.tensor_copy(out=hn[:, 0:256], in_=pin[:, s, :]), reads=[pin[:, s, :]], writes=[hn[:, 0:256]])
                bk = nbank()
                bkb = bk[:].bitcast(BF16)
                fns = [lambda e, kc=kc, bkb=bkb, hn=hn: e.transpose(out=bkb[:, kc * 128:(kc + 1) * 128], in_=hn[:, kc * 128:(kc + 1) * 128], identity=identb[:])
                       for kc in range(2)]
                P.group(pe, fns, reads=[hn[:, 0:256], identb[:]], writes=[bk[:]])
                P.op(dve, lambda e, s=s, bkb=bkb: e.tensor_copy(out=pT[:, :, s * 128:(s + 1) * 128],
                                                                 in_=bkb[:, 0:256].rearrange("p (k t) -> p k t", k=2)),
                     reads=[bk[:]], writes=[pT[:, :, s * 128:(s + 1) * 128]])

        def post_norm_res(src, gidx, base_sub, out_sub, stc):
            ss = stt[:, stc:stc + 1]
            vv = stt[:, stc + 1:stc + 2]
            rs = stt[:, stc + 2:stc + 3]
            tm = stt[:, stc + 3:stc + 5]
            jk = njunk()
            shp = list(src.shape)
            jkv = jk[:] if len(shp) == 2 else jk[:].rearrange("p (a b) -> p a b", a=2)
            P.op(act, lambda e: e.activation(out=jkv, in_=src, func=AF.Square, scale=1.0 / 32.0, accum_out=ss),
                 reads=[src], writes=[ss, jk[:]])
            P.op(dve, lambda e: e.tensor_scalar(out=vv, in0=ss, scalar1=EPS, scalar2=None, op0=ALU.add), reads=[ss], writes=[vv])
            rstd_newton(vv, rs, tm, scalar_v=True)
            gp = gpost[:, gidx, :] if len(shp) == 2 else gpost[:, gidx, :].rearrange("p (a b) -> p a b", a=2)
            tv = get2[:] if len(shp) == 2 else get2[:].rearrange("p (a b) -> p a b", a=2)
            P.op(dve, lambda e: e.scalar_tensor_tensor(out=tv, in0=src, scalar=rs, in1=gp, op0=ALU.mult, op1=ALU.mult),
                 reads=[src, rs, gpost[:, gidx, :]], writes=[get2[:]])
            P.op(pool, lambda e: e.tensor_tensor(out=out_sub, in0=base_sub, in1=get2[:], op=ALU.add),
                 reads=[base_sub, get2[:]], writes=[out_sub])

        hTBf = hTB[:].rearrange("p s k t -> p (s k t)")
        lru_a = hTBf[:, 0:2048].bitcast(F32).rearrange("p (a b) -> p a b", a=2)
        lru_a2 = hTBf[:, 2048:4096].bitcast(F32).rearrange("p (a b) -> p a b", a=2)
        lru_q = get[:].rearrange("p (a b) -> p a b", a=2)

        def fm_gelu(gi, dst, chunks=(0, 1, 2, 3), h=0):
            wv = slab_view(gi)
            for j in chunks:
                m = h * 4 + j
                bk = nbank()
                fm_group(bk, wv, j, hTA)
                P.op(act, lambda e, bk=bk, m=m, dst=dst: e.activation(out=dst[:, m, :], in_=bk[:], func=AF.Gelu_apprx_tanh),
                     reads=[bk[:]], writes=[dst[:, m, :]])

        def lru_batch(b0):
            for c in range(2):
                ch = b0 + c
                bkr, bki = nbank(), nbank()
                P.group(pe, [lambda e, bkr=bkr, ch=ch: e.matmul(bkr[:], lhsT=BDr[:, ch, :], rhs=xcm[:, ch, :], start=True, stop=True)],
                        reads=[BDr[:], xcm[:, ch, :]], writes=[bkr[:]])
                P.group(pe, [lambda e, bki=bki, ch=ch: e.matmul(bki[:], lhsT=BDi[:, ch, :], rhs=xcm[:, ch, :], start=True, stop=True)],
                        reads=[BDi[:], xcm[:, ch, :]], writes=[bki[:]])
                tr = ntp()
                P.op(act, lambda e, bkr=bkr, ch=ch, tr=tr: e.activation(out=tr[:], in_=bkr[:], func=AF.Tanh, scale=0.5,
                                                                        bias=drv[:, ch:ch + 1]),
                     reads=[bkr[:], drv[:]], writes=[tr[:]])
                P.op(act, lambda e, c=c, ch=ch, tr=tr: e.activation(out=lru_a[:, c, :], in_=tr[:], func=AF.Exp,
                                                                    scale=drv[:, 24 + ch:25 + ch], bias=drv[:, 24 + ch:25 + ch]),
                     reads=[tr[:], drv[:]], writes=[lru_a[:, c, :]])
                P.op(pool, lambda e, c=c: e.tensor_tensor(out=lru_a2[:, c, :], in0=lru_a[:, c, :], in1=lru_a[:, c, :], op=ALU.mult),
                     reads=[lru_a[:, c, :]], writes=[lru_a2[:, c, :]])
                ti = ntp()
                P.op(act, lambda e, bki=bki, ch=ch, ti=ti: e.activation(out=ti[:], in_=bki[:], func=AF.Tanh, scale=0.5,
                                                                        bias=drv[:, 8 + ch:9 + ch]),
                     reads=[bki[:], drv[:]], writes=[ti[:]])
                P.op(dve, lambda e, c=c, ch=ch, ti=ti: e.scalar_tensor_tensor(out=lru_q[:, c, :], in0=ti[:], scalar=1.0, in1=xcm[:, ch, :],
                                                                             op0=ALU.add, op1=ALU.mult),
                     reads=[ti[:], xcm[:, ch, :]], writes=[lru_q[:, c, :]])
            for c in range(2):
                P.op(act, lambda e, c=c: e.activation(out=lru_a2[:, c, :], in_=lru_a2[:, c, :], func=AF.Sqrt, scale=-0.25, bias=0.25),
                     reads=[lru_a2[:, c, :]], writes=[lru_a2[:, c, :]])
            for c in range(2):
                ch = b0 + c
                P.op(dve, lambda e, c=c: e.tensor_tensor(out=lru_q[:, c, :], in0=lru_q[:, c, :], in1=lru_a2[:, c, :], op=ALU.mult),
                     reads=[lru_q[:, c, :], lru_a2[:, c, :]], writes=[lru_q[:, c, :]])
                hb = ntp()
                P.op(dve, lambda e, c=c, ch=ch, hb=hb: e.tensor_tensor_scan(out=hb[:], data0=lru_a[:, c, :], data1=lru_q[:, c, :],
                                                                          initial=hstate[:, ch:ch + 1], op0=ALU.mult, op1=ALU.add),
                     reads=[lru_a[:, c, :], lru_q[:, c, :], hstate[:, ch:ch + 1]], writes=[hb[:]])
                P.op(dve, lambda e, ch=ch, hb=hb: e.tensor_copy(out=hstate[:, ch:ch + 1], in_=hb[:, 511:512]),
                     reads=[hb[:, 511:512]], writes=[hstate[:, ch:ch + 1]])
                P.op(pool, lambda e, ch=ch, hb=hb: e.tensor_tensor(out=gy[:, ch, :], in0=hb[:], in1=gy[:, ch, :], op=ALU.mult),
                     reads=[hb[:], gy[:, ch, :]], writes=[gy[:, ch, :]])

        def yr_half(t, h):
            CUR_LABEL[0] = "yr"
            gi = gp(t, 0 + h)
            loads_upto(gi)
            fm_gelu(gi, gy, h=h)
            loads_upto(gi + RING)

        def xr_half(t, h):
            CUR_LABEL[0] = "xr_conv"
            gi = gp(t, 4 + h)
            loads_upto(gi)
            wv = slab_view(gi)
            for j in range(4):
                ch = h * 4 + j
                bk = nbank()
                fm_group(bk, wv, j, hTA)
                P.op(act, lambda e, bk=bk, ch=ch: e.activation(out=xrb[:, ch, 3:515], in_=bk[:], func=AF.Copy),
                     reads=[bk[:]], writes=[xrb[:, ch, 3:515]])
            loads_upto(gi + RING)

        def conv_half(h):
            CUR_LABEL[0] = "xr_conv"
            for ch in range(4 * h, 4 * h + 4):
                bk = nbank()
                fns = [lambda e, k=k, bk=bk, ch=ch: e.matmul(bk[:], lhsT=Dcv[:, k, ch, :], rhs=xrb[:, ch, k:k + 512],
                                                             start=(k == 0), stop=(k == 3)) for k in range(4)]
                P.group(pe, fns, reads=[Dcv[:, :, ch, :], xrb[:, ch, 0:515]], writes=[bk[:]])
                P.op(act, lambda e, bk=bk, ch=ch: e.activation(out=xcm[:, ch, :], in_=bk[:], func=AF.Identity,
                                                                bias=csb[:, C_CVB + ch:C_CVB + ch + 1]),
                     reads=[bk[:], csb[:]], writes=[xcm[:, ch, :]])
                P.op(pool, lambda e, ch=ch: e.tensor_copy(out=xrb[:, ch, 0:3], in_=xrb[:, ch, 512:515]),
                     reads=[xrb[:, ch, 512:515]], writes=[xrb[:, ch, 0:3]])

        def phase_mix(t):
            hT = hTA
            nxt = t + 1 < NT
            for s in range(4):
                P.op(pool, lambda e, s=s: e.tensor_copy(out=xin[:, s, :], in_=xs[:, s, :]), reads=[xs[:, s, :]], writes=[xin[:, s, :]])
            if nxt:
                load_x(t + 1)
                pn1, pn2 = mk_prenorm(xs, hTA, 64)
            CUR_LABEL[0] = "v"
            giv = gp(t, 2)
            loads_upto(giv + 1)
            wv0, wv1 = slab_view(giv), slab_view(giv + 1)
            for s in range(4):
                for h, wv in ((0, wv0), (1, wv1)):
                    bk = nbank()
                    fns = [lambda e, kc=kc, bk=bk, wv=wv, s=s: e.matmul(bk[:], lhsT=hT[:, kc, s * 128:(s + 1) * 128], rhs=wv[:, kc, :],
                                                                       start=(kc == 0), stop=(kc == 7)) for kc in range(8)]
                    P.group(pe, fns, reads=[wv, hT[:]], writes=[bk[:]])
                    P.op(act, lambda e, bk=bk, s=s, h=h: e.activation(out=gv[:, s, h * 512:(h + 1) * 512], in_=bk[:], func=AF.Gelu_apprx_tanh),
                         reads=[bk[:]], writes=[gv[:, s, h * 512:(h + 1) * 512]])
                    P.op(dve, lambda e, s=s, h=h: e.bn_stats(out=mvs[:, s * 12 + h * 6:s * 12 + h * 6 + 6], in_=gv[:, s, h * 512:(h + 1) * 512]),
                         reads=[gv[:, s, h * 512:(h + 1) * 512]], writes=[mvs[:, s * 12 + h * 6:s * 12 + h * 6 + 6]])
                P.op(dve, lambda e, s=s: e.bn_aggr(out=mvs[:, 48 + 2 * s:50 + 2 * s], in_=mvs[:, s * 12:s * 12 + 12]),
                     reads=[mvs[:, s * 12:s * 12 + 12]], writes=[mvs[:, 48 + 2 * s:50 + 2 * s]])
            loads_upto(giv + 1 + RING)
            mv2 = mvs[:, 48:56].rearrange("p (s two) -> p s two", two=2)
            vv = stt[:, 48:52]
            rs = stt[:, 52:56]
            tm = stt[:, 56:60]
            P.op(dve, lambda e: e.tensor_scalar(out=vv, in0=mv2[:, :, 1], scalar1=EPS, scalar2=None, op0=ALU.add),
                 reads=[mvs[:, 48:56]], writes=[vv])
            rstd_newton(vv, rs, tm)
            for s in range(4):
                P.op(dve, lambda e, s=s: e.tensor_scalar(out=vln[:, s, :], in0=gv[:, s, :], scalar1=mvs[:, 48 + 2 * s:49 + 2 * s],
                                                          scalar2=rs[:, s:s + 1], op0=ALU.subtract, op1=ALU.mult),
                     reads=[gv[:, s, :], mvs[:, 48:56], rs], writes=[vln[:, s, :]])
            giu = gp(t, 6)
            loads_upto(giu + 1)
            for k in range(4):
                CUR_LABEL[0] = "lru"
                lru_batch(2 * k)
                CUR_LABEL[0] = "u"
                fm_gelu(giu + k // 2, gu, chunks=(2 * (k % 2), 2 * (k % 2) + 1), h=k // 2)
            loads_upto(giu + 1 + RING)
            CUR_LABEL[0] = "spatial"
            for g in range(8):
                bk = nbank()
                fns = [lambda e, n=n, bk=bk, g=g: e.matmul(bk[:, n * 128:(n + 1) * 128], lhsT=vln[:, n, g * 128:(g + 1) * 128], rhs=WsT[:, g, :],
                                                           start=True, stop=True) for n in range(4)]
                P.group(pe, fns, reads=[vln[:], WsT[:]], writes=[bk[:]])
                tmp = ntp()
                bgb = Bg[:, g:g + 1, :].to_broadcast([128, 4, 128])
                P.op(dve, lambda e, bk=bk, g=g, tmp=tmp, bgb=bgb: e.scalar_tensor_tensor(
                    out=tmp[:].rearrange("p (n i) -> p n i", n=4), in0=bk[:].rearrange("p (n i) -> p n i", n=4),
                    scalar=csb[:, C_LNG + g:C_LNG + g + 1], in1=bgb, op0=ALU.mult, op1=ALU.add),
                    reads=[bk[:], csb[:], Bg[:, g, :]], writes=[tmp[:]])
                P.op(pool, lambda e, g=g, tmp=tmp: e.tensor_tensor(out=gu[:, g, :], in0=tmp[:], in1=gu[:, g, :], op=ALU.mult),
                     reads=[tmp[:], gu[:, g, :]], writes=[gu[:, g, :]])
            CUR_LABEL[0] = "merge"
            for h in range(2):
                gi = gp(t, 8 + 4 * h)
                loads_upto(gi + 3)
                wga, wgb, wa, wb = (slab_view(gi + i) for i in range(4))
                for j in range(4):
                    m = h * 4 + j
                    bga, bgb_, bpa, bpb = nbank(), nbank(), nbank(), nbank()
                    fm_group(bga, wga, j, hT)
                    fm_group(bgb_, wgb, j, hT)
                    fm_group(bpa, wa, j, gu)
                    fm_group(bpb, wb, j, gy)
                    ta, tb = ntp(), ntp()
                    P.op(act, lambda e, bga=bga, ta=ta: e.activation(out=ta[:], in_=bga[:], func=AF.Tanh, scale=0.5), reads=[bga[:]], writes=[ta[:]])
                    P.op(act, lambda e, bgb_=bgb_, tb=tb: e.activation(out=tb[:], in_=bgb_[:], func=AF.Tanh, scale=0.5), reads=[bgb_[:]], writes=[tb[:]])
                    P.op(dve, lambda e, ta=ta, bpa=bpa: e.scalar_tensor_tensor(out=ta[:], in0=ta[:], scalar=1.0, in1=bpa[:], op0=ALU.add, op1=ALU.mult),
                         reads=[ta[:], bpa[:]], writes=[ta[:]])
                    P.op(dve, lambda e, tb=tb, bpb=bpb: e.scalar_tensor_tensor(out=tb[:], in0=tb[:], scalar=1.0, in1=bpb[:], op0=ALU.add, op1=ALU.mult),
                         reads=[tb[:], bpb[:]], writes=[tb[:]])
                    P.op(pool, lambda e, ta=ta, tb=tb, m=m: e.tensor_tensor(out=xcm[:, m, :], in0=ta[:], in1=tb[:], op=ALU.add),
                         reads=[ta[:], tb[:]], writes=[xcm[:, m, :]])
                loads_upto(gi + 3 + RING)
                if h == 0 and nxt:
                    pn1()
                    CUR_LABEL[0] = "merge"
            if nxt:
                CUR_LABEL[0] = "next_prenorm"
                pn2()
            CUR_LABEL[0] = "wout"
            gi = gp(t, 16)
            loads_upto(gi + 1)
            w0, w1 = slab_view(gi), slab_view(gi + 1)
            pres = [mk_pre(s_, 80 + 5 * s_) for s_ in range(4)]

            def wout_sub(s):
                CUR_LABEL[0] = "wout"
                pr = nbank_pair()
                for hh, wv in enumerate((w0, w1)):
                    fns = [lambda e, kc=kc, pr=pr, hh=hh, wv=wv, s=s: e.matmul(pr[:, hh, :], lhsT=xcm[:, kc, s * 128:(s + 1) * 128], rhs=wv[:, kc, :],
                                                                              start=(kc == 0), stop=(kc == 7)) for kc in range(8)]
                    P.group(pe, fns, reads=[wv, xcm[:]], writes=[pr[:, hh, :]])
                post_norm_res(pr, 0, xin[:, s, :], xin[:, s, :], 16 + 5 * s)

            def pst(s, k):
                CUR_LABEL[0] = "ffn_prenorm"
                pres[s][k]()

            wout_sub(0)
            wout_sub(1); pst(0, 0)
            wout_sub(2); pst(0, 1); pst(0, 2); pst(1, 0)
            wout_sub(3)
            loads_upto(gi + 1 + RING)
            if nxt:
                yr_half(t + 1, 0)
            pst(0, 3); pst(0, 4); pst(1, 1); pst(1, 2); pst(2, 0)
            pst(1, 3); pst(1, 4); pst(2, 1); pst(2, 2); pst(3, 0); pst(3, 1); pst(3, 2)
            if nxt:
                yr_half(t + 1, 1)
            pst(2, 3); pst(2, 4); pst(3, 3); pst(3, 4)
            if nxt:
                load_p(t + 1)

        def phase_ffn(t):
            nxt = t + 1 < NT
            CUR_LABEL[0] = "ffn_gateup"
            for jb in range(6):
                gi = gp(t, 18 + 2 * jb)
                loads_upto(gi + 1)
                wg, wu = slab_view(gi), slab_view(gi + 1)
                for jj in range(4 if jb < 5 else 2):
                    f = jb * 4 + jj
                    bg, bu = nbank(), nbank()
                    fm_group(bg, wg, jj, hTB, submajor=True)
                    fm_group(bu, wu, jj, hTB, submajor=True)
                    sg = ntp()
                    P.op(act, lambda e, bg=bg, sg=sg: e.activation(out=sg[:], in_=bg[:], func=AF.Silu), reads=[bg[:]], writes=[sg[:]])
                    P.op(dve, lambda e, bu=bu, sg=sg, f=f: e.tensor_tensor(out=act_v[:, f, :], in0=sg[:], in1=bu[:], op=ALU.mult),
                         reads=[sg[:], bu[:]], writes=[act_v[:, f, :]])
                loads_upto(gi + 1 + RING)
                if jb == 3 and nxt:
                    CUR_LABEL[0] = "p_transpose"
                    p_transpose(pTs[(t + 1) % 2])
                    CUR_LABEL[0] = "ffn_gateup"
            CUR_LABEL[0] = "down"
            prd = [nbank_pair() for _ in range(4)]
            for h in range(2):
                for pi_, (k0, nk) in enumerate(((0, 8), (8, 8), (16, 6))):
                    gi = gp(t, 30 + 3 * h + pi_)
                    loads_upto(gi)
                    wv = slab_view(gi)
                    for s in range(4):
                        bk = _Bk(prd[s][:, h, :])
                        fns = [lambda e, kc=kc, bk=bk, wv=wv, s=s, k0=k0: e.matmul(
                            bk[:], lhsT=act_v[:, k0 + kc, s * 128:(s + 1) * 128], rhs=wv[:, kc, :],
                            start=(k0 + kc == 0), stop=(k0 + kc == NF - 1)) for kc in range(nk)]
                        P.group(pe, fns, reads=[wv, act_v[:, k0:k0 + nk, :]], writes=[bk[:]], accum=(k0 > 0))
                    loads_upto(gi + RING)
            pT = pTs[t % 2]
            ple_w = {}

            def ple_sub(s):
                CUR_LABEL[0] = "ple"
                if not ple_w:
                    gi = gp(t, 36)
                    loads_upto(gi + 3)
                    ple_w["g"] = (slab_view(gi), slab_view(gi + 1))
                    ple_w["i"] = (slab_view(gi + 2), slab_view(gi + 3))
                for h in range(2):
                    wg, wi = ple_w["g"][h], ple_w["i"][h]
                    bg, be = nbank(), nbank()
                    fns = [lambda e, kc=kc, bg=bg, wg=wg: e.matmul(bg[:], lhsT=hTB[:, s, kc, :], rhs=wg[:, kc, :],
                                                                  start=(kc == 0), stop=(kc == 7)) for kc in range(8)]
                    P.group(pe, fns, reads=[wg, hTB[:, s, :, :]], writes=[bg[:]])
                    fns = [lambda e, kc=kc, be=be, wi=wi: e.matmul(be[:], lhsT=pT[:, kc, s * 128:(s + 1) * 128], rhs=wi[:, kc, :],
                                                                  start=(kc == 0), stop=(kc == 1)) for kc in range(2)]
                    P.group(pe, fns, reads=[wi, pT[:]], writes=[be[:]])
                    tg = ntp()
                    P.op(act, lambda e, bg=bg, tg=tg: e.activation(out=tg[:], in_=bg[:], func=AF.Tanh, scale=0.5), reads=[bg[:]], writes=[tg[:]])
                    P.op(dve, lambda e, tg=tg, be=be, h=h: e.scalar_tensor_tensor(out=get[:, h * 512:(h + 1) * 512], in0=tg[:], scalar=1.0, in1=be[:],
                                                                                 op0=ALU.add, op1=ALU.mult),
                         reads=[tg[:], be[:]], writes=[get[:, h * 512:(h + 1) * 512]])
                post_norm_res(get[:], 2, xin[:, s, :], xin[:, s, :], 36 + 5 * s)
                dst = y_d[t * T + s * 128:t * T + (s + 1) * 128, :]
                P.dma(pool, d_out[s], dst, xin[:, s, :], reads=[xin[:, s, :]])

            def pn(s):
                CUR_LABEL[0] = "down"
                post_norm_res(prd[s], 1, xin[:, s, :], xin[:, s, :], 16 + 5 * s)

            pres = [mk_pre(s_, 80 + 5 * s_) for s_ in range(4)]

            def pst(s, k):
                CUR_LABEL[0] = "ple_prenorm"
                pres[s][k]()

            pn(0); pn(1); pst(0, 0); pn(2); pst(0, 1); pst(0, 2); pst(1, 0); pn(3)
            if nxt:
                xr_half(t + 1, 0)
            pst(0, 3); pst(0, 4); pst(1, 1); pst(1, 2); pst(2, 0)
            if nxt:
                xr_half(t + 1, 1)
            pst(1, 3); pst(1, 4); pst(2, 1); pst(2, 2); pst(3, 0)
            if nxt:
                conv_half(0)
            pst(2, 3); pst(2, 4); pst(3, 1); pst(3, 2)
            if nxt:
                conv_half(1)
            pst(3, 3); pst(3, 4)
            ple_sub(0); ple_sub(1); ple_sub(2); ple_sub(3)
            loads_upto(gp(t, 39) + RING)

        loads_upto(RING - 1)
        load_x(0)
        load_p(0)
        p0a, p0b = mk_prenorm(xs, hTA, 0)
        p0a()
        p0b()
        p_transpose(pTs[0])
        yr_half(0, 0)
        yr_half(0, 1)
        xr_half(0, 0)
        xr_half(0, 1)
        conv_half(0)
        conv_half(1)
        for t in range(NT):
            phase_mix(t)
            phase_ffn(t)
        for dd in d_out:
            sp.wait(dd, dd.n)
            pool.wait(dd, dd.n)

        with nc.Block() as block:
            @block.sync
            def _(e):
                for f in sp.ops:
                    f(e)

            @block.tensor
            def _(e):
                for f in pe.ops:
                    f(e)

            @block.scalar
            def _(e):
                for f in act.ops:
                    f(e)

            @block.vector
            def _(e):
                for f in dve.ops:
                    f(e)

            @block.gpsimd
            def _(e):
                for f in pool.ops:
                    f(e)
    return nc


def _host_consts(inp):
    def colT(v):
        return np.ascontiguousarray(np.asarray(v, np.float32).reshape(8, 128).T)

    cst = np.zeros((128, NCST), np.float32)
    cst[:, C_GMIX:C_GMIX + 8] = colT(inp["norm_mix_pre"][0])
    cst[:, C_GFFN:C_GFFN + 8] = colT(inp["norm_ffn_pre"][0])
    cst[:, C_GPLE:C_GPLE + 8] = colT(inp["norm_ple_pre"][0])
    cst[:, C_HALF] = 0.5
    cst[:, C_ONE] = 1.0
    cst[:, C_LNG:C_LNG + 8] = colT(inp["gmlp_ln_g"][0])
    cst[:, C_LNB:C_LNB + 8] = colT(inp["gmlp_ln_b"][0])
    for k in range(4):
        cst[:, C_CVW + k * 8:C_CVW + k * 8 + 8] = colT(inp["conv_w"][0, k])
    cst[:, C_CVB:C_CVB + 8] = colT(inp["conv_b"][0])
    cst[:, C_BR:C_BR + 8] = colT(inp["lru_b_r"][0])
    cst[:, C_BI:C_BI + 8] = colT(inp["lru_b_i"][0])
    cst[:, C_LAM:C_LAM + 8] = colT(inp["lru_lambda"][0])
    wst = np.ascontiguousarray(np.asarray(inp["gmlp_w_s"][0], np.float32).transpose(2, 0, 1)).reshape(128, 1024)

    def bd(w):
        w = np.asarray(w, np.float32)
        out = np.zeros((128, 8, 128), np.float32)
        for hd in range(16):
            ch, hl = hd // 2, hd % 2
            out[hl * 64:(hl + 1) * 64, ch, hl * 64:(hl + 1) * 64] = w[hd]
        return out.reshape(128, 1024)

    bcv = np.concatenate([np.asarray(inp["gmlp_b_s"][0], np.float32).reshape(-1),
                          np.asarray(inp["norm_mix_post"][0], np.float32),
                          np.asarray(inp["norm_ffn_post"][0], np.float32),
                          np.asarray(inp["norm_ple_post"][0], np.float32)]).reshape(1, 4096)
    mats = {"w_in": np.asarray(inp["w_in"][0], np.float32), "w_a": np.asarray(inp["w_branch_a"][0], np.float32),
            "w_b": np.asarray(inp["w_branch_b"][0], np.float32), "w_out": np.asarray(inp["w_out"][0], np.float32),
            "f_gate": np.asarray(inp["ffn_w_gate"][0], np.float32), "f_up": np.asarray(inp["ffn_w_up"][0], np.float32),
            "f_down": np.asarray(inp["ffn_w_down"][0], np.float32), "p_gate": np.asarray(inp["ple_w_gate"][0], np.float32),
            "p_in": np.asarray(inp["ple_w_in"][0], np.float32)}
    return dict(wf=_pack_weights(mats), cst=cst, idn=np.eye(128, dtype=np.float32), wst=wst,
                bdr=bd(inp["lru_w_r"][0]), bdi=bd(inp["lru_w_i"][0]), bcv=bcv)


def kernel(**inputs):
    x = np.asarray(inputs["x"], np.float32)
    p = np.asarray(inputs["p"], np.float32)
    B = x.shape[0]
    NT = x.shape[1] // T
    shared = _host_consts(inputs)
    nc = build_nc(NT)
    in_maps = []
    for b in range(B):
        m = dict(shared)
        m["x"] = np.ascontiguousarray(x[b])
        m["p"] = np.ascontiguousarray(p[0, b])
        in_maps.append(m)
    res = run_bass_kernel_spmd(nc, in_maps, core_ids=list(range(B)))
    return np.stack([np.asarray(r["y"], np.float32) for r in res.results], axis=0)
```

```python
import numpy as np
from contextlib import ExitStack
import concourse.bass as bass
import concourse.mybir as mybir
from concourse.bass_utils import run_bass_kernel_spmd

F32 = mybir.dt.float32
BF16 = mybir.dt.bfloat16
I32 = mybir.dt.int32
AF = mybir.ActivationFunctionType
ALU = mybir.AluOpType

_ISZ = {F32: 4, BF16: 2, I32: 4}

S_TOK = 8192
D = 1024
NT_DEFAULT = 16
T = 512
DFF = 2816
NF = 22
MAGIC = 1597463007.0
EPS = 1e-6
RING = 5
SLOT = 4096


PE_LABELS = []
CUR_LABEL = ["setup"]


class Sem:
    def __init__(self, h):
        self.h = h
        self.n = 0


class Eng:
    def __init__(self, name, sem):
        self.name = name
        self.sem = sem
        self.ops = []
        self.seen = {}

    def wait(self, sem, val):
        if sem is self.sem and self.name == "pe":
            return
        if self.seen.get(sem, 0) >= val:
            return
        self.seen[sem] = val
        h = sem.h
        self.ops.append(lambda e, h=h, val=val: e.wait_ge(h, val))


def _rng(ap):
    a = ap.ap
    isz = _ISZ[ap.dtype]
    row = a[0][0]
    off = ap.offset
    lo = off % row if row > 0 else off
    span = 1
    for (st, cnt) in a[1:]:
        span += abs(st) * (cnt - 1)
    return ap.tensor.name, lo * isz, (lo + span) * isz


class Prog:
    def __init__(self):
        self.psum = set()
        self.recs = {}

    def deps_and_record(self, E, reads, writes, ticket_fn):
        acc = []
        for lst_, w in ((reads, False), (writes, True)):
            for ap in lst_:
                k, lo, hi = _rng(ap)
                if k in self.psum:
                    for bnk in range(lo // 2048, (hi - 1) // 2048 + 1):
                        acc.append(((k, bnk), 0, 1 << 30, w, True))
                else:
                    acc.append((k, lo, hi, w, False))
        waits = {}
        for (k, lo, hi, w, ps) in acc:
            for r in self.recs.get(k, ()):
                if r[1] <= lo or r[0] >= hi:
                    continue
                same = r[3] is E
                if same and E.name == "pe":
                    continue
                if (not w) and (not r[2]):
                    if not ps or same:
                        continue
                s_, v = r[4], r[5]
                if waits.get(s_, 0) < v:
                    waits[s_] = v
        for s_, v in waits.items():
            E.wait(s_, v)
        sem, val = ticket_fn()
        for (k, lo, hi, w, ps) in acc:
            lst = self.recs.setdefault(k, [])
            if w:
                lst[:] = [r for r in lst if not (r[0] >= lo and r[1] <= hi)]
            else:
                lst[:] = [r for r in lst if not ((not r[2]) and r[4] is sem and r[0] >= lo and r[1] <= hi)]
            lst.append([lo, hi, w, E, sem, val])

    def op(self, E, fn, reads=(), writes=()):
        def ticket():
            E.sem.n += 1
            return E.sem, E.sem.n
        self.deps_and_record(E, reads, writes, ticket)
        h = E.sem.h
        E.ops.append(lambda e, fn=fn, h=h: fn(e).then_inc(h, 1))

    def group(self, E, fns, reads=(), writes=(), accum=False):
        PE_LABELS.append((CUR_LABEL[0], len(fns)))
        if not accum:
            for ap in writes:
                k, lo, hi = _rng(ap)
                if k not in self.psum:
                    continue
                for bnk in range(lo // 2048, (hi - 1) // 2048 + 1):
                    lst = self.recs.get((k, bnk), ())
                    if lst and lst[-1][2] and lst[-1][3] is E:
                        raise AssertionError("PSUM bank %s overwritten before being read (%s)" % (bnk, CUR_LABEL[0]))

        def ticket():
            E.sem.n += 1
            return E.sem, E.sem.n
        self.deps_and_record(E, reads, writes, ticket)
        h = E.sem.h
        for fn in fns[:-1]:
            E.ops.append(fn)
        last = fns[-1]
        E.ops.append(lambda e, fn=last, h=h: fn(e).then_inc(h, 1))

    def dma(self, Q, dsem, out, in_, reads=(), writes=()):
        def ticket():
            dsem.n += 16
            return dsem, dsem.n
        self.deps_and_record(Q, reads, writes, ticket)
        h = dsem.h
        Q.ops.append(lambda e, out=out, in_=in_, h=h: e.dma_start(out=out, in_=in_).then_inc(h, 16))
        return dsem, dsem.n


C_GMIX, C_GFFN, C_GPLE, C_HALF, C_ONE = 0, 8, 16, 24, 25
C_LNG, C_LNB, C_CVW, C_CVB, C_BR, C_BI, C_LAM = 26, 34, 42, 74, 82, 90, 98
NCST = 106


def _slab_plan():
    sl = []

    def add(name, kc0, nkc, c0, nc_, sc, per):
        sl.append(dict(name=name, kc0=kc0, nkc=nkc, c0=c0, ncols=nc_, sc=sc, per=per))

    for c0 in (3072, 3584):
        add("w_in", 0, 8, c0, 512, C_GMIX, True)
    for c0 in (1024, 1536):
        add("w_in", 0, 8, c0, 512, C_GMIX, True)
    for c0 in (2048, 2560):
        add("w_in", 0, 8, c0, 512, C_GMIX, True)
    for c0 in (0, 512):
        add("w_in", 0, 8, c0, 512, C_GMIX, True)
    for h in range(2):
        add("w_in", 0, 8, 4096 + h * 512, 512, C_GMIX, True)
        add("w_in", 0, 8, 5120 + h * 512, 512, C_GMIX, True)
        add("w_a", 0, 8, h * 512, 512, C_ONE, False)
        add("w_b", 0, 8, h * 512, 512, C_ONE, False)
    for h in range(2):
        add("w_out", 0, 8, h * 512, 512, C_HALF, False)
    for j in range(6):
        ncols = 512 if j < 5 else 256
        add("f_gate", 0, 8, j * 512, ncols, C_GFFN, True)
        add("f_up", 0, 8, j * 512, ncols, C_GFFN, True)
    for h in range(2):
        for (k0, nk) in ((0, 8), (8, 8), (16, 6)):
            add("f_down", k0, nk, h * 512, 512, C_ONE, False)
    for h in range(2):
        add("p_gate", 0, 8, h * 512, 512, C_GPLE, True)
    for h in range(2):
        add("p_in", 0, 2, h * 512, 512, C_HALF, False)
    off = 0
    for s in sl:
        s["off"] = off
        s["len"] = s["nkc"] * s["ncols"]
        off += s["len"]
    return sl, off


SLABS, WTOT = _slab_plan()
NSL = len(SLABS)
(SL_YR, SL_V, SL_XR, SL_U, SL_MERGE, SL_WOUT, SL_FFN, SL_DOWN, SL_PGATE, SL_PIN) = (0, 2, 4, 6, 8, 16, 18, 30, 36, 38)


def _pack_weights(mats):
    wf = np.empty((128, WTOT), np.float32)
    for s in SLABS:
        W = mats[s["name"]]
        blk = W[s["kc0"] * 128:(s["kc0"] + s["nkc"]) * 128, s["c0"]:s["c0"] + s["ncols"]]
        blk = blk.reshape(s["nkc"], 128, s["ncols"]).transpose(1, 0, 2).reshape(128, -1)
        wf[:, s["off"]:s["off"] + s["len"]] = blk
    return wf


def build_nc(NT):
    nc = bass.Bass("TRN2", target_bir_lowering=False)
    ntok = NT * T
    x_d = nc.dram_tensor("x", [ntok, D], F32, kind="ExternalInput").ap()
    p_d = nc.dram_tensor("p", [ntok, 256], F32, kind="ExternalInput").ap()
    wf_d = nc.dram_tensor("wf", [128, WTOT], F32, kind="ExternalInput").ap()
    cst_d = nc.dram_tensor("cst", [128, NCST], F32, kind="ExternalInput").ap()
    idn_d = nc.dram_tensor("idn", [128, 128], F32, kind="ExternalInput").ap()
    wst_d = nc.dram_tensor("wst", [128, 1024], F32, kind="ExternalInput").ap()
    bdr_d = nc.dram_tensor("bdr", [128, 1024], F32, kind="ExternalInput").ap()
    bdi_d = nc.dram_tensor("bdi", [128, 1024], F32, kind="ExternalInput").ap()
    bcv_d = nc.dram_tensor("bcv", [1, 4096], F32, kind="ExternalInput").ap()
    y_d = nc.dram_tensor("y", [ntok, D], F32, kind="ExternalOutput").ap()
    wbf_d = nc.dram_tensor("wbf", [128, WTOT], BF16, kind="Internal").ap()

    with ExitStack() as es:
        def sb(name, shape, dt):
            return es.enter_context(nc.sbuf_tensor(name, shape, dt))

        def sem(name):
            return Sem(es.enter_context(nc.semaphore(name)))

        P = Prog()
        pe = Eng("pe", sem("s_pe"))
        act = Eng("act", sem("s_act"))
        dve = Eng("dve", sem("s_dve"))
        pool = Eng("pool", sem("s_pool"))
        sp = Eng("sp", None)

        psum_all = es.enter_context(nc.psum_tensor("psum", [128, 8, 512], F32))
        P.psum.add("psum")
        banks = [psum_all[:, i, :] for i in range(8)]
        bank_i = [0]

        class _Bk:
            def __init__(self, ap):
                self.ap = ap

            def __getitem__(self, key):
                return self.ap[key]

        def nbank():
            b = _Bk(banks[bank_i[0] % 8])
            bank_i[0] += 1
            return b

        def nbank_pair():
            if bank_i[0] % 2:
                bank_i[0] += 1
            i = bank_i[0] % 8
            bank_i[0] += 2
            return psum_all[:, i:i + 2, :]

        xs = sb("xs", [128, 4, 1024], F32)
        xin = sb("xin", [128, 4, 1024], F32)
        pin = sb("pin", [128, 4, 256], F32)
        pTs = [sb("pT%d" % i, [128, 2, 512], BF16) for i in range(2)]
        hns = [sb("hn%d" % i, [128, 1024], BF16) for i in range(2)]
        hn_i = [0]

        def nhn():
            h_ = hns[hn_i[0] % 2]
            hn_i[0] += 1
            return h_
        hTA = sb("hTA", [128, 8, 512], BF16)
        hTB = sb("hTB", [128, 4, 8, 128], BF16)
        gu = sb("gu", [128, 8, 512], BF16)
        gy = sb("gy", [128, 8, 512], BF16)
        xrb = sb("xrb", [128, 8, 516], BF16)
        xcm = sb("xcm", [128, 8, 512], BF16)
        arena = sb("arena", [128, 12288], BF16)
        TP = [sb("tp%d" % i, [128, 512], F32) for i in range(4)]
        get = sb("get", [128, 1024], F32)
        get2 = sb("get2", [128, 1024], F32)
        junks = [sb("junk%d" % i, [128, 1024], BF16) for i in range(2)]
        junk_i = [0]

        def njunk():
            j_ = junks[junk_i[0] % 2]
            junk_i[0] += 1
            return j_
        csb = sb("csb", [128, NCST], F32)
        drv = sb("drv", [128, 40], F32)
        identb = sb("identb", [128, 128], BF16)
        WsT = sb("WsT", [128, 8, 128], BF16)
        BDr = sb("BDr", [128, 8, 128], BF16)
        BDi = sb("BDi", [128, 8, 128], BF16)
        Dcv = sb("Dcv", [128, 4, 8, 128], BF16)
        Bg = sb("Bg", [128, 8, 128], F32)
        gpost = sb("gpost", [128, 3, 1024], F32)
        stt = sb("stt", [128, 128], F32)
        mvs = sb("mvs", [128, 64], F32)
        hstate = sb("hstate", [128, 8], F32)
        wring = [sb("wr%d" % i, [128, SLOT], BF16) for i in range(RING)]

        d_ring = [sem("d_wr%d" % i) for i in range(RING)]
        d_xs = sem("d_xs")
        d_pin = sem("d_pin")
        d_out = [sem("d_out%d" % i) for i in range(4)]
        d_c = [sem("d_c%d" % i) for i in range(7)]
        d_st = [sem("d_st%d" % i) for i in range(2)]

        tp_i = [0]

        def ntp():
            t_ = TP[tp_i[0] % len(TP)]
            tp_i[0] += 1
            return t_

        act_v = arena[:, 0:NF * 512].rearrange("p (a b) -> p a b", a=NF)
        vln = arena[:, 0:4096].rearrange("p (a b) -> p a b", a=4)
        gv = arena[:, 4096:12288].bitcast(F32).rearrange("p (a b) -> p a b", a=4)
        a_t = arena[:, 0:4096].bitcast(F32).rearrange("p (a b) -> p a b", a=4)
        a2_t = arena[:, 4096:8192].bitcast(F32).rearrange("p (a b) -> p a b", a=4)
        q_t = arena[:, 8192:12288].bitcast(F32).rearrange("p (a b) -> p a b", a=4)

        def rstd_newton(v, y, tmp, scalar_v=False):
            vi = v.bitcast(I32)
            yi = y.bitcast(I32)
            P.op(dve, lambda e: e.tensor_scalar(out=yi, in0=vi, scalar1=-0.5, scalar2=MAGIC, op0=ALU.mult, op1=ALU.add),
                 reads=[v], writes=[y])
            if scalar_v:
                hv = tmp[:, 0:1]
                t_ = tmp[:, 1:2]
                P.op(dve, lambda e: e.tensor_scalar(out=hv, in0=v, scalar1=-0.5, scalar2=None, op0=ALU.mult), reads=[v], writes=[hv])
                for _ in range(2):
                    P.op(dve, lambda e: e.scalar_tensor_tensor(out=t_, in0=y, scalar=hv, in1=y, op0=ALU.mult, op1=ALU.mult),
                         reads=[y, hv], writes=[t_])
                    P.op(dve, lambda e: e.scalar_tensor_tensor(out=y, in0=t_, scalar=1.5, in1=y, op0=ALU.add, op1=ALU.mult),
                         reads=[t_, y], writes=[y])
                return
            for _ in range(2):
                P.op(dve, lambda e: e.tensor_tensor(out=tmp, in0=y, in1=y, op=ALU.mult), reads=[y], writes=[tmp])
                P.op(dve, lambda e: e.tensor_tensor(out=tmp, in0=tmp, in1=v, op=ALU.mult), reads=[tmp, v], writes=[tmp])
                P.op(dve, lambda e: e.tensor_scalar(out=tmp, in0=tmp, scalar1=-0.5, scalar2=1.5, op0=ALU.mult, op1=ALU.add),
                     reads=[tmp], writes=[tmp])
                P.op(dve, lambda e: e.tensor_tensor(out=y, in0=y, in1=tmp, op=ALU.mult), reads=[y, tmp], writes=[y])

        P.dma(sp, d_c[0], csb[:], cst_d[:, :], writes=[csb[:]])
        identf = TP[0][:, 0:128]
        P.dma(sp, d_c[1], identf, idn_d[:, :], writes=[identf])
        wsf = arena[:, 0:2048].bitcast(F32)
        bdrf = arena[:, 2048:4096].bitcast(F32)
        bdif = arena[:, 4096:6144].bitcast(F32)
        bsbc = arena[:, 6144:8192].bitcast(F32)
        P.dma(sp, d_c[2], wsf, wst_d[:, :], writes=[wsf])
        P.dma(sp, d_c[3], bdrf, bdr_d[:, :], writes=[bdrf])
        P.dma(sp, d_c[4], bdif, bdi_d[:, :], writes=[bdif])
        P.dma(sp, d_c[5], bsbc, bcv_d[0:1, 0:1024].partition_broadcast(128), writes=[bsbc])
        P.dma(sp, d_c[6], gpost[:].rearrange("p a b -> p (a b)"), bcv_d[0:1, 1024:4096].partition_broadcast(128),
              writes=[gpost[:]])

        P.op(dve, lambda e: e.tensor_copy(out=identb[:], in_=identf), reads=[identf], writes=[identb[:]])
        wsf3 = wsf.rearrange("p (g i) -> p g i", g=8)
        P.op(dve, lambda e: e.memset(wsf3[64:128, :, 0:64], 0.0), reads=[], writes=[wsf])
        P.op(dve, lambda e: e.tensor_copy(out=WsT[:].rearrange("p g i -> p (g i)"), in_=wsf), reads=[wsf], writes=[WsT[:]])
        P.op(dve, lambda e: e.tensor_copy(out=BDr[:].rearrange("p g i -> p (g i)"), in_=bdrf), reads=[bdrf], writes=[BDr[:]])
        P.op(dve, lambda e: e.tensor_copy(out=BDi[:].rearrange("p g i -> p (g i)"), in_=bdif), reads=[bdif], writes=[BDi[:]])
        ones = TP[1][:, 0:128]
        P.op(dve, lambda e: e.memset(ones, 1.0), writes=[ones])
        for h in range(2):
            bk = nbank()
            P.group(pe, [lambda e, bk=bk, h=h: e.matmul(bk[:], lhsT=ones, rhs=wsf[:, h * 512:(h + 1) * 512], start=True, stop=True)],
                    reads=[ones, wsf], writes=[bk[:]])
            for gg in range(4):
                g = h * 4 + gg
                P.op(dve, lambda e, bk=bk, g=g, gg=gg: e.scalar_tensor_tensor(
                    out=Bg[:, g, :], in0=bk[:, gg * 128:(gg + 1) * 128], scalar=csb[:, C_LNB + g:C_LNB + g + 1],
                    in1=bsbc[:, g * 128:(g + 1) * 128], op0=ALU.mult, op1=ALU.add),
                    reads=[bk[:], csb[:], bsbc], writes=[Bg[:, g, :]])
        for k in range(4):
            for ch in range(8):
                col = C_CVW + k * 8 + ch
                P.op(dve, lambda e, k=k, ch=ch, col=col: e.tensor_scalar(
                    out=Dcv[:, k, ch, :], in0=identf, scalar1=csb[:, col:col + 1], scalar2=None, op0=ALU.mult),
                    reads=[identf, csb[:]], writes=[Dcv[:, k, ch, :]])
        P.op(dve, lambda e: e.tensor_scalar(out=drv[:, 0:16], in0=csb[:, C_BR:C_BR + 16], scalar1=0.5, scalar2=None, op0=ALU.mult),
             reads=[csb[:]], writes=[drv[:, 0:16]])
        P.op(act, lambda e: e.activation(out=drv[:, 32:40], in_=csb[:, C_LAM:C_LAM + 8], func=AF.Exp, scale=-1.0),
             reads=[csb[:]], writes=[drv[:, 32:40]])
        P.op(act, lambda e: e.activation(out=drv[:, 32:40], in_=drv[:, 32:40], func=AF.Ln, bias=1.0),
             reads=[drv[:, 32:40]], writes=[drv[:, 32:40]])
        P.op(dve, lambda e: e.tensor_scalar(out=drv[:, 16:24], in0=drv[:, 32:40], scalar1=-8.0, scalar2=None, op0=ALU.mult),
             reads=[drv[:, 32:40]], writes=[drv[:, 16:24]])
        P.op(dve, lambda e: e.tensor_scalar(out=drv[:, 24:32], in0=drv[:, 32:40], scalar1=-4.0, scalar2=None, op0=ALU.mult),
             reads=[drv[:, 32:40]], writes=[drv[:, 24:32]])
        P.op(dve, lambda e: e.memset(hstate[:], 0.0), writes=[hstate[:]])
        P.op(dve, lambda e: e.memset(xrb[:], 0.0), writes=[xrb[:]])

        stage = [xs[:].rearrange("p a b -> p (a b)"), xin[:].rearrange("p a b -> p (a b)")]
        cvt_engs = [act, dve]
        ci = 0
        last_store = {}
        def prep_load(si):
            s = SLABS[si]
            L = s["len"]
            sti = stage[si % 2]
            P.dma(sp, d_xs if si % 2 == 0 else d_pin, sti[:, 0:L], wf_d[:, s["off"]:s["off"] + L], writes=[sti[:, 0:L]])

        prep_load(0)
        for si, s in enumerate(SLABS):
            L = s["len"]
            sti = stage[si % 2]
            sto = wring[si % 2]
            if si + 1 < len(SLABS):
                prep_load(si + 1)
            for kc in range(s["nkc"]):
                col = (s["sc"] + kc) if s["per"] else s["sc"]
                lo, hi = kc * s["ncols"], (kc + 1) * s["ncols"]
                E = cvt_engs[ci % 2]
                ci += 1
                if E is act:
                    P.op(E, lambda e, sti=sti, sto=sto, lo=lo, hi=hi, col=col: e.activation(
                        out=sto[:, lo:hi], in_=sti[:, lo:hi], func=AF.Copy, scale=csb[:, col:col + 1]),
                        reads=[sti[:, lo:hi], csb[:]], writes=[sto[:, lo:hi]])
                else:
                    P.op(E, lambda e, sti=sti, sto=sto, lo=lo, hi=hi, col=col: e.tensor_scalar(
                        out=sto[:, lo:hi], in0=sti[:, lo:hi], scalar1=csb[:, col:col + 1], scalar2=None, op0=ALU.mult),
                        reads=[sti[:, lo:hi], csb[:]], writes=[sto[:, lo:hi]])
            last_store[si % 2] = P.dma(sp, d_st[si % 2], wbf_d[:, s["off"]:s["off"] + L], sto[:, 0:L], reads=[sto[:, 0:L]])
        for k_, tk in last_store.items():
            sp.wait(tk[0], tk[1])

        SEQ = [0, 1, 4, 5]
        POS = {(0, 0): 0, (0, 1): 1, (0, 4): 2, (0, 5): 3}
        for t_ in range(NT):
            order = [2, 3, 6, 7] + list(range(8, 18))
            for sid in order:
                POS[(t_, sid)] = len(SEQ)
                SEQ.append(sid)
            if t_ + 1 < NT:
                for sid in (0, 1):
                    POS[(t_ + 1, sid)] = len(SEQ)
                    SEQ.append(sid)
            for sid in range(18, 36):
                POS[(t_, sid)] = len(SEQ)
                SEQ.append(sid)
            if t_ + 1 < NT:
                for sid in (4, 5):
                    POS[(t_ + 1, sid)] = len(SEQ)
                    SEQ.append(sid)
            for sid in range(36, 40):
                POS[(t_, sid)] = len(SEQ)
                SEQ.append(sid)
        NSEQ = len(SEQ)

        def gp(t_, sid):
            return POS[(t_, sid)]

        next_load = [0]

        def emit_load(gi):
            s = SLABS[SEQ[gi]]
            slot = gi % RING
            L = s["len"]
            P.dma(sp, d_ring[slot], wring[slot][:, 0:L], wbf_d[:, s["off"]:s["off"] + L], writes=[wring[slot][:, 0:L]])

        def loads_upto(gi):
            gi = min(gi, NSEQ - 1)
            while next_load[0] <= gi:
                emit_load(next_load[0])
                next_load[0] += 1

        def slab_view(gi):
            s = SLABS[SEQ[gi]]
            assert next_load[0] > gi
            return wring[gi % RING][:, 0:s["len"]].rearrange("p (k c) -> p k c", k=s["nkc"])

        def fm_group(bk, wv, j, hT, submajor=False):
            fns = []
            for kc in range(8):
                rhs = hT[:, :, kc, :] if submajor else hT[:, kc, :]
                fns.append(lambda e, kc=kc, rhs=rhs: e.matmul(bk[:], lhsT=wv[:, kc, j * 128:(j + 1) * 128], rhs=rhs,
                                                              start=(kc == 0), stop=(kc == 7)))
            P.group(pe, fns, reads=[wv, hT[:]], writes=[bk[:]])

        def load_x(t):
            src = x_d[t * T:(t + 1) * T, :].rearrange("(s p) f -> p s f", p=128)
            P.dma(sp, d_xs, xs[:], src, writes=[xs[:]])

        def load_p(t):
            src = p_d[t * T:(t + 1) * T, :].rearrange("(s p) f -> p s f", p=128)
            P.dma(sp, d_pin, pin[:], src, writes=[pin[:]])

        def mk_prenorm(src, hT, stc):
            ss = stt[:, stc:stc + 4]
            vv = stt[:, stc + 4:stc + 8]
            rs = stt[:, stc + 8:stc + 12]
            tm = stt[:, stc + 12:stc + 16]

            def stage1():
                for s in range(4):
                    jk = njunk()
                    P.op(act, lambda e, s=s, jk=jk: e.activation(out=jk[:], in_=src[:, s, :], func=AF.Square, scale=1.0 / 32.0,
                                                                 accum_out=ss[:, s:s + 1]),
                         reads=[src[:, s, :]], writes=[ss[:, s:s + 1], jk[:]])
                P.op(dve, lambda e: e.tensor_scalar(out=vv, in0=ss, scalar1=EPS, scalar2=None, op0=ALU.add),
                     reads=[ss], writes=[vv])
                rstd_newton(vv, rs, tm)

            def stage2():
                for s in range(4):
                    hn = nhn()
                    P.op(act, lambda e, s=s, hn=hn: e.activation(out=hn[:], in_=src[:, s, :], func=AF.Copy, scale=rs[:, s:s + 1]),
                         reads=[src[:, s, :], rs], writes=[hn[:]])
                    bk = nbank()
                    bkb = bk[:].bitcast(BF16)
                    fns = [lambda e, kc=kc, bkb=bkb, hn=hn: e.transpose(out=bkb[:, kc * 128:(kc + 1) * 128], in_=hn[:, kc * 128:(kc + 1) * 128], identity=identb[:])
                           for kc in range(8)]
                    P.group(pe, fns, reads=[hn[:], identb[:]], writes=[bk[:]])
                    P.op(dve, lambda e, s=s, bkb=bkb: e.tensor_copy(out=hT[:, :, s * 128:(s + 1) * 128],
                                                                     in_=bkb.rearrange("p (k t) -> p k t", k=8)),
                         reads=[bk[:]], writes=[hT[:, :, s * 128:(s + 1) * 128]])

            return stage1, stage2

        def mk_pre(s, stc):
            ss = stt[:, stc:stc + 1]
            vv = stt[:, stc + 1:stc + 2]
            rs = stt[:, stc + 2:stc + 3]
            tm = stt[:, stc + 3:stc + 5]
            st = {}

            def sq():
                jk = njunk()
                P.op(act, lambda e, jk=jk: e.activation(out=jk[:], in_=xin[:, s, :], func=AF.Square, scale=1.0 / 32.0, accum_out=ss),
                     reads=[xin[:, s, :]], writes=[ss, jk[:]])

            def newton():
                P.op(dve, lambda e: e.tensor_scalar(out=vv, in0=ss, scalar1=EPS, scalar2=None, op0=ALU.add), reads=[ss], writes=[vv])
                rstd_newton(vv, rs, tm, scalar_v=True)

            def hnf():
                hn = nhn()
                st["hn"] = hn
                P.op(act, lambda e, hn=hn: e.activation(out=hn[:], in_=xin[:, s, :], func=AF.Copy, scale=rs),
                     reads=[xin[:, s, :], rs], writes=[hn[:]])

            def tr():
                hn = st["hn"]
                bk = nbank()
                st["bk"] = bk
                bkb = bk[:].bitcast(BF16)
                fns = [lambda e, kc=kc, bkb=bkb, hn=hn: e.transpose(out=bkb[:, kc * 128:(kc + 1) * 128], in_=hn[:, kc * 128:(kc + 1) * 128], identity=identb[:])
                       for kc in range(8)]
                P.group(pe, fns, reads=[hn[:], identb[:]], writes=[bk[:]])

            def evac():
                bk = st["bk"]
                bkb = bk[:].bitcast(BF16)
                P.op(act, lambda e, bkb=bkb: e.activation(out=hTB[:, s, :, :], in_=bkb.rearrange("p (k t) -> p k t", k=8), func=AF.Copy),
                     reads=[bk[:]], writes=[hTB[:, s, :, :]])

            return sq, newton, hnf, tr, evac

        def p_transpose(pT):
            for s in range(4):
                hn = nhn()
                P.op(pool, lambda e, s=s, hn=hn: e.tensor_copy(out=hn[:, 0:256], in_=pin[:, s, :]), reads=[pin[:, s, :]], writes=[hn[:, 0:256]])
                bk = nbank()
                bkb = bk[:].bitcast(BF16)
                fns = [lambda e, kc=kc, bkb=bkb, hn=hn: e.transpose(out=bkb[:, kc * 128:(kc + 1) * 128], in_=hn[:, kc * 128:(kc + 1) * 128], identity=identb[:])
                       for kc in range(2)]
                P.group(pe, fns, reads=[hn[:, 0:256], identb[:]], writes=[bk[:]])
                P.op(dve, lambda e, s=s, bkb=bkb: e.tensor_copy(out=pT[:, :, s * 128:(s + 1) * 128],
                                                                 in_=bkb[:, 0:256].rearrange("p (k t) -> p k t", k=2)),
                     reads=[bk[:]], writes=[pT[:, :, s * 128:(s + 1) * 128]])

        def post_norm_res(src, gidx, base_sub, out_sub, stc):
            ss = stt[:, stc:stc + 1]
            vv = stt[:, stc + 1:stc + 2]
            rs = stt[:, stc + 2:stc + 3]
            tm = stt[:, stc + 3:stc + 5]
            jk = njunk()
            shp = list(src.shape)
            jkv = jk[:] if len(shp) == 2 else jk[:].rearrange("p (a b) -> p a b", a=2)
            P.op(act, lambda e: e.activation(out=jkv, in_=src, func=AF.Square, scale=1.0 / 32.0, accum_out=ss),
                 reads=[src], writes=[ss, jk[:]])
            P.op(dve, lambda e: e.tensor_scalar(out=vv, in0=ss, scalar1=EPS, scalar2=None, op0=ALU.add), reads=[ss], writes=[vv])
            rstd_newton(vv, rs, tm, scalar_v=True)
            gp = gpost[:, gidx, :] if len(shp) == 2 else gpost[:, gidx, :].rearrange("p (a b) -> p a b", a=2)
            tv = get2[:] if len(shp) == 2 else get2[:].rearrange("p (a b) -> p a b", a=2)
            P.op(dve, lambda e: e.scalar_tensor_tensor(out=tv, in0=src, scalar=rs, in1=gp, op0=ALU.mult, op1=ALU.mult),
                 reads=[src, rs, gpost[:, gidx, :]], writes=[get2[:]])
            P.op(pool, lambda e: e.tensor_tensor(out=out_sub, in0=base_sub, in1=get2[:], op=ALU.add),
                 reads=[base_sub, get2[:]], writes=[out_sub])

        hTBf = hTB[:].rearrange("p s k t -> p (s k t)")
        lru_a = hTBf[:, 0:2048].bitcast(F32).rearrange("p (a b) -> p a b", a=2)
        lru_a2 = hTBf[:, 2048:4096].bitcast(F32).rearrange("p (a b) -> p a b", a=2)
        lru_q = get[:].rearrange("p (a b) -> p a b", a=2)

        def fm_gelu(gi, dst, chunks=(0, 1, 2, 3), h=0):
            wv = slab_view(gi)
            for j in chunks:
                m = h * 4 + j
                bk = nbank()
                fm_group(bk, wv, j, hTA)
                P.op(act, lambda e, bk=bk, m=m, dst=dst: e.activation(out=dst[:, m, :], in_=bk[:], func=AF.Gelu_apprx_tanh),
                     reads=[bk[:]], writes=[dst[:, m, :]])

        def lru_batch(b0):
            for c in range(2):
                ch = b0 + c
                bkr, bki = nbank(), nbank()
                P.group(pe, [lambda e, bkr=bkr, ch=ch: e.matmul(bkr[:], lhsT=BDr[:, ch, :], rhs=xcm[:, ch, :], start=True, stop=True)],
                        reads=[BDr[:], xcm[:, ch, :]], writes=[bkr[:]])
                P.group(pe, [lambda e, bki=bki, ch=ch: e.matmul(bki[:], lhsT=BDi[:, ch, :], rhs=xcm[:, ch, :], start=True, stop=True)],
                        reads=[BDi[:], xcm[:, ch, :]], writes=[bki[:]])
                tr = ntp()
                P.op(act, lambda e, bkr=bkr, ch=ch, tr=tr: e.activation(out=tr[:], in_=bkr[:], func=AF.Tanh, scale=0.5,
                                                                        bias=drv[:, ch:ch + 1]),
                     reads=[bkr[:], drv[:]], writes=[tr[:]])
                P.op(act, lambda e, c=c, ch=ch, tr=tr: e.activation(out=lru_a[:, c, :], in_=tr[:], func=AF.Exp,
                                                                    scale=drv[:, 24 + ch:25 + ch], bias=drv[:, 24 + ch:25 + ch]),
                     reads=[tr[:], drv[:]], writes=[lru_a[:, c, :]])
                P.op(pool, lambda e, c=c: e.tensor_tensor(out=lru_a2[:, c, :], in0=lru_a[:, c, :], in1=lru_a[:, c, :], op=ALU.mult),
                     reads=[lru_a[:, c, :]], writes=[lru_a2[:, c, :]])
                ti = ntp()
                P.op(act, lambda e, bki=bki, ch=ch, ti=ti: e.activation(out=ti[:], in_=bki[:], func=AF.Tanh, scale=0.5,
                                                                        bias=drv[:, 8 + ch:9 + ch]),
                     reads=[bki[:], drv[:]], writes=[ti[:]])
                P.op(dve, lambda e, c=c, ch=ch, ti=ti: e.scalar_tensor_tensor(out=lru_q[:, c, :], in0=ti[:], scalar=1.0, in1=xcm[:, ch, :],
                                                                             op0=ALU.add, op1=ALU.mult),
                     reads=[ti[:], xcm[:, ch, :]], writes=[lru_q[:, c, :]])
            for c in range(2):
                P.op(act, lambda e, c=c: e.activation(out=lru_a2[:, c, :], in_=lru_a2[:, c, :], func=AF.Sqrt, scale=-0.25, bias=0.25),
                     reads=[lru_a2[:, c, :]], writes=[lru_a2[:, c, :]])
            for c in range(2):
                ch = b0 + c
                P.op(dve, lambda e, c=c: e.tensor_tensor(out=lru_q[:, c, :], in0=lru_q[:, c, :], in1=lru_a2[:, c, :], op=ALU.mult),
                     reads=[lru_q[:, c, :], lru_a2[:, c, :]], writes=[lru_q[:, c, :]])
                hb = ntp()
                P.op(dve, lambda e, c=c, ch=ch, hb=hb: e.tensor_tensor_scan(out=hb[:], data0=lru_a[:, c, :], data1=lru_q[:, c, :],
                                                                          initial=hstate[:, ch:ch + 1], op0=ALU.mult, op1=ALU.add),
                     reads=[lru_a[:, c, :], lru_q[:, c, :], hstate[:, ch:ch + 1]], writes=[hb[:]])
                P.op(dve, lambda e, ch=ch, hb=hb: e.tensor_copy(out=hstate[:, ch:ch + 1], in_=hb[:, 511:512]),
                     reads=[hb[:, 511:512]], writes=[hstate[:, ch:ch + 1]])
                P.op(pool, lambda e, ch=ch, hb=hb: e.tensor_tensor(out=gy[:, ch, :], in0=hb[:], in1=gy[:, ch, :], op=ALU.mult),
                     reads=[hb[:], gy[:, ch, :]], writes=[gy[:, ch, :]])

        def yr_half(t, h):
            CUR_LABEL[0] = "yr"
            gi = gp(t, 0 + h)
            loads_upto(gi)
            fm_gelu(gi, gy, h=h)
            loads_upto(gi + RING)

        def xr_half(t, h):
            CUR_LABEL[0] = "xr_conv"
            gi = gp(t, 4 + h)
            loads_upto(gi)
            wv = slab_view(gi)
            for j in range(4):
                ch = h * 4 + j
                bk = nbank()
                fm_group(bk, wv, j, hTA)
                P.op(act, lambda e, bk=bk, ch=ch: e.activation(out=xrb[:, ch, 3:515], in_=bk[:], func=AF.Copy),
                     reads=[bk[:]], writes=[xrb[:, ch, 3:515]])
            loads_upto(gi + RING)

        def conv_half(h):
            CUR_LABEL[0] = "xr_conv"
            for ch in range(4 * h, 4 * h + 4):
                bk = nbank()
                fns = [lambda e, k=k, bk=bk, ch=ch: e.matmul(bk[:], lhsT=Dcv[:, k, ch, :], rhs=xrb[:, ch, k:k + 512],
                                                             start=(k == 0), stop=(k == 3)) for k in range(4)]
                P.group(pe, fns, reads=[Dcv[:, :, ch, :], xrb[:, ch, 0:515]], writes=[bk[:]])
                P.op(act, lambda e, bk=bk, ch=ch: e.activation(out=xcm[:, ch, :], in_=bk[:], func=AF.Identity,
                                                                bias=csb[:, C_CVB + ch:C_CVB + ch + 1]),
                     reads=[bk[:], csb[:]], writes=[xcm[:, ch, :]])
                P.op(pool, lambda e, ch=ch: e.tensor_copy(out=xrb[:, ch, 0:3], in_=xrb[:, ch, 512:515]),
                     reads=[xrb[:, ch, 512:515]], writes=[xrb[:, ch, 0:3]])

        def phase_mix(t):
            hT = hTA
            nxt = t + 1 < NT
            def x_to_residual():
                P.dma(sp, d_c[2], xin[:], xs[:], reads=[xs[:]], writes=[xin[:]])
                if nxt:
                    load_x(t + 1)
            if nxt:
                pn1, pn2 = mk_prenorm(xs, hTA, 64)
            CUR_LABEL[0] = "v"
            giv = gp(t, 2)
            loads_upto(giv + 1)
            wv0, wv1 = slab_view(giv), slab_view(giv + 1)
            for s in range(4):
                for h, wv in ((0, wv0), (1, wv1)):
                    bk = nbank()
                    fns = [lambda e, kc=kc, bk=bk, wv=wv, s=s: e.matmul(bk[:], lhsT=hT[:, kc, s * 128:(s + 1) * 128], rhs=wv[:, kc, :],
                                                                       start=(kc == 0), stop=(kc == 7)) for kc in range(8)]
                    P.group(pe, fns, reads=[wv, hT[:]], writes=[bk[:]])
                    P.op(act, lambda e, bk=bk, s=s, h=h: e.activation(out=gv[:, s, h * 512:(h + 1) * 512], in_=bk[:], func=AF.Gelu_apprx_tanh),
                         reads=[bk[:]], writes=[gv[:, s, h * 512:(h + 1) * 512]])
                    P.op(dve, lambda e, s=s, h=h: e.bn_stats(out=mvs[:, s * 12 + h * 6:s * 12 + h * 6 + 6], in_=gv[:, s, h * 512:(h + 1) * 512]),
                         reads=[gv[:, s, h * 512:(h + 1) * 512]], writes=[mvs[:, s * 12 + h * 6:s * 12 + h * 6 + 6]])
                P.op(dve, lambda e, s=s: e.bn_aggr(out=mvs[:, 48 + 2 * s:50 + 2 * s], in_=mvs[:, s * 12:s * 12 + 12]),
                     reads=[mvs[:, s * 12:s * 12 + 12]], writes=[mvs[:, 48 + 2 * s:50 + 2 * s]])
            loads_upto(giv + 1 + RING)
            mv2 = mvs[:, 48:56].rearrange("p (s two) -> p s two", two=2)
            vv = stt[:, 48:52]
            rs = stt[:, 52:56]
            tm = stt[:, 56:60]
            P.op(dve, lambda e: e.tensor_scalar(out=vv, in0=mv2[:, :, 1], scalar1=EPS, scalar2=None, op0=ALU.add),
                 reads=[mvs[:, 48:56]], writes=[vv])
            rstd_newton(vv, rs, tm)
            for s in range(4):
                P.op(dve, lambda e, s=s: e.tensor_scalar(out=vln[:, s, :], in0=gv[:, s, :], scalar1=mvs[:, 48 + 2 * s:49 + 2 * s],
                                                          scalar2=rs[:, s:s + 1], op0=ALU.subtract, op1=ALU.mult),
                     reads=[gv[:, s, :], mvs[:, 48:56], rs], writes=[vln[:, s, :]])
            x_to_residual()
            giu = gp(t, 6)
            loads_upto(giu + 1)
            for k in range(4):
                CUR_LABEL[0] = "lru"
                lru_batch(2 * k)
                CUR_LABEL[0] = "u"
                fm_gelu(giu + k // 2, gu, chunks=(2 * (k % 2), 2 * (k % 2) + 1), h=k // 2)
            loads_upto(giu + 1 + RING)
            CUR_LABEL[0] = "spatial"
            for g in range(8):
                bk = nbank()
                fns = [lambda e, n=n, bk=bk, g=g: e.matmul(bk[:, n * 128:(n + 1) * 128], lhsT=vln[:, n, g * 128:(g + 1) * 128], rhs=WsT[:, g, :],
                                                           start=True, stop=True) for n in range(4)]
                P.group(pe, fns, reads=[vln[:], WsT[:]], writes=[bk[:]])
                tmp = ntp()
                bgb = Bg[:, g:g + 1, :].to_broadcast([128, 4, 128])
                P.op(dve, lambda e, bk=bk, g=g, tmp=tmp, bgb=bgb: e.scalar_tensor_tensor(
                    out=tmp[:].rearrange("p (n i) -> p n i", n=4), in0=bk[:].rearrange("p (n i) -> p n i", n=4),
                    scalar=csb[:, C_LNG + g:C_LNG + g + 1], in1=bgb, op0=ALU.mult, op1=ALU.add),
                    reads=[bk[:], csb[:], Bg[:, g, :]], writes=[tmp[:]])
                P.op(pool, lambda e, g=g, tmp=tmp: e.tensor_tensor(out=gu[:, g, :], in0=tmp[:], in1=gu[:, g, :], op=ALU.mult),
                     reads=[tmp[:], gu[:, g, :]], writes=[gu[:, g, :]])
            CUR_LABEL[0] = "merge"
            for h in range(2):
                gi = gp(t, 8 + 4 * h)
                loads_upto(gi + 3)
                wga, wgb, wa, wb = (slab_view(gi + i) for i in range(4))
                for j in range(4):
                    m = h * 4 + j
                    bga, bgb_, bpa, bpb = nbank(), nbank(), nbank(), nbank()
                    fm_group(bga, wga, j, hT)
                    fm_group(bgb_, wgb, j, hT)
                    fm_group(bpa, wa, j, gu)
                    fm_group(bpb, wb, j, gy)
                    ta, tb = ntp(), ntp()
                    P.op(act, lambda e, bga=bga, ta=ta: e.activation(out=ta[:], in_=bga[:], func=AF.Tanh, scale=0.5), reads=[bga[:]], writes=[ta[:]])
                    P.op(act, lambda e, bgb_=bgb_, tb=tb: e.activation(out=tb[:], in_=bgb_[:], func=AF.Tanh, scale=0.5), reads=[bgb_[:]], writes=[tb[:]])
                    P.op(dve, lambda e, ta=ta, bpa=bpa: e.scalar_tensor_tensor(out=ta[:], in0=ta[:], scalar=1.0, in1=bpa[:], op0=ALU.add, op1=ALU.mult),
                         reads=[ta[:], bpa[:]], writes=[ta[:]])
                    P.op(dve, lambda e, tb=tb, bpb=bpb: e.scalar_tensor_tensor(out=tb[:], in0=tb[:], scalar=1.0, in1=bpb[:], op0=ALU.add, op1=ALU.mult),
                         reads=[tb[:], bpb[:]], writes=[tb[:]])
                    P.op(pool, lambda e, ta=ta, tb=tb, m=m: e.tensor_tensor(out=xcm[:, m, :], in0=ta[:], in1=tb[:], op=ALU.add),
                         reads=[ta[:], tb[:]], writes=[xcm[:, m, :]])
                loads_upto(gi + 3 + RING)
                if h == 0 and nxt:
                    pn1()
                    CUR_LABEL[0] = "merge"
            if nxt:
                CUR_LABEL[0] = "next_prenorm"
                pn2()
            CUR_LABEL[0] = "wout"
            gi = gp(t, 16)
            loads_upto(gi + 1)
            w0, w1 = slab_view(gi), slab_view(gi + 1)
            pres = [mk_pre(s_, 80 + 5 * s_) for s_ in range(4)]

            def wout_sub(s):
                CUR_LABEL[0] = "wout"
                pr = nbank_pair()
                for hh, wv in enumerate((w0, w1)):
                    fns = [lambda e, kc=kc, pr=pr, hh=hh, wv=wv, s=s: e.matmul(pr[:, hh, :], lhsT=xcm[:, kc, s * 128:(s + 1) * 128], rhs=wv[:, kc, :],
                                                                              start=(kc == 0), stop=(kc == 7)) for kc in range(8)]
                    P.group(pe, fns, reads=[wv, xcm[:]], writes=[pr[:, hh, :]])
                post_norm_res(pr, 0, xin[:, s, :], xin[:, s, :], 16 + 5 * s)

            def pst(s, k):
                CUR_LABEL[0] = "ffn_prenorm"
                pres[s][k]()

            wout_sub(0)
            wout_sub(1); pst(0, 0)
            wout_sub(2); pst(0, 1); pst(0, 2); pst(1, 0)
            wout_sub(3)
            loads_upto(gi + 1 + RING)
            if nxt:
                yr_half(t + 1, 0)
            pst(0, 3); pst(0, 4); pst(1, 1); pst(1, 2); pst(2, 0)
            pst(1, 3); pst(1, 4); pst(2, 1); pst(2, 2); pst(3, 0); pst(3, 1); pst(3, 2)
            if nxt:
                yr_half(t + 1, 1)
            pst(2, 3); pst(2, 4); pst(3, 3); pst(3, 4)
            if nxt:
                load_p(t + 1)

        def phase_ffn(t):
            nxt = t + 1 < NT
            CUR_LABEL[0] = "ffn_gateup"
            for jb in range(6):
                gi = gp(t, 18 + 2 * jb)
                loads_upto(gi + 1)
                wg, wu = slab_view(gi), slab_view(gi + 1)
                for jj in range(4 if jb < 5 else 2):
                    f = jb * 4 + jj
                    bg, bu = nbank(), nbank()
                    fm_group(bg, wg, jj, hTB, submajor=True)
                    fm_group(bu, wu, jj, hTB, submajor=True)
                    sg = ntp()
                    P.op(act, lambda e, bg=bg, sg=sg: e.activation(out=sg[:], in_=bg[:], func=AF.Silu), reads=[bg[:]], writes=[sg[:]])
                    P.op(dve, lambda e, bu=bu, sg=sg, f=f: e.tensor_tensor(out=act_v[:, f, :], in0=sg[:], in1=bu[:], op=ALU.mult),
                         reads=[sg[:], bu[:]], writes=[act_v[:, f, :]])
                loads_upto(gi + 1 + RING)
                if jb == 3 and nxt:
                    CUR_LABEL[0] = "p_transpose"
                    p_transpose(pTs[(t + 1) % 2])
                    CUR_LABEL[0] = "ffn_gateup"
            CUR_LABEL[0] = "down"
            prd = [nbank_pair() for _ in range(4)]
            for h in range(2):
                for pi_, (k0, nk) in enumerate(((0, 8), (8, 8), (16, 6))):
                    gi = gp(t, 30 + 3 * h + pi_)
                    loads_upto(gi)
                    wv = slab_view(gi)
                    for s in range(4):
                        bk = _Bk(prd[s][:, h, :])
                        fns = [lambda e, kc=kc, bk=bk, wv=wv, s=s, k0=k0: e.matmul(
                            bk[:], lhsT=act_v[:, k0 + kc, s * 128:(s + 1) * 128], rhs=wv[:, kc, :],
                            start=(k0 + kc == 0), stop=(k0 + kc == NF - 1)) for kc in range(nk)]
                        P.group(pe, fns, reads=[wv, act_v[:, k0:k0 + nk, :]], writes=[bk[:]], accum=(k0 > 0))
                    loads_upto(gi + RING)
            pT = pTs[t % 2]
            ple_w = {}

            def ple_sub(s):
                CUR_LABEL[0] = "ple"
                if not ple_w:
                    gi = gp(t, 36)
                    loads_upto(gi + 3)
                    ple_w["g"] = (slab_view(gi), slab_view(gi + 1))
                    ple_w["i"] = (slab_view(gi + 2), slab_view(gi + 3))
                for h in range(2):
                    wg, wi = ple_w["g"][h], ple_w["i"][h]
                    bg, be = nbank(), nbank()
                    fns = [lambda e, kc=kc, bg=bg, wg=wg: e.matmul(bg[:], lhsT=hTB[:, s, kc, :], rhs=wg[:, kc, :],
                                                                  start=(kc == 0), stop=(kc == 7)) for kc in range(8)]
                    P.group(pe, fns, reads=[wg, hTB[:, s, :, :]], writes=[bg[:]])
                    fns = [lambda e, kc=kc, be=be, wi=wi: e.matmul(be[:], lhsT=pT[:, kc, s * 128:(s + 1) * 128], rhs=wi[:, kc, :],
                                                                  start=(kc == 0), stop=(kc == 1)) for kc in range(2)]
                    P.group(pe, fns, reads=[wi, pT[:]], writes=[be[:]])
                    tg = ntp()
                    P.op(act, lambda e, bg=bg, tg=tg: e.activation(out=tg[:], in_=bg[:], func=AF.Tanh, scale=0.5), reads=[bg[:]], writes=[tg[:]])
                    P.op(dve, lambda e, tg=tg, be=be, h=h: e.scalar_tensor_tensor(out=get[:, h * 512:(h + 1) * 512], in0=tg[:], scalar=1.0, in1=be[:],
                                                                                 op0=ALU.add, op1=ALU.mult),
                         reads=[tg[:], be[:]], writes=[get[:, h * 512:(h + 1) * 512]])
                post_norm_res(get[:], 2, xin[:, s, :], xin[:, s, :], 36 + 5 * s)
                dst = y_d[t * T + s * 128:t * T + (s + 1) * 128, :]
                P.dma(pool, d_out[s], dst, xin[:, s, :], reads=[xin[:, s, :]])

            def pn(s):
                CUR_LABEL[0] = "down"
                post_norm_res(prd[s], 1, xin[:, s, :], xin[:, s, :], 16 + 5 * s)

            pres = [mk_pre(s_, 80 + 5 * s_) for s_ in range(4)]

            def pst(s, k):
                CUR_LABEL[0] = "ple_prenorm"
                pres[s][k]()

            pn(0); pn(1); pst(0, 0); pn(2); pst(0, 1); pst(0, 2); pst(1, 0); pn(3)
            if nxt:
                xr_half(t + 1, 0)
            pst(0, 3); pst(0, 4); pst(1, 1); pst(1, 2); pst(2, 0)
            if nxt:
                xr_half(t + 1, 1)
            pst(1, 3); pst(1, 4); pst(2, 1); pst(2, 2); pst(3, 0)
            if nxt:
                conv_half(0)
            pst(2, 3); pst(2, 4); pst(3, 1); pst(3, 2)
            if nxt:
                conv_half(1)
            pst(3, 3); pst(3, 4)
            ple_sub(0); ple_sub(1); ple_sub(2); ple_sub(3)
            loads_upto(gp(t, 39) + RING)

        loads_upto(RING - 1)
        load_x(0)
        load_p(0)
        p0a, p0b = mk_prenorm(xs, hTA, 0)
        p0a()
        p0b()
        p_transpose(pTs[0])
        yr_half(0, 0)
        yr_half(0, 1)
        xr_half(0, 0)
        xr_half(0, 1)
        conv_half(0)
        conv_half(1)
        for t in range(NT):
            phase_mix(t)
            phase_ffn(t)
        for dd in d_out:
            sp.wait(dd, dd.n)
            pool.wait(dd, dd.n)

        with nc.Block() as block:
            @block.sync
            def _(e):
                for f in sp.ops:
                    f(e)

            @block.tensor
            def _(e):
                for f in pe.ops:
                    f(e)

            @block.scalar
            def _(e):
                for f in act.ops:
                    f(e)

            @block.vector
            def _(e):
                for f in dve.ops:
                    f(e)

            @block.gpsimd
            def _(e):
                for f in pool.ops:
                    f(e)
    return nc


def _host_consts(inp):
    def colT(v):
        return np.ascontiguousarray(np.asarray(v, np.float32).reshape(8, 128).T)

    cst = np.zeros((128, NCST), np.float32)
    cst[:, C_GMIX:C_GMIX + 8] = colT(inp["norm_mix_pre"][0])
    cst[:, C_GFFN:C_GFFN + 8] = colT(inp["norm_ffn_pre"][0])
    cst[:, C_GPLE:C_GPLE + 8] = colT(inp["norm_ple_pre"][0])
    cst[:, C_HALF] = 0.5
    cst[:, C_ONE] = 1.0
    cst[:, C_LNG:C_LNG + 8] = colT(inp["gmlp_ln_g"][0])
    cst[:, C_LNB:C_LNB + 8] = colT(inp["gmlp_ln_b"][0])
    for k in range(4):
        cst[:, C_CVW + k * 8:C_CVW + k * 8 + 8] = colT(inp["conv_w"][0, k])
    cst[:, C_CVB:C_CVB + 8] = colT(inp["conv_b"][0])
    cst[:, C_BR:C_BR + 8] = colT(inp["lru_b_r"][0])
    cst[:, C_BI:C_BI + 8] = colT(inp["lru_b_i"][0])
    cst[:, C_LAM:C_LAM + 8] = colT(inp["lru_lambda"][0])
    wst = np.ascontiguousarray(np.asarray(inp["gmlp_w_s"][0], np.float32).transpose(2, 0, 1)).reshape(128, 1024)

    def bd(w):
        w = np.asarray(w, np.float32)
        out = np.zeros((128, 8, 128), np.float32)
        for hd in range(16):
            ch, hl = hd // 2, hd % 2
            out[hl * 64:(hl + 1) * 64, ch, hl * 64:(hl + 1) * 64] = w[hd]
        return out.reshape(128, 1024)

    bcv = np.concatenate([np.asarray(inp["gmlp_b_s"][0], np.float32).reshape(-1),
                          np.asarray(inp["norm_mix_post"][0], np.float32),
                          np.asarray(inp["norm_ffn_post"][0], np.float32),
                          np.asarray(inp["norm_ple_post"][0], np.float32)]).reshape(1, 4096)
    mats = {"w_in": np.asarray(inp["w_in"][0], np.float32), "w_a": np.asarray(inp["w_branch_a"][0], np.float32),
            "w_b": np.asarray(inp["w_branch_b"][0], np.float32), "w_out": np.asarray(inp["w_out"][0], np.float32),
            "f_gate": np.asarray(inp["ffn_w_gate"][0], np.float32), "f_up": np.asarray(inp["ffn_w_up"][0], np.float32),
            "f_down": np.asarray(inp["ffn_w_down"][0], np.float32), "p_gate": np.asarray(inp["ple_w_gate"][0], np.float32),
            "p_in": np.asarray(inp["ple_w_in"][0], np.float32)}
    return dict(wf=_pack_weights(mats), cst=cst, idn=np.eye(128, dtype=np.float32), wst=wst,
                bdr=bd(inp["lru_w_r"][0]), bdi=bd(inp["lru_w_i"][0]), bcv=bcv)


def kernel(**inputs):
    x = np.asarray(inputs["x"], np.float32)
    p = np.asarray(inputs["p"], np.float32)
    B = x.shape[0]
    NT = x.shape[1] // T
    shared = _host_consts(inputs)
    nc = build_nc(NT)
    in_maps = []
    for b in range(B):
        m = dict(shared)
        m["x"] = np.ascontiguousarray(x[b])
        m["p"] = np.ascontiguousarray(p[0, b])
        in_maps.append(m)
    res = run_bass_kernel_spmd(nc, in_maps, core_ids=list(range(B)))
    return np.stack([np.asarray(r["y"], np.float32) for r in res.results], axis=0)
```

```python
import numpy as np
from contextlib import ExitStack
import concourse.bass as bass
import concourse.mybir as mybir
from concourse.bass_utils import run_bass_kernel_spmd

F32 = mybir.dt.float32
BF16 = mybir.dt.bfloat16
I32 = mybir.dt.int32
AF = mybir.ActivationFunctionType
ALU = mybir.AluOpType

_ISZ = {F32: 4, BF16: 2, I32: 4}

S_TOK = 8192
D = 1024
NT_DEFAULT = 16
T = 512
DFF = 2816
NF = 22
MAGIC = 1597463007.0
EPS = 1e-6
RING = 5
SLOT = 4096


PE_LABELS = []
CUR_LABEL = ["setup"]


class Sem:
    def __init__(self, h):
        self.h = h
        self.n = 0


class Eng:
    def __init__(self, name, sem):
        self.name = name
        self.sem = sem
        self.ops = []
        self.seen = {}

    def wait(self, sem, val):
        if sem is self.sem and self.name == "pe":
            return
        if self.seen.get(sem, 0) >= val:
            return
        self.seen[sem] = val
        h = sem.h
        self.ops.append(lambda e, h=h, val=val: e.wait_ge(h, val))


def _rng(ap):
    a = ap.ap
    isz = _ISZ[ap.dtype]
    row = a[0][0]
    off = ap.offset
    lo = off % row if row > 0 else off
    span = 1
    for (st, cnt) in a[1:]:
        span += abs(st) * (cnt - 1)
    return ap.tensor.name, lo * isz, (lo + span) * isz


class Prog:
    def __init__(self):
        self.psum = set()
        self.recs = {}

    def deps_and_record(self, E, reads, writes, ticket_fn):
        acc = []
        for lst_, w in ((reads, False), (writes, True)):
            for ap in lst_:
                k, lo, hi = _rng(ap)
                if k in self.psum:
                    for bnk in range(lo // 2048, (hi - 1) // 2048 + 1):
                        acc.append(((k, bnk), 0, 1 << 30, w, True))
                else:
                    acc.append((k, lo, hi, w, False))
        waits = {}
        for (k, lo, hi, w, ps) in acc:
            for r in self.recs.get(k, ()):
                if r[1] <= lo or r[0] >= hi:
                    continue
                same = r[3] is E
                if same and E.name == "pe":
                    continue
                if (not w) and (not r[2]):
                    if not ps or same:
                        continue
                s_, v = r[4], r[5]
                if waits.get(s_, 0) < v:
                    waits[s_] = v
        for s_, v in waits.items():
            E.wait(s_, v)
        sem, val = ticket_fn()
        for (k, lo, hi, w, ps) in acc:
            lst = self.recs.setdefault(k, [])
            if w:
                lst[:] = [r for r in lst if not (r[0] >= lo and r[1] <= hi)]
            else:
                lst[:] = [r for r in lst if not ((not r[2]) and r[4] is sem and r[0] >= lo and r[1] <= hi)]
            lst.append([lo, hi, w, E, sem, val])

    def op(self, E, fn, reads=(), writes=()):
        def ticket():
            E.sem.n += 1
            return E.sem, E.sem.n
        self.deps_and_record(E, reads, writes, ticket)
        h = E.sem.h
        E.ops.append(lambda e, fn=fn, h=h: fn(e).then_inc(h, 1))

    def group(self, E, fns, reads=(), writes=(), accum=False):
        PE_LABELS.append((CUR_LABEL[0], len(fns)))
        if not accum:
            for ap in writes:
                k, lo, hi = _rng(ap)
                if k not in self.psum:
                    continue
                for bnk in range(lo // 2048, (hi - 1) // 2048 + 1):
                    lst = self.recs.get((k, bnk), ())
                    if lst and lst[-1][2] and lst[-1][3] is E:
                        raise AssertionError("PSUM bank %s overwritten before being read (%s)" % (bnk, CUR_LABEL[0]))

        def ticket():
            E.sem.n += 1
            return E.sem, E.sem.n
        self.deps_and_record(E, reads, writes, ticket)
        h = E.sem.h
        for fn in fns[:-1]:
            E.ops.append(fn)
        last = fns[-1]
        E.ops.append(lambda e, fn=last, h=h: fn(e).then_inc(h, 1))

    def dma(self, Q, dsem, out, in_, reads=(), writes=()):
        def ticket():
            dsem.n += 16
            return dsem, dsem.n
        self.deps_and_record(Q, reads, writes, ticket)
        h = dsem.h
        Q.ops.append(lambda e, out=out, in_=in_, h=h: e.dma_start(out=out, in_=in_).then_inc(h, 16))
        return dsem, dsem.n


C_GMIX, C_GFFN, C_GPLE, C_HALF, C_ONE = 0, 8, 16, 24, 25
C_LNG, C_LNB, C_CVW, C_CVB, C_BR, C_BI, C_LAM = 26, 34, 42, 74, 82, 90, 98
NCST = 106


def _slab_plan():
    sl = []

    def add(name, kc0, nkc, c0, nc_, sc, per):
        sl.append(dict(name=name, kc0=kc0, nkc=nkc, c0=c0, ncols=nc_, sc=sc, per=per))

    for c0 in (3072, 3584):
        add("w_in", 0, 8, c0, 512, C_GMIX, True)
    for c0 in (1024, 1536):
        add("w_in", 0, 8, c0, 512, C_GMIX, True)
    for c0 in (2048, 2560):
        add("w_in", 0, 8, c0, 512, C_GMIX, True)
    for c0 in (0, 512):
        add("w_in", 0, 8, c0, 512, C_GMIX, True)
    for h in range(2):
        add("w_in", 0, 8, 4096 + h * 512, 512, C_GMIX, True)
        add("w_in", 0, 8, 5120 + h * 512, 512, C_GMIX, True)
        add("w_a", 0, 8, h * 512, 512, C_ONE, False)
        add("w_b", 0, 8, h * 512, 512, C_ONE, False)
    for h in range(2):
        add("w_out", 0, 8, h * 512, 512, C_HALF, False)
    for j in range(6):
        ncols = 512 if j < 5 else 256
        add("f_gate", 0, 8, j * 512, ncols, C_GFFN, True)
        add("f_up", 0, 8, j * 512, ncols, C_GFFN, True)
    for h in range(2):
        for (k0, nk) in ((0, 8), (8, 8), (16, 6)):
            add("f_down", k0, nk, h * 512, 512, C_ONE, False)
    for h in range(2):
        add("p_gate", 0, 8, h * 512, 512, C_GPLE, True)
    for h in range(2):
        add("p_in", 0, 2, h * 512, 512, C_HALF, False)
    off = 0
    for s in sl:
        s["off"] = off
        s["len"] = s["nkc"] * s["ncols"]
        off += s["len"]
    return sl, off


SLABS, WTOT = _slab_plan()
NSL = len(SLABS)
(SL_YR, SL_V, SL_XR, SL_U, SL_MERGE, SL_WOUT, SL_FFN, SL_DOWN, SL_PGATE, SL_PIN) = (0, 2, 4, 6, 8, 16, 18, 30, 36, 38)


def _pack_weights(mats):
    wf = np.empty((128, WTOT), np.float32)
    for s in SLABS:
        W = mats[s["name"]]
        blk = W[s["kc0"] * 128:(s["kc0"] + s["nkc"]) * 128, s["c0"]:s["c0"] + s["ncols"]]
        blk = blk.reshape(s["nkc"], 128, s["ncols"]).transpose(1, 0, 2).reshape(128, -1)
        wf[:, s["off"]:s["off"] + s["len"]] = blk
    return wf


def build_nc(NT):
    nc = bass.Bass("TRN2", target_bir_lowering=False)
    ntok = NT * T
    x_d = nc.dram_tensor("x", [ntok, D], F32, kind="ExternalInput").ap()
    p_d = nc.dram_tensor("p", [ntok, 256], F32, kind="ExternalInput").ap()
    wf_d = nc.dram_tensor("wf", [128, WTOT], F32, kind="ExternalInput").ap()
    cst_d = nc.dram_tensor("cst", [128, NCST], F32, kind="ExternalInput").ap()
    idn_d = nc.dram_tensor("idn", [128, 128], F32, kind="ExternalInput").ap()
    wst_d = nc.dram_tensor("wst", [128, 1024], F32, kind="ExternalInput").ap()
    bdr_d = nc.dram_tensor("bdr", [128, 1024], F32, kind="ExternalInput").ap()
    bdi_d = nc.dram_tensor("bdi", [128, 1024], F32, kind="ExternalInput").ap()
    bcv_d = nc.dram_tensor("bcv", [1, 4096], F32, kind="ExternalInput").ap()
    y_d = nc.dram_tensor("y", [ntok, D], F32, kind="ExternalOutput").ap()
    wbf_d = nc.dram_tensor("wbf", [128, WTOT], BF16, kind="Internal").ap()

    with ExitStack() as es:
        def sb(name, shape, dt):
            return es.enter_context(nc.sbuf_tensor(name, shape, dt))

        def sem(name):
            return Sem(es.enter_context(nc.semaphore(name)))

        P = Prog()
        pe = Eng("pe", sem("s_pe"))
        act = Eng("act", sem("s_act"))
        dve = Eng("dve", sem("s_dve"))
        pool = Eng("pool", sem("s_pool"))
        sp = Eng("sp", None)

        psum_all = es.enter_context(nc.psum_tensor("psum", [128, 8, 512], F32))
        P.psum.add("psum")
        banks = [psum_all[:, i, :] for i in range(8)]
        bank_i = [0]

        class _Bk:
            def __init__(self, ap):
                self.ap = ap

            def __getitem__(self, key):
                return self.ap[key]

        def nbank():
            b = _Bk(banks[bank_i[0] % 8])
            bank_i[0] += 1
            return b

        def nbank_pair():
            if bank_i[0] % 2:
                bank_i[0] += 1
            i = bank_i[0] % 8
            bank_i[0] += 2
            return psum_all[:, i:i + 2, :]

        xs = sb("xs", [128, 4, 1024], F32)
        xin = sb("xin", [128, 4, 1024], F32)
        pin = sb("pin", [128, 4, 256], F32)
        pTs = [sb("pT%d" % i, [128, 2, 512], BF16) for i in range(2)]
        hns = [sb("hn%d" % i, [128, 1024], BF16) for i in range(2)]
        hn_i = [0]

        def nhn():
            h_ = hns[hn_i[0] % 2]
            hn_i[0] += 1
            return h_
        hTA = sb("hTA", [128, 8, 512], BF16)
        hTB = sb("hTB", [128, 4, 8, 128], BF16)
        gu = sb("gu", [128, 8, 512], BF16)
        gy = sb("gy", [128, 8, 512], BF16)
        xrb = sb("xrb", [128, 8, 516], BF16)
        xcm = sb("xcm", [128, 8, 512], BF16)
        arena = sb("arena", [128, 12288], BF16)
        TP = [sb("tp%d" % i, [128, 512], F32) for i in range(4)]
        get = sb("get", [128, 1024], F32)
        get2 = sb("get2", [128, 1024], F32)
        junks = [sb("junk%d" % i, [128, 1024], BF16) for i in range(2)]
        junk_i = [0]

        def njunk():
            j_ = junks[junk_i[0] % 2]
            junk_i[0] += 1
            return j_
        csb = sb("csb", [128, NCST], F32)
        drv = sb("drv", [128, 40], F32)
        identb = sb("identb", [128, 128], BF16)
        WsT = sb("WsT", [128, 8, 128], BF16)
        BDr = sb("BDr", [128, 8, 128], BF16)
        BDi = sb("BDi", [128, 8, 128], BF16)
        Dcv = sb("Dcv", [128, 4, 8, 128], BF16)
        Bg = sb("Bg", [128, 8, 128], F32)
        gpost = sb("gpost", [128, 3, 1024], F32)
        stt = sb("stt", [128, 128], F32)
        mvs = sb("mvs", [128, 64], F32)
        hstate = sb("hstate", [128, 8], F32)
        wring = [sb("wr%d" % i, [128, SLOT], BF16) for i in range(RING)]

        d_ring = [sem("d_wr%d" % i) for i in range(RING)]
        d_xs = sem("d_xs")
        d_pin = sem("d_pin")
        d_out = [sem("d_out%d" % i) for i in range(4)]
        d_c = [sem("d_c%d" % i) for i in range(7)]
        d_st = [sem("d_st%d" % i) for i in range(2)]

        tp_i = [0]

        def ntp():
            t_ = TP[tp_i[0] % len(TP)]
            tp_i[0] += 1
            return t_

        act_v = arena[:, 0:NF * 512].rearrange("p (a b) -> p a b", a=NF)
        vln = arena[:, 0:4096].rearrange("p (a b) -> p a b", a=4)
        gv = arena[:, 4096:12288].bitcast(F32).rearrange("p (a b) -> p a b", a=4)
        a_t = arena[:, 0:4096].bitcast(F32).rearrange("p (a b) -> p a b", a=4)
        a2_t = arena[:, 4096:8192].bitcast(F32).rearrange("p (a b) -> p a b", a=4)
        q_t = arena[:, 8192:12288].bitcast(F32).rearrange("p (a b) -> p a b", a=4)

        def rstd_newton(v, y, tmp, scalar_v=False):
            vi = v.bitcast(I32)
            yi = y.bitcast(I32)
            P.op(dve, lambda e: e.tensor_scalar(out=yi, in0=vi, scalar1=-0.5, scalar2=MAGIC, op0=ALU.mult, op1=ALU.add),
                 reads=[v], writes=[y])
            if scalar_v:
                hv = tmp[:, 0:1]
                t_ = tmp[:, 1:2]
                P.op(dve, lambda e: e.tensor_scalar(out=hv, in0=v, scalar1=-0.5, scalar2=None, op0=ALU.mult), reads=[v], writes=[hv])
                for _ in range(2):
                    P.op(dve, lambda e: e.scalar_tensor_tensor(out=t_, in0=y, scalar=hv, in1=y, op0=ALU.mult, op1=ALU.mult),
                         reads=[y, hv], writes=[t_])
                    P.op(dve, lambda e: e.scalar_tensor_tensor(out=y, in0=t_, scalar=1.5, in1=y, op0=ALU.add, op1=ALU.mult),
                         reads=[t_, y], writes=[y])
                return
            for _ in range(2):
                P.op(dve, lambda e: e.tensor_tensor(out=tmp, in0=y, in1=y, op=ALU.mult), reads=[y], writes=[tmp])
                P.op(dve, lambda e: e.tensor_tensor(out=tmp, in0=tmp, in1=v, op=ALU.mult), reads=[tmp, v], writes=[tmp])
                P.op(dve, lambda e: e.tensor_scalar(out=tmp, in0=tmp, scalar1=-0.5, scalar2=1.5, op0=ALU.mult, op1=ALU.add),
                     reads=[tmp], writes=[tmp])
                P.op(dve, lambda e: e.tensor_tensor(out=y, in0=y, in1=tmp, op=ALU.mult), reads=[y, tmp], writes=[y])

        P.dma(sp, d_c[0], csb[:], cst_d[:, :], writes=[csb[:]])
        identf = TP[0][:, 0:128]
        P.dma(sp, d_c[1], identf, idn_d[:, :], writes=[identf])
        wsf = arena[:, 0:2048].bitcast(F32)
        bdrf = arena[:, 2048:4096].bitcast(F32)
        bdif = arena[:, 4096:6144].bitcast(F32)
        bsbc = arena[:, 6144:8192].bitcast(F32)
        P.dma(sp, d_c[2], wsf, wst_d[:, :], writes=[wsf])
        P.dma(sp, d_c[3], bdrf, bdr_d[:, :], writes=[bdrf])
        P.dma(sp, d_c[4], bdif, bdi_d[:, :], writes=[bdif])
        P.dma(sp, d_c[5], bsbc, bcv_d[0:1, 0:1024].partition_broadcast(128), writes=[bsbc])
        P.dma(sp, d_c[6], gpost[:].rearrange("p a b -> p (a b)"), bcv_d[0:1, 1024:4096].partition_broadcast(128),
              writes=[gpost[:]])

        P.op(dve, lambda e: e.tensor_copy(out=identb[:], in_=identf), reads=[identf], writes=[identb[:]])
        wsf3 = wsf.rearrange("p (g i) -> p g i", g=8)
        P.op(dve, lambda e: e.memset(wsf3[64:128, :, 0:64], 0.0), reads=[], writes=[wsf])
        P.op(dve, lambda e: e.tensor_copy(out=WsT[:].rearrange("p g i -> p (g i)"), in_=wsf), reads=[wsf], writes=[WsT[:]])
        P.op(dve, lambda e: e.tensor_copy(out=BDr[:].rearrange("p g i -> p (g i)"), in_=bdrf), reads=[bdrf], writes=[BDr[:]])
        P.op(dve, lambda e: e.tensor_copy(out=BDi[:].rearrange("p g i -> p (g i)"), in_=bdif), reads=[bdif], writes=[BDi[:]])
        ones = TP[1][:, 0:128]
        P.op(dve, lambda e: e.memset(ones, 1.0), writes=[ones])
        for h in range(2):
            bk = nbank()
            P.group(pe, [lambda e, bk=bk, h=h: e.matmul(bk[:], lhsT=ones, rhs=wsf[:, h * 512:(h + 1) * 512], start=True, stop=True)],
                    reads=[ones, wsf], writes=[bk[:]])
            for gg in range(4):
                g = h * 4 + gg
                P.op(dve, lambda e, bk=bk, g=g, gg=gg: e.scalar_tensor_tensor(
                    out=Bg[:, g, :], in0=bk[:, gg * 128:(gg + 1) * 128], scalar=csb[:, C_LNB + g:C_LNB + g + 1],
                    in1=bsbc[:, g * 128:(g + 1) * 128], op0=ALU.mult, op1=ALU.add),
                    reads=[bk[:], csb[:], bsbc], writes=[Bg[:, g, :]])
        for k in range(4):
            for ch in range(8):
                col = C_CVW + k * 8 + ch
                P.op(dve, lambda e, k=k, ch=ch, col=col: e.tensor_scalar(
                    out=Dcv[:, k, ch, :], in0=identf, scalar1=csb[:, col:col + 1], scalar2=None, op0=ALU.mult),
                    reads=[identf, csb[:]], writes=[Dcv[:, k, ch, :]])
        P.op(dve, lambda e: e.tensor_scalar(out=drv[:, 0:16], in0=csb[:, C_BR:C_BR + 16], scalar1=0.5, scalar2=None, op0=ALU.mult),
             reads=[csb[:]], writes=[drv[:, 0:16]])
        P.op(act, lambda e: e.activation(out=drv[:, 32:40], in_=csb[:, C_LAM:C_LAM + 8], func=AF.Exp, scale=-1.0),
             reads=[csb[:]], writes=[drv[:, 32:40]])
        P.op(act, lambda e: e.activation(out=drv[:, 32:40], in_=drv[:, 32:40], func=AF.Ln, bias=1.0),
             reads=[drv[:, 32:40]], writes=[drv[:, 32:40]])
        P.op(dve, lambda e: e.tensor_scalar(out=drv[:, 16:24], in0=drv[:, 32:40], scalar1=-8.0, scalar2=None, op0=ALU.mult),
             reads=[drv[:, 32:40]], writes=[drv[:, 16:24]])
        P.op(dve, lambda e: e.tensor_scalar(out=drv[:, 24:32], in0=drv[:, 32:40], scalar1=-4.0, scalar2=None, op0=ALU.mult),
             reads=[drv[:, 32:40]], writes=[drv[:, 24:32]])
        P.op(dve, lambda e: e.memset(hstate[:], 0.0), writes=[hstate[:]])
        P.op(dve, lambda e: e.memset(xrb[:], 0.0), writes=[xrb[:]])

        stage = [xs[:].rearrange("p a b -> p (a b)"), xin[:].rearrange("p a b -> p (a b)")]
        cvt_engs = [act, dve]
        ci = 0
        last_store = {}
        def prep_load(si):
            s = SLABS[si]
            L = s["len"]
            sti = stage[si % 2]
            P.dma(sp, d_xs if si % 2 == 0 else d_pin, sti[:, 0:L], wf_d[:, s["off"]:s["off"] + L], writes=[sti[:, 0:L]])

        prep_load(0)
        for si, s in enumerate(SLABS):
            L = s["len"]
            sti = stage[si % 2]
            sto = wring[si % 2]
            if si + 1 < len(SLABS):
                prep_load(si + 1)
            for kc in range(s["nkc"]):
                col = (s["sc"] + kc) if s["per"] else s["sc"]
                lo, hi = kc * s["ncols"], (kc + 1) * s["ncols"]
                E = cvt_engs[ci % 2]
                ci += 1
                if E is act:
                    P.op(E, lambda e, sti=sti, sto=sto, lo=lo, hi=hi, col=col: e.activation(
                        out=sto[:, lo:hi], in_=sti[:, lo:hi], func=AF.Copy, scale=csb[:, col:col + 1]),
                        reads=[sti[:, lo:hi], csb[:]], writes=[sto[:, lo:hi]])
                else:
                    P.op(E, lambda e, sti=sti, sto=sto, lo=lo, hi=hi, col=col: e.tensor_scalar(
                        out=sto[:, lo:hi], in0=sti[:, lo:hi], scalar1=csb[:, col:col + 1], scalar2=None, op0=ALU.mult),
                        reads=[sti[:, lo:hi], csb[:]], writes=[sto[:, lo:hi]])
            last_store[si % 2] = P.dma(sp, d_st[si % 2], wbf_d[:, s["off"]:s["off"] + L], sto[:, 0:L], reads=[sto[:, 0:L]])
        for k_, tk in last_store.items():
            sp.wait(tk[0], tk[1])

        SEQ = [0, 1, 4, 5]
        POS = {(0, 0): 0, (0, 1): 1, (0, 4): 2, (0, 5): 3}
        for t_ in range(NT):
            order = [2, 3, 6, 7] + list(range(8, 18))
            for sid in order:
                POS[(t_, sid)] = len(SEQ)
                SEQ.append(sid)
            if t_ + 1 < NT:
                for sid in (0, 1):
                    POS[(t_ + 1, sid)] = len(SEQ)
                    SEQ.append(sid)
            for sid in range(18, 36):
                POS[(t_, sid)] = len(SEQ)
                SEQ.append(sid)
            if t_ + 1 < NT:
                for sid in (4, 5):
                    POS[(t_ + 1, sid)] = len(SEQ)
                    SEQ.append(sid)
            for sid in range(36, 40):
                POS[(t_, sid)] = len(SEQ)
                SEQ.append(sid)
        NSEQ = len(SEQ)

        def gp(t_, sid):
            return POS[(t_, sid)]

        next_load = [0]

        def emit_load(gi):
            s = SLABS[SEQ[gi]]
            slot = gi % RING
            L = s["len"]
            P.dma(sp, d_ring[slot], wring[slot][:, 0:L], wbf_d[:, s["off"]:s["off"] + L], writes=[wring[slot][:, 0:L]])

        def loads_upto(gi):
            gi = min(gi, NSEQ - 1)
            while next_load[0] <= gi:
                emit_load(next_load[0])
                next_load[0] += 1

        def slab_view(gi):
            s = SLABS[SEQ[gi]]
            assert next_load[0] > gi
            return wring[gi % RING][:, 0:s["len"]].rearrange("p (k c) -> p k c", k=s["nkc"])

        def fm_group(bk, wv, j, hT, submajor=False):
            fns = []
            for kc in range(8):
                rhs = hT[:, :, kc, :] if submajor else hT[:, kc, :]
                fns.append(lambda e, kc=kc, rhs=rhs: e.matmul(bk[:], lhsT=wv[:, kc, j * 128:(j + 1) * 128], rhs=rhs,
                                                              start=(kc == 0), stop=(kc == 7)))
            P.group(pe, fns, reads=[wv, hT[:]], writes=[bk[:]])

        def load_x(t):
            src = x_d[t * T:(t + 1) * T, :].rearrange("(s p) f -> p s f", p=128)
            P.dma(sp, d_xs, xs[:], src, writes=[xs[:]])

        def load_p(t):
            src = p_d[t * T:(t + 1) * T, :].rearrange("(s p) f -> p s f", p=128)
            P.dma(sp, d_pin, pin[:], src, writes=[pin[:]])

        def mk_prenorm(src, hT, stc):
            ss = stt[:, stc:stc + 4]
            vv = stt[:, stc + 4:stc + 8]
            rs = stt[:, stc + 8:stc + 12]
            tm = stt[:, stc + 12:stc + 16]

            def stage1():
                for s in range(4):
                    jk = njunk()
                    P.op(act, lambda e, s=s, jk=jk: e.activation(out=jk[:], in_=src[:, s, :], func=AF.Square, scale=1.0 / 32.0,
                                                                 accum_out=ss[:, s:s + 1]),
                         reads=[src[:, s, :]], writes=[ss[:, s:s + 1], jk[:]])
                P.op(dve, lambda e: e.tensor_scalar(out=vv, in0=ss, scalar1=EPS, scalar2=None, op0=ALU.add),
                     reads=[ss], writes=[vv])
                rstd_newton(vv, rs, tm)

            def stage2():
                for s in range(4):
                    hn = nhn()
                    P.op(act, lambda e, s=s, hn=hn: e.activation(out=hn[:], in_=src[:, s, :], func=AF.Copy, scale=rs[:, s:s + 1]),
                         reads=[src[:, s, :], rs], writes=[hn[:]])
                    bk = nbank()
                    bkb = bk[:].bitcast(BF16)
                    fns = [lambda e, kc=kc, bkb=bkb, hn=hn: e.transpose(out=bkb[:, kc * 128:(kc + 1) * 128], in_=hn[:, kc * 128:(kc + 1) * 128], identity=identb[:])
                           for kc in range(8)]
                    P.group(pe, fns, reads=[hn[:], identb[:]], writes=[bk[:]])
                    P.op(dve, lambda e, s=s, bkb=bkb: e.tensor_copy(out=hT[:, :, s * 128:(s + 1) * 128],
                                                                     in_=bkb.rearrange("p (k t) -> p k t", k=8)),
                         reads=[bk[:]], writes=[hT[:, :, s * 128:(s + 1) * 128]])

            return stage1, stage2

        def mk_pre(s, stc):
            ss = stt[:, stc:stc + 1]
            vv = stt[:, stc + 1:stc + 2]
            rs = stt[:, stc + 2:stc + 3]
            tm = stt[:, stc + 3:stc + 5]
            st = {}

            def sq():
                jk = njunk()
                P.op(act, lambda e, jk=jk: e.activation(out=jk[:], in_=xin[:, s, :], func=AF.Square, scale=1.0 / 32.0, accum_out=ss),
                     reads=[xin[:, s, :]], writes=[ss, jk[:]])

            def newton():
                P.op(dve, lambda e: e.tensor_scalar(out=vv, in0=ss, scalar1=EPS, scalar2=None, op0=ALU.add), reads=[ss], writes=[vv])
                rstd_newton(vv, rs, tm, scalar_v=True)

            def hnf():
                hn = nhn()
                st["hn"] = hn
                P.op(act, lambda e, hn=hn: e.activation(out=hn[:], in_=xin[:, s, :], func=AF.Copy, scale=rs),
                     reads=[xin[:, s, :], rs], writes=[hn[:]])

            def tr():
                hn = st["hn"]
                bk = nbank()
                st["bk"] = bk
                bkb = bk[:].bitcast(BF16)
                fns = [lambda e, kc=kc, bkb=bkb, hn=hn: e.transpose(out=bkb[:, kc * 128:(kc + 1) * 128], in_=hn[:, kc * 128:(kc + 1) * 128], identity=identb[:])
                       for kc in range(8)]
                P.group(pe, fns, reads=[hn[:], identb[:]], writes=[bk[:]])

            def evac():
                bk = st["bk"]
                bkb = bk[:].bitcast(BF16)
                P.op(act, lambda e, bkb=bkb: e.activation(out=hTB[:, s, :, :], in_=bkb.rearrange("p (k t) -> p k t", k=8), func=AF.Copy),
                     reads=[bk[:]], writes=[hTB[:, s, :, :]])

            return sq, newton, hnf, tr, evac

        def p_transpose(pT):
            for s in range(4):
                hn = nhn()
                P.op(pool, lambda e, s=s, hn=hn: e.tensor_copy(out=hn[:, 0:256], in_=pin[:, s, :]), reads=[pin[:, s, :]], writes=[hn[:, 0:256]])
                bk = nbank()
                bkb = bk[:].bitcast(BF16)
                fns = [lambda e, kc=kc, bkb=bkb, hn=hn: e.transpose(out=bkb[:, kc * 128:(kc + 1) * 128], in_=hn[:, kc * 128:(kc + 1) * 128], identity=identb[:])
                       for kc in range(2)]
                P.group(pe, fns, reads=[hn[:, 0:256], identb[:]], writes=[bk[:]])
                P.op(dve, lambda e, s=s, bkb=bkb: e.tensor_copy(out=pT[:, :, s * 128:(s + 1) * 128],
                                                                 in_=bkb[:, 0:256].rearrange("p (k t) -> p k t", k=2)),
                     reads=[bk[:]], writes=[pT[:, :, s * 128:(s + 1) * 128]])

        def post_norm_res(src, gidx, base_sub, out_sub, stc):
            ss = stt[:, stc:stc + 1]
            vv = stt[:, stc + 1:stc + 2]
            rs = stt[:, stc + 2:stc + 3]
            tm = stt[:, stc + 3:stc + 5]
            jk = njunk()
            shp = list(src.shape)
            jkv = jk[:] if len(shp) == 2 else jk[:].rearrange("p (a b) -> p a b", a=2)
            P.op(act, lambda e: e.activation(out=jkv, in_=src, func=AF.Square, scale=1.0 / 32.0, accum_out=ss),
                 reads=[src], writes=[ss, jk[:]])
            P.op(dve, lambda e: e.tensor_scalar(out=vv, in0=ss, scalar1=EPS, scalar2=None, op0=ALU.add), reads=[ss], writes=[vv])
            rstd_newton(vv, rs, tm, scalar_v=True)
            gp = gpost[:, gidx, :] if len(shp) == 2 else gpost[:, gidx, :].rearrange("p (a b) -> p a b", a=2)
            tv = get2[:] if len(shp) == 2 else get2[:].rearrange("p (a b) -> p a b", a=2)
            P.op(dve, lambda e: e.scalar_tensor_tensor(out=tv, in0=src, scalar=rs, in1=gp, op0=ALU.mult, op1=ALU.mult),
                 reads=[src, rs, gpost[:, gidx, :]], writes=[get2[:]])
            P.op(dve, lambda e: e.tensor_tensor(out=out_sub, in0=base_sub, in1=get2[:], op=ALU.add),
                 reads=[base_sub, get2[:]], writes=[out_sub])

        hTBf = hTB[:].rearrange("p s k t -> p (s k t)")
        lru_a = hTBf[:, 0:2048].bitcast(F32).rearrange("p (a b) -> p a b", a=2)
        lru_a2 = hTBf[:, 2048:4096].bitcast(F32).rearrange("p (a b) -> p a b", a=2)
        lru_q = get[:].rearrange("p (a b) -> p a b", a=2)

        def fm_gelu(gi, dst, chunks=(0, 1, 2, 3), h=0):
            wv = slab_view(gi)
            for j in chunks:
                m = h * 4 + j
                bk = nbank()
                fm_group(bk, wv, j, hTA)
                P.op(act, lambda e, bk=bk, m=m, dst=dst: e.activation(out=dst[:, m, :], in_=bk[:], func=AF.Gelu_apprx_tanh),
                     reads=[bk[:]], writes=[dst[:, m, :]])

        def lru_batch(b0):
            for c in range(2):
                ch = b0 + c
                bkr, bki = nbank(), nbank()
                P.group(pe, [lambda e, bkr=bkr, ch=ch: e.matmul(bkr[:], lhsT=BDr[:, ch, :], rhs=xcm[:, ch, :], start=True, stop=True)],
                        reads=[BDr[:], xcm[:, ch, :]], writes=[bkr[:]])
                P.group(pe, [lambda e, bki=bki, ch=ch: e.matmul(bki[:], lhsT=BDi[:, ch, :], rhs=xcm[:, ch, :], start=True, stop=True)],
                        reads=[BDi[:], xcm[:, ch, :]], writes=[bki[:]])
                tr = ntp()
                P.op(act, lambda e, bkr=bkr, ch=ch, tr=tr: e.activation(out=tr[:], in_=bkr[:], func=AF.Tanh, scale=0.5,
                                                                        bias=drv[:, ch:ch + 1]),
                     reads=[bkr[:], drv[:]], writes=[tr[:]])
                P.op(act, lambda e, c=c, ch=ch, tr=tr: e.activation(out=lru_a[:, c, :], in_=tr[:], func=AF.Exp,
                                                                    scale=drv[:, 24 + ch:25 + ch], bias=drv[:, 24 + ch:25 + ch]),
                     reads=[tr[:], drv[:]], writes=[lru_a[:, c, :]])
                P.op(pool, lambda e, c=c: e.tensor_tensor(out=lru_a2[:, c, :], in0=lru_a[:, c, :], in1=lru_a[:, c, :], op=ALU.mult),
                     reads=[lru_a[:, c, :]], writes=[lru_a2[:, c, :]])
                ti = ntp()
                P.op(act, lambda e, bki=bki, ch=ch, ti=ti: e.activation(out=ti[:], in_=bki[:], func=AF.Tanh, scale=0.5,
                                                                        bias=drv[:, 8 + ch:9 + ch]),
                     reads=[bki[:], drv[:]], writes=[ti[:]])
                P.op(dve, lambda e, c=c, ch=ch, ti=ti: e.scalar_tensor_tensor(out=lru_q[:, c, :], in0=ti[:], scalar=1.0, in1=xcm[:, ch, :],
                                                                             op0=ALU.add, op1=ALU.mult),
                     reads=[ti[:], xcm[:, ch, :]], writes=[lru_q[:, c, :]])
            for c in range(2):
                P.op(act, lambda e, c=c: e.activation(out=lru_a2[:, c, :], in_=lru_a2[:, c, :], func=AF.Sqrt, scale=-0.25, bias=0.25),
                     reads=[lru_a2[:, c, :]], writes=[lru_a2[:, c, :]])
            for c in range(2):
                ch = b0 + c
                P.op(dve, lambda e, c=c: e.tensor_tensor(out=lru_q[:, c, :], in0=lru_q[:, c, :], in1=lru_a2[:, c, :], op=ALU.mult),
                     reads=[lru_q[:, c, :], lru_a2[:, c, :]], writes=[lru_q[:, c, :]])
                hb = ntp()
                P.op(dve, lambda e, c=c, ch=ch, hb=hb: e.tensor_tensor_scan(out=hb[:], data0=lru_a[:, c, :], data1=lru_q[:, c, :],
                                                                          initial=hstate[:, ch:ch + 1], op0=ALU.mult, op1=ALU.add),
                     reads=[lru_a[:, c, :], lru_q[:, c, :], hstate[:, ch:ch + 1]], writes=[hb[:]])
                P.op(dve, lambda e, ch=ch, hb=hb: e.tensor_copy(out=hstate[:, ch:ch + 1], in_=hb[:, 511:512]),
                     reads=[hb[:, 511:512]], writes=[hstate[:, ch:ch + 1]])
                P.op(pool, lambda e, ch=ch, hb=hb: e.tensor_tensor(out=gy[:, ch, :], in0=hb[:], in1=gy[:, ch, :], op=ALU.mult),
                     reads=[hb[:], gy[:, ch, :]], writes=[gy[:, ch, :]])

        def yr_half(t, h):
            CUR_LABEL[0] = "yr"
            gi = gp(t, 0 + h)
            loads_upto(gi)
            fm_gelu(gi, gy, h=h)
            loads_upto(gi + RING)

        def xr_half(t, h):
            CUR_LABEL[0] = "xr_conv"
            gi = gp(t, 4 + h)
            loads_upto(gi)
            wv = slab_view(gi)
            for j in range(4):
                ch = h * 4 + j
                bk = nbank()
                fm_group(bk, wv, j, hTA)
                P.op(act, lambda e, bk=bk, ch=ch: e.activation(out=xrb[:, ch, 3:515], in_=bk[:], func=AF.Copy),
                     reads=[bk[:]], writes=[xrb[:, ch, 3:515]])
            loads_upto(gi + RING)

        def conv_half(h):
            CUR_LABEL[0] = "xr_conv"
            for ch in range(4 * h, 4 * h + 4):
                bk = nbank()
                fns = [lambda e, k=k, bk=bk, ch=ch: e.matmul(bk[:], lhsT=Dcv[:, k, ch, :], rhs=xrb[:, ch, k:k + 512],
                                                             start=(k == 0), stop=(k == 3)) for k in range(4)]
                P.group(pe, fns, reads=[Dcv[:, :, ch, :], xrb[:, ch, 0:515]], writes=[bk[:]])
                P.op(act, lambda e, bk=bk, ch=ch: e.activation(out=xcm[:, ch, :], in_=bk[:], func=AF.Identity,
                                                                bias=csb[:, C_CVB + ch:C_CVB + ch + 1]),
                     reads=[bk[:], csb[:]], writes=[xcm[:, ch, :]])
                P.op(pool, lambda e, ch=ch: e.tensor_copy(out=xrb[:, ch, 0:3], in_=xrb[:, ch, 512:515]),
                     reads=[xrb[:, ch, 512:515]], writes=[xrb[:, ch, 0:3]])

        def phase_mix(t):
            hT = hTA
            nxt = t + 1 < NT
            def x_to_residual():
                P.dma(sp, d_c[2], xin[:], xs[:], reads=[xs[:]], writes=[xin[:]])
                if nxt:
                    load_x(t + 1)
            if nxt:
                pn1, pn2 = mk_prenorm(xs, hTA, 64)
            CUR_LABEL[0] = "v"
            giv = gp(t, 2)
            loads_upto(giv + 1)
            wv0, wv1 = slab_view(giv), slab_view(giv + 1)
            for s in range(4):
                for h, wv in ((0, wv0), (1, wv1)):
                    bk = nbank()
                    fns = [lambda e, kc=kc, bk=bk, wv=wv, s=s: e.matmul(bk[:], lhsT=hT[:, kc, s * 128:(s + 1) * 128], rhs=wv[:, kc, :],
                                                                       start=(kc == 0), stop=(kc == 7)) for kc in range(8)]
                    P.group(pe, fns, reads=[wv, hT[:]], writes=[bk[:]])
                    P.op(act, lambda e, bk=bk, s=s, h=h: e.activation(out=gv[:, s, h * 512:(h + 1) * 512], in_=bk[:], func=AF.Gelu_apprx_tanh),
                         reads=[bk[:]], writes=[gv[:, s, h * 512:(h + 1) * 512]])
                    P.op(dve, lambda e, s=s, h=h: e.bn_stats(out=mvs[:, s * 12 + h * 6:s * 12 + h * 6 + 6], in_=gv[:, s, h * 512:(h + 1) * 512]),
                         reads=[gv[:, s, h * 512:(h + 1) * 512]], writes=[mvs[:, s * 12 + h * 6:s * 12 + h * 6 + 6]])
                P.op(dve, lambda e, s=s: e.bn_aggr(out=mvs[:, 48 + 2 * s:50 + 2 * s], in_=mvs[:, s * 12:s * 12 + 12]),
                     reads=[mvs[:, s * 12:s * 12 + 12]], writes=[mvs[:, 48 + 2 * s:50 + 2 * s]])
            loads_upto(giv + 1 + RING)
            mv2 = mvs[:, 48:56].rearrange("p (s two) -> p s two", two=2)
            vv = stt[:, 48:52]
            rs = stt[:, 52:56]
            tm = stt[:, 56:60]
            P.op(dve, lambda e: e.tensor_scalar(out=vv, in0=mv2[:, :, 1], scalar1=EPS, scalar2=None, op0=ALU.add),
                 reads=[mvs[:, 48:56]], writes=[vv])
            rstd_newton(vv, rs, tm)
            for s in range(4):
                P.op(dve, lambda e, s=s: e.tensor_scalar(out=vln[:, s, :], in0=gv[:, s, :], scalar1=mvs[:, 48 + 2 * s:49 + 2 * s],
                                                          scalar2=rs[:, s:s + 1], op0=ALU.subtract, op1=ALU.mult),
                     reads=[gv[:, s, :], mvs[:, 48:56], rs], writes=[vln[:, s, :]])
            x_to_residual()
            giu = gp(t, 6)
            loads_upto(giu + 1)
            for k in range(4):
                CUR_LABEL[0] = "lru"
                lru_batch(2 * k)
                CUR_LABEL[0] = "u"
                fm_gelu(giu + k // 2, gu, chunks=(2 * (k % 2), 2 * (k % 2) + 1), h=k // 2)
            loads_upto(giu + 1 + RING)
            CUR_LABEL[0] = "spatial"
            for g in range(8):
                bk = nbank()
                fns = [lambda e, n=n, bk=bk, g=g: e.matmul(bk[:, n * 128:(n + 1) * 128], lhsT=vln[:, n, g * 128:(g + 1) * 128], rhs=WsT[:, g, :],
                                                           start=True, stop=True) for n in range(4)]
                P.group(pe, fns, reads=[vln[:], WsT[:]], writes=[bk[:]])
                tmp = ntp()
                bgb = Bg[:, g:g + 1, :].to_broadcast([128, 4, 128])
                P.op(dve, lambda e, bk=bk, g=g, tmp=tmp, bgb=bgb: e.scalar_tensor_tensor(
                    out=tmp[:].rearrange("p (n i) -> p n i", n=4), in0=bk[:].rearrange("p (n i) -> p n i", n=4),
                    scalar=csb[:, C_LNG + g:C_LNG + g + 1], in1=bgb, op0=ALU.mult, op1=ALU.add),
                    reads=[bk[:], csb[:], Bg[:, g, :]], writes=[tmp[:]])
                P.op(pool, lambda e, g=g, tmp=tmp: e.tensor_tensor(out=gu[:, g, :], in0=tmp[:], in1=gu[:, g, :], op=ALU.mult),
                     reads=[tmp[:], gu[:, g, :]], writes=[gu[:, g, :]])
            CUR_LABEL[0] = "merge"
            for h in range(2):
                gi = gp(t, 8 + 4 * h)
                loads_upto(gi + 3)
                wga, wgb, wa, wb = (slab_view(gi + i) for i in range(4))
                for j in range(4):
                    m = h * 4 + j
                    bga, bgb_, bpa, bpb = nbank(), nbank(), nbank(), nbank()
                    fm_group(bga, wga, j, hT)
                    fm_group(bgb_, wgb, j, hT)
                    fm_group(bpa, wa, j, gu)
                    fm_group(bpb, wb, j, gy)
                    ta, tb = ntp(), ntp()
                    P.op(act, lambda e, bga=bga, ta=ta: e.activation(out=ta[:], in_=bga[:], func=AF.Tanh, scale=0.5), reads=[bga[:]], writes=[ta[:]])
                    P.op(act, lambda e, bgb_=bgb_, tb=tb: e.activation(out=tb[:], in_=bgb_[:], func=AF.Tanh, scale=0.5), reads=[bgb_[:]], writes=[tb[:]])
                    P.op(dve, lambda e, ta=ta, bpa=bpa: e.scalar_tensor_tensor(out=ta[:], in0=ta[:], scalar=1.0, in1=bpa[:], op0=ALU.add, op1=ALU.mult),
                         reads=[ta[:], bpa[:]], writes=[ta[:]])
                    P.op(dve, lambda e, tb=tb, bpb=bpb: e.scalar_tensor_tensor(out=tb[:], in0=tb[:], scalar=1.0, in1=bpb[:], op0=ALU.add, op1=ALU.mult),
                         reads=[tb[:], bpb[:]], writes=[tb[:]])
                    P.op(pool, lambda e, ta=ta, tb=tb, m=m: e.tensor_tensor(out=xcm[:, m, :], in0=ta[:], in1=tb[:], op=ALU.add),
                         reads=[ta[:], tb[:]], writes=[xcm[:, m, :]])
                loads_upto(gi + 3 + RING)
                if h == 0 and nxt:
                    pn1()
                    CUR_LABEL[0] = "merge"
            if nxt:
                CUR_LABEL[0] = "next_prenorm"
                pn2()
            CUR_LABEL[0] = "wout"
            gi = gp(t, 16)
            loads_upto(gi + 1)
            w0, w1 = slab_view(gi), slab_view(gi + 1)
            pres = [mk_pre(s_, 80 + 5 * s_) for s_ in range(4)]

            def wout_sub(s):
                CUR_LABEL[0] = "wout"
                pr = nbank_pair()
                for hh, wv in enumerate((w0, w1)):
                    fns = [lambda e, kc=kc, pr=pr, hh=hh, wv=wv, s=s: e.matmul(pr[:, hh, :], lhsT=xcm[:, kc, s * 128:(s + 1) * 128], rhs=wv[:, kc, :],
                                                                              start=(kc == 0), stop=(kc == 7)) for kc in range(8)]
                    P.group(pe, fns, reads=[wv, xcm[:]], writes=[pr[:, hh, :]])
                post_norm_res(pr, 0, xin[:, s, :], xin[:, s, :], 16 + 5 * s)

            def pst(s, k):
                CUR_LABEL[0] = "ffn_prenorm"
                pres[s][k]()

            wout_sub(0)
            wout_sub(1); pst(0, 0)
            wout_sub(2); pst(0, 1); pst(0, 2); pst(1, 0)
            wout_sub(3)
            loads_upto(gi + 1 + RING)
            if nxt:
                yr_half(t + 1, 0)
            pst(0, 3); pst(0, 4); pst(1, 1); pst(1, 2); pst(2, 0)
            pst(1, 3); pst(1, 4); pst(2, 1); pst(2, 2); pst(3, 0); pst(3, 1); pst(3, 2)
            if nxt:
                yr_half(t + 1, 1)
            pst(2, 3); pst(2, 4); pst(3, 3); pst(3, 4)
            if nxt:
                load_p(t + 1)

        def phase_ffn(t):
            nxt = t + 1 < NT
            CUR_LABEL[0] = "ffn_gateup"
            for jb in range(6):
                gi = gp(t, 18 + 2 * jb)
                loads_upto(gi + 1)
                wg, wu = slab_view(gi), slab_view(gi + 1)
                for jj in range(4 if jb < 5 else 2):
                    f = jb * 4 + jj
                    bg, bu = nbank(), nbank()
                    fm_group(bg, wg, jj, hTB, submajor=True)
                    fm_group(bu, wu, jj, hTB, submajor=True)
                    sg = ntp()
                    P.op(act, lambda e, bg=bg, sg=sg: e.activation(out=sg[:], in_=bg[:], func=AF.Silu), reads=[bg[:]], writes=[sg[:]])
                    P.op(dve, lambda e, bu=bu, sg=sg, f=f: e.tensor_tensor(out=act_v[:, f, :], in0=sg[:], in1=bu[:], op=ALU.mult),
                         reads=[sg[:], bu[:]], writes=[act_v[:, f, :]])
                loads_upto(gi + 1 + RING)
                if jb == 3 and nxt:
                    CUR_LABEL[0] = "p_transpose"
                    p_transpose(pTs[(t + 1) % 2])
                    CUR_LABEL[0] = "ffn_gateup"
            CUR_LABEL[0] = "down"
            prd = [nbank_pair() for _ in range(4)]
            for h in range(2):
                for pi_, (k0, nk) in enumerate(((0, 8), (8, 8), (16, 6))):
                    gi = gp(t, 30 + 3 * h + pi_)
                    loads_upto(gi)
                    wv = slab_view(gi)
                    for s in range(4):
                        bk = _Bk(prd[s][:, h, :])
                        fns = [lambda e, kc=kc, bk=bk, wv=wv, s=s, k0=k0: e.matmul(
                            bk[:], lhsT=act_v[:, k0 + kc, s * 128:(s + 1) * 128], rhs=wv[:, kc, :],
                            start=(k0 + kc == 0), stop=(k0 + kc == NF - 1)) for kc in range(nk)]
                        P.group(pe, fns, reads=[wv, act_v[:, k0:k0 + nk, :]], writes=[bk[:]], accum=(k0 > 0))
                    loads_upto(gi + RING)
            pT = pTs[t % 2]
            ple_w = {}

            def ple_sub(s):
                CUR_LABEL[0] = "ple"
                if not ple_w:
                    gi = gp(t, 36)
                    loads_upto(gi + 3)
                    ple_w["g"] = (slab_view(gi), slab_view(gi + 1))
                    ple_w["i"] = (slab_view(gi + 2), slab_view(gi + 3))
                for h in range(2):
                    wg, wi = ple_w["g"][h], ple_w["i"][h]
                    bg, be = nbank(), nbank()
                    fns = [lambda e, kc=kc, bg=bg, wg=wg: e.matmul(bg[:], lhsT=hTB[:, s, kc, :], rhs=wg[:, kc, :],
                                                                  start=(kc == 0), stop=(kc == 7)) for kc in range(8)]
                    P.group(pe, fns, reads=[wg, hTB[:, s, :, :]], writes=[bg[:]])
                    fns = [lambda e, kc=kc, be=be, wi=wi: e.matmul(be[:], lhsT=pT[:, kc, s * 128:(s + 1) * 128], rhs=wi[:, kc, :],
                                                                  start=(kc == 0), stop=(kc == 1)) for kc in range(2)]
                    P.group(pe, fns, reads=[wi, pT[:]], writes=[be[:]])
                    tg = ntp()
                    P.op(act, lambda e, bg=bg, tg=tg: e.activation(out=tg[:], in_=bg[:], func=AF.Tanh, scale=0.5), reads=[bg[:]], writes=[tg[:]])
                    P.op(dve, lambda e, tg=tg, be=be, h=h: e.scalar_tensor_tensor(out=get[:, h * 512:(h + 1) * 512], in0=tg[:], scalar=1.0, in1=be[:],
                                                                                 op0=ALU.add, op1=ALU.mult),
                         reads=[tg[:], be[:]], writes=[get[:, h * 512:(h + 1) * 512]])
                post_norm_res(get[:], 2, xin[:, s, :], xin[:, s, :], 36 + 5 * s)
                dst = y_d[t * T + s * 128:t * T + (s + 1) * 128, :]
                P.dma(pool, d_out[s], dst, xin[:, s, :], reads=[xin[:, s, :]])

            def pn(s):
                CUR_LABEL[0] = "down"
                post_norm_res(prd[s], 1, xin[:, s, :], xin[:, s, :], 16 + 5 * s)

            pres = [mk_pre(s_, 80 + 5 * s_) for s_ in range(4)]

            def pst(s, k):
                CUR_LABEL[0] = "ple_prenorm"
                pres[s][k]()

            pn(0); pn(1); pst(0, 0); pn(2); pst(0, 1); pst(0, 2); pst(1, 0); pn(3)
            if nxt:
                xr_half(t + 1, 0)
            pst(0, 3); pst(0, 4); pst(1, 1); pst(1, 2); pst(2, 0)
            if nxt:
                xr_half(t + 1, 1)
            pst(1, 3); pst(1, 4); pst(2, 1); pst(2, 2); pst(3, 0)
            if nxt:
                conv_half(0)
            pst(2, 3); pst(2, 4); pst(3, 1); pst(3, 2)
            if nxt:
                conv_half(1)
            pst(3, 3); pst(3, 4)
            ple_sub(0); ple_sub(1); ple_sub(2); ple_sub(3)
            loads_upto(gp(t, 39) + RING)

        loads_upto(RING - 1)
        load_x(0)
        load_p(0)
        p0a, p0b = mk_prenorm(xs, hTA, 0)
        p0a()
        p0b()
        p_transpose(pTs[0])
        yr_half(0, 0)
        yr_half(0, 1)
        xr_half(0, 0)
        xr_half(0, 1)
        conv_half(0)
        conv_half(1)
        for t in range(NT):
            phase_mix(t)
            phase_ffn(t)
        for dd in d_out:
            sp.wait(dd, dd.n)
            pool.wait(dd, dd.n)

        with nc.Block() as block:
            @block.sync
            def _(e):
                for f in sp.ops:
                    f(e)

            @block.tensor
            def _(e):
                for f in pe.ops:
                    f(e)

            @block.scalar
            def _(e):
                for f in act.ops:
                    f(e)

            @block.vector
            def _(e):
                for f in dve.ops:
                    f(e)

            @block.gpsimd
            def _(e):
                for f in pool.ops:
                    f(e)
    return nc


def _host_consts(inp):
    def colT(v):
        return np.ascontiguousarray(np.asarray(v, np.float32).reshape(8, 128).T)

    cst = np.zeros((128, NCST), np.float32)
    cst[:, C_GMIX:C_GMIX + 8] = colT(inp["norm_mix_pre"][0])
    cst[:, C_GFFN:C_GFFN + 8] = colT(inp["norm_ffn_pre"][0])
    cst[:, C_GPLE:C_GPLE + 8] = colT(inp["norm_ple_pre"][0])
    cst[:, C_HALF] = 0.5
    cst[:, C_ONE] = 1.0
    cst[:, C_LNG:C_LNG + 8] = colT(inp["gmlp_ln_g"][0])
    cst[:, C_LNB:C_LNB + 8] = colT(inp["gmlp_ln_b"][0])
    for k in range(4):
        cst[:, C_CVW + k * 8:C_CVW + k * 8 + 8] = colT(inp["conv_w"][0, k])
    cst[:, C_CVB:C_CVB + 8] = colT(inp["conv_b"][0])
    cst[:, C_BR:C_BR + 8] = colT(inp["lru_b_r"][0])
    cst[:, C_BI:C_BI + 8] = colT(inp["lru_b_i"][0])
    cst[:, C_LAM:C_LAM + 8] = colT(inp["lru_lambda"][0])
    wst = np.ascontiguousarray(np.asarray(inp["gmlp_w_s"][0], np.float32).transpose(2, 0, 1)).reshape(128, 1024)

    def bd(w):
        w = np.asarray(w, np.float32)
        out = np.zeros((128, 8, 128), np.float32)
        for hd in range(16):
            ch, hl = hd // 2, hd % 2
            out[hl * 64:(hl + 1) * 64, ch, hl * 64:(hl + 1) * 64] = w[hd]
        return out.reshape(128, 1024)

    bcv = np.concatenate([np.asarray(inp["gmlp_b_s"][0], np.float32).reshape(-1),
                          np.asarray(inp["norm_mix_post"][0], np.float32),
                          np.asarray(inp["norm_ffn_post"][0], np.float32),
                          np.asarray(inp["norm_ple_post"][0], np.float32)]).reshape(1, 4096)
    mats = {"w_in": np.asarray(inp["w_in"][0], np.float32), "w_a": np.asarray(inp["w_branch_a"][0], np.float32),
            "w_b": np.asarray(inp["w_branch_b"][0], np.float32), "w_out": np.asarray(inp["w_out"][0], np.float32),
            "f_gate": np.asarray(inp["ffn_w_gate"][0], np.float32), "f_up": np.asarray(inp["ffn_w_up"][0], np.float32),
            "f_down": np.asarray(inp["ffn_w_down"][0], np.float32), "p_gate": np.asarray(inp["ple_w_gate"][0], np.float32),
            "p_in": np.asarray(inp["ple_w_in"][0], np.float32)}
    return dict(wf=_pack_weights(mats), cst=cst, idn=np.eye(128, dtype=np.float32), wst=wst,
                bdr=bd(inp["lru_w_r"][0]), bdi=bd(inp["lru_w_i"][0]), bcv=bcv)


def kernel(**inputs):
    x = np.asarray(inputs["x"], np.float32)
    p = np.asarray(inputs["p"], np.float32)
    B = x.shape[0]
    NT = x.shape[1] // T
    shared = _host_consts(inputs)
    nc = build_nc(NT)
    in_maps = []
    for b in range(B):
        m = dict(shared)
        m["x"] = np.ascontiguousarray(x[b])
        m["p"] = np.ascontiguousarray(p[0, b])
        in_maps.append(m)
    res = run_bass_kernel_spmd(nc, in_maps, core_ids=list(range(B)))
    return np.stack([np.asarray(r["y"], np.float32) for r in res.results], axis=0)
```
